# Optimizing a Trainium2 kernel written in Bass

```python
import jax
import jax.numpy as jnp
from jax import lax
import numpy as np

D_MODEL = 4096
BATCH = 2
SEQ = 4096
DEPTH = 2

CTX_LEN = 256
GRID_W = 64
N_SUB = 3
FFN_DIM = 5632
FFN_RES = 0.5
GM_GROUPS = 8
GM_GROUP_CH = 128
GM_CHUNK = 128
GM_WIDTH = GM_GROUPS * GM_GROUP_CH
DN_HEADS = 8
DN_DK = 128
DN_DV = 128
DN_WIDTH = DN_HEADS * DN_DV
DN_CHUNK = 64
CONV_K = 5
NA_HEADS = 16
NA_DH = 128
NA_WIDTH = NA_HEADS * NA_DH
NA_KH_MAX = 8
NA_KW = 16
ROPE_BASE = 10000.0
EPS = 1e-6
NEG = -1e30
IN_SPLITS = (2 * DN_WIDTH, 2 * DN_HEADS, 2 * DN_HEADS, NA_WIDTH, NA_WIDTH,
             DN_WIDTH, NA_WIDTH, DN_WIDTH, GM_WIDTH, GM_WIDTH, D_MODEL, D_MODEL, D_MODEL)
N_CTX_STATE_SPLITS = 5
N_CTX_STATE_COLS = sum(IN_SPLITS[:N_CTX_STATE_SPLITS])
IN_WIDTH = sum(IN_SPLITS)

kernel_name = 'hybrid_gmlp_deltanet_natten_dit_block'


def rmsnorm(x, w):
    xf = x.astype(jnp.float32)
    y = xf * lax.rsqrt(jnp.mean(xf * xf, axis=-1, keepdims=True) + EPS)
    return (y * w.astype(jnp.float32)).astype(x.dtype)


def layernorm(x, w):
    xf = x.astype(jnp.float32)
    mu = jnp.mean(xf, axis=-1, keepdims=True)
    var = jnp.mean(jnp.square(xf - mu), axis=-1, keepdims=True)
    return ((xf - mu) * lax.rsqrt(var + EPS) * w.astype(jnp.float32)).astype(x.dtype)


def l2norm(x):
    xf = x.astype(jnp.float32)
    return xf * lax.rsqrt(jnp.sum(xf * xf, axis=-1, keepdims=True) + EPS)


def adaln(h, norm_w, mod, i):
    return rmsnorm(h, norm_w) * (1 + mod[..., i, 1, :]) + mod[..., i, 0, :]


def swiglu(x, w1, w3, w2):
    return (jax.nn.silu(x @ w1) * (x @ w3)) @ w2


def ffn_sublayer(h, mod, i, norm_w, w1, w3, w2):
    return h + FFN_RES * mod[..., i, 2, :] * swiglu(adaln(h, norm_w, mod, i), w1, w3, w2)


def split_cols(z, sizes):
    offs = np.cumsum(np.array(sizes))[:-1].tolist()
    return jnp.split(z, offs, axis=-1)


def axial_rope(x, rows, cols):
    half = x.shape[-1] // 2
    nf = half // 2
    freqs = ROPE_BASE ** (-jnp.arange(nf, dtype=jnp.float32) / nf)

    def rot(xp, pos):
        ang = pos.astype(jnp.float32)[:, None] * freqs[None, :]
        cos = jnp.cos(ang)[None, :, None, :]
        sin = jnp.sin(ang)[None, :, None, :]
        x1, x2 = xp[..., :nf], xp[..., nf:]
        return jnp.concatenate([x1 * cos - x2 * sin, x2 * cos + x1 * sin], axis=-1)

    return jnp.concatenate([rot(x[..., :half], rows), rot(x[..., half:], cols)], axis=-1).astype(x.dtype)


def short_conv(x, w):
    c = x.shape[-1]
    y = lax.conv_general_dilated(x, w[:, None, :].astype(x.dtype), window_strides=(1,),
                                 padding=[(CONV_K // 2, CONV_K // 2)],
                                 dimension_numbers=('NWC', 'WIO', 'NWC'), feature_group_count=c)
    return jax.nn.silu(y)


def chunk_gated_delta(q, k, v, g, beta, s0):
    b, t, h, _ = k.shape
    n = t // DN_CHUNK
    cv = lambda u: u.reshape(b, n, DN_CHUNK, h, u.shape[-1]).transpose(1, 0, 3, 2, 4)
    cs = lambda u: u.reshape(b, n, DN_CHUNK, h).transpose(1, 0, 3, 2)
    k, v = cv(k), cv(v)
    gc = jnp.cumsum(cs(g), axis=-1)
    bt = cs(beta)[..., None]
    idx = jnp.arange(DN_CHUNK)
    incl = idx[:, None] >= idx[None, :]
    strict = idx[:, None] > idx[None, :]
    diff = gc[..., :, None] - gc[..., None, :]
    decay = jnp.where(incl, jnp.exp(jnp.where(incl, diff, 0.0)), 0.0)
    k_beta = k * bt
    a = jnp.where(strict, jnp.einsum('nbhid,nbhjd->nbhij', k_beta, k) * decay, 0.0) + jnp.eye(DN_CHUNK, dtype=k.dtype)
    rhs = jnp.concatenate([v * bt, k_beta * jnp.exp(gc)[..., None]], axis=-1)
    sol = lax.linalg.triangular_solve(a, rhs, left_side=True, lower=True, unit_diagonal=True)
    u_val, w_cum = sol[..., :DN_DV], sol[..., DN_DV:]
    with_out = q is not None
    xs = (k, u_val, w_cum, gc)
    if with_out:
        q = cv(q)
        qk = jnp.where(incl, jnp.einsum('nbhid,nbhjd->nbhij', q, k) * decay, 0.0)
        xs = xs + (q, qk)

    def step(s, xs_i):
        k_i, u_i, w_i, g_i = xs_i[:4]
        v_new = u_i - jnp.einsum('bhck,bhkv->bhcv', w_i, s)
        g_last = g_i[..., -1:]
        s_next = s * jnp.exp(g_last)[..., None] + jnp.einsum(
            'bhck,bhcv->bhkv', k_i * jnp.exp(g_last - g_i)[..., None], v_new)
        if not with_out:
            return s_next, None
        q_i, qk_i = xs_i[4:]
        o = (jnp.einsum('bhck,bhkv->bhcv', q_i * jnp.exp(g_i)[..., None], s)
             + jnp.einsum('bhij,bhjv->bhiv', qk_i, v_new))
        return s_next, o

    s_fin, o = lax.scan(step, s0, xs)
    if with_out:
        o = o.transpose(1, 0, 3, 2, 4).reshape(b, t, h, DN_DV)
    return o, s_fin


def dn_prep(z_kv, z_alpha, z_beta, z_q, conv_w, a_log, dt_bias, pos):
    b, t = z_kv.shape[:2]
    hk = lambda u: u.reshape(b, t, DN_HEADS, DN_DK)
    hv = lambda u: u.reshape(b, t, DN_HEADS, DN_DV)
    k, v = jnp.split(short_conv(z_kv, conv_w[:, :2 * DN_WIDTH]), 2, axis=-1)
    k = l2norm(hk(k))
    v = hv(v).astype(jnp.float32)
    q = None
    if z_q is not None:
        q = l2norm(hk(short_conv(z_q, conv_w[:, 2 * DN_WIDTH:])))
    if pos is not None:
        k = axial_rope(k, *pos)
        q = axial_rope(q, *pos)
    if q is not None:
        q = q * DN_DK ** -0.5
    alpha = z_alpha.astype(jnp.float32).reshape(b, t, 2, DN_HEADS)
    g = -jnp.exp(a_log.astype(jnp.float32)) * jax.nn.softplus(alpha + dt_bias.astype(jnp.float32))
    beta = jax.nn.sigmoid(z_beta.astype(jnp.float32).reshape(b, t, 2, DN_HEADS))
    return q, k, v, g, beta


def dn_bidir(q, k, v, g, beta, s0_f, s0_b):
    rev = lambda u: None if u is None else jnp.flip(u, axis=1)
    o_f, s_f = chunk_gated_delta(q, k, v, g[:, :, 0], beta[:, :, 0], s0_f)
    o_b, s_b = chunk_gated_delta(rev(q), rev(k), rev(v), rev(g[:, :, 1]), rev(beta[:, :, 1]), s0_b)
    o = None if q is None else o_f + rev(o_b)
    return o, s_f, s_b


def dn_out(o, z_gate, w):
    b, t = z_gate.shape[:2]
    y = rmsnorm(o, w).reshape(b, t, DN_WIDTH)
    return (y * jax.nn.silu(z_gate.astype(jnp.float32))).astype(z_gate.dtype)


def gmlp_mix(z_u, z_v, sgu_w, sgu_b, sgu_norm_w):
    b, t, _ = z_u.shape
    u = jax.nn.gelu(z_u)
    v = layernorm(jax.nn.gelu(z_v), sgu_norm_w)
    vc = v.reshape(b, t // GM_CHUNK, GM_CHUNK, GM_GROUPS, GM_GROUP_CH)
    s = jnp.einsum('gpq,bnqgc->bnpgc', sgu_w.astype(v.dtype), vc) + sgu_b.T[None, None, :, :, None]
    return u * s.reshape(b, t, GM_WIDTH)


def na_attend(q, k, v, k_ctx, v_ctx, rpb):
    b, s, h, dh = q.shape
    rows = s // GRID_W
    kh = min(NA_KH_MAX, rows)
    qg = q.reshape(b, rows, GRID_W, h, dh)
    kg = k.reshape(b, rows, GRID_W, h, dh)
    vg = v.reshape(b, rows, GRID_W, h, dh)
    cols = jnp.arange(GRID_W)
    col_start = jnp.clip(cols - NA_KW // 2, 0, GRID_W - NA_KW)
    col_ok = (cols[None, :] >= col_start[:, None]) & (cols[None, :] < col_start[:, None] + NA_KW)
    dc_idx = jnp.clip(cols[None, :] - cols[:, None] + NA_KW - 1, 0, 2 * NA_KW - 2)
    rpb = rpb.astype(jnp.float32)
    scale = dh ** -0.5

    def row_block(r):
        r0 = jnp.clip(r - kh // 2, 0, rows - kh)
        q_r = lax.dynamic_index_in_dim(qg, r, axis=1, keepdims=False)
        k_r = lax.dynamic_slice_in_dim(kg, r0, kh, axis=1)
        v_r = lax.dynamic_slice_in_dim(vg, r0, kh, axis=1)
        dr_idx = r0 + jnp.arange(kh) - r + NA_KH_MAX - 1
        bias = rpb[:, dr_idx[:, None, None], dc_idx[None, :, :]].transpose(0, 2, 1, 3)
        bias = jnp.where(col_ok[None, :, None, :], bias, NEG)
        s_loc = jnp.einsum('bqhd,bkwhd->bhqkw', q_r, k_r).astype(jnp.float32) * scale + bias[None]
        s_ctx = jnp.einsum('bqhd,bchd->bhqc', q_r, k_ctx).astype(jnp.float32) * scale
        n_loc = kh * GRID_W
        p = jax.nn.softmax(jnp.concatenate([s_loc.reshape(b, h, GRID_W, n_loc), s_ctx], axis=-1), axis=-1)
        p = p.astype(v.dtype)
        p_loc = p[..., :n_loc].reshape(b, h, GRID_W, kh, GRID_W)
        return (jnp.einsum('bhqkw,bkwhd->bqhd', p_loc, v_r)
                + jnp.einsum('bhqc,bchd->bqhd', p[..., n_loc:], v_ctx))

    o = lax.map(row_block, jnp.arange(rows))
    return o.transpose(1, 0, 2, 3, 4).reshape(b, s, h * dh)


def ctx_attend(q, k, v):
    b, l, h, dh = q.shape
    s = jnp.einsum('bqhd,bkhd->bhqk', q, k).astype(jnp.float32) * dh ** -0.5
    p = jax.nn.softmax(s, axis=-1).astype(v.dtype)
    return jnp.einsum('bhqk,bkhd->bqhd', p, v).reshape(b, l, h * dh)


def merge(y_a, y_b, y_c, g_a, g_b, g_c, proj_a, proj_b, proj_c, w_out):
    y = (jax.nn.sigmoid(g_a) * (y_a @ proj_a) + jax.nn.sigmoid(g_b) * (y_b @ proj_b)
         + jax.nn.sigmoid(g_c) * (y_c @ proj_c))
    return y @ w_out


def mixer(h, hc, w_in, conv_w, a_log, dt_bias, dn_norm_w, sgu_w, sgu_b, sgu_norm_w, rpb,
          proj_a, proj_b, proj_c, w_out, ctx_out):
    b, s, _ = h.shape
    z = split_cols(h @ w_in, IN_SPLITS)
    if ctx_out:
        zc = split_cols(hc @ w_in, IN_SPLITS)
    else:
        zc = split_cols(hc @ w_in[:, :N_CTX_STATE_COLS], IN_SPLITS[:N_CTX_STATE_SPLITS])
    t = jnp.arange(s)
    pos = (t // GRID_W, t % GRID_W)
    qc, kc, vc, gc, bc = dn_prep(zc[0], zc[1], zc[2], zc[5] if ctx_out else None, conv_w, a_log, dt_bias, None)
    s0 = jnp.zeros((b, DN_HEADS, DN_DK, DN_DV), jnp.float32)
    oc_dn, s_f, s_b = dn_bidir(qc, kc, vc, gc, bc, s0, s0)
    ql, kl, vl, gl, bl = dn_prep(z[0], z[1], z[2], z[5], conv_w, a_log, dt_bias, pos)
    o_dn, _, _ = dn_bidir(ql, kl, vl, gl, bl, s_f, s_b)
    y_b = dn_out(o_dn, z[7], dn_norm_w)
    heads = lambda u: u.reshape(u.shape[0], u.shape[1], NA_HEADS, NA_DH)
    k_ctx, v_ctx = heads(zc[3]), heads(zc[4])
    y_c = na_attend(heads(z[6]), heads(z[3]), heads(z[4]), k_ctx, v_ctx, rpb)
    y_a = gmlp_mix(z[8], z[9], sgu_w, sgu_b, sgu_norm_w)
    y = merge(y_a, y_b, y_c, z[10], z[11], z[12], proj_a, proj_b, proj_c, w_out)
    if not ctx_out:
        return y, None
    yc = merge(gmlp_mix(zc[8], zc[9], sgu_w, sgu_b, sgu_norm_w), dn_out(oc_dn, zc[7], dn_norm_w),
               ctx_attend(heads(zc[6]), k_ctx, v_ctx), zc[10], zc[11], zc[12], proj_a, proj_b, proj_c, w_out)
    return y, yc


def setup_inputs(seed: int = 0) -> dict:
    key = jax.random.key(seed)
    ks = jax.random.split(key, 24)
    f32 = jnp.float32
    nrm = lambda k, shape, sc: jax.random.normal(k, shape, f32) * sc
    L, D = DEPTH, D_MODEL
    dt = jnp.exp(jax.random.uniform(ks[13], (L, 2, DN_HEADS), f32, np.log(1e-3), np.log(1e-1)))
    return {
        'x': nrm(ks[0], (BATCH, SEQ, D), 1.0),
        'c': nrm(ks[1], (BATCH, D), 1.0),
        'ctx': nrm(ks[2], (BATCH, CTX_LEN, D), 1.0),
        'c_ctx': nrm(ks[3], (D,), 1.0),
        'ada_w': nrm(ks[4], (L, D, N_SUB * 3 * D), 0.5 * D ** -0.5),
        'ada_b': nrm(ks[5], (L, N_SUB * 3 * D), 0.02),
        'norm_w': 1.0 + nrm(ks[6], (L, N_SUB, D), 0.02),
        'ffn_w1': nrm(ks[7], (L, 2, D, FFN_DIM), D ** -0.5),
        'ffn_w3': nrm(ks[8], (L, 2, D, FFN_DIM), D ** -0.5),
        'ffn_w2': nrm(ks[9], (L, 2, FFN_DIM, D), FFN_DIM ** -0.5),
        'w_in': nrm(ks[10], (L, D, IN_WIDTH), D ** -0.5),
        'conv_w': nrm(ks[11], (L, CONV_K, 3 * DN_WIDTH), CONV_K ** -0.5),
        'dn_a_log': jnp.log(jax.random.uniform(ks[12], (L, 2, DN_HEADS), f32, 1.0, 16.0)),
        'dn_dt_bias': dt + jnp.log(-jnp.expm1(-dt)),
        'dn_norm_w': 1.0 + nrm(ks[14], (L, DN_DV), 0.02),
        'sgu_w': nrm(ks[15], (L, GM_GROUPS, GM_CHUNK, GM_CHUNK), GM_CHUNK ** -0.5),
        'sgu_b': 1.0 + nrm(ks[16], (L, GM_GROUPS, GM_CHUNK), 0.02),
        'sgu_norm_w': 1.0 + nrm(ks[17], (L, GM_WIDTH), 0.02),
        'na_rpb': nrm(ks[18], (L, NA_HEADS, 2 * NA_KH_MAX - 1, 2 * NA_KW - 1), 0.1),
        'proj_a': nrm(ks[19], (L, GM_WIDTH, D), GM_WIDTH ** -0.5),
        'proj_b': nrm(ks[20], (L, DN_WIDTH, D), DN_WIDTH ** -0.5),
        'proj_c': nrm(ks[21], (L, NA_WIDTH, D), NA_WIDTH ** -0.5),
        'w_out': nrm(ks[22], (L, D, D), D ** -0.5),
        'final_norm_w': 1.0 + nrm(ks[23], (D,), 0.02),
    }


def reference(x, c, ctx, c_ctx, ada_w, ada_b, norm_w, ffn_w1, ffn_w3, ffn_w2, w_in, conv_w, dn_a_log,
              dn_dt_bias, dn_norm_w, sgu_w, sgu_b, sgu_norm_w, na_rpb, proj_a, proj_b, proj_c, w_out,
              final_norm_w):
    b = x.shape[0]
    h, hc = x, ctx
    for l in range(DEPTH):
        last = l == DEPTH - 1
        m = (jax.nn.silu(c) @ ada_w[l] + ada_b[l]).reshape(b, 1, N_SUB, 3, D_MODEL)
        mc = (jax.nn.silu(c_ctx) @ ada_w[l] + ada_b[l]).reshape(N_SUB, 3, D_MODEL)
        h = ffn_sublayer(h, m, 0, norm_w[l, 0], ffn_w1[l, 0], ffn_w3[l, 0], ffn_w2[l, 0])
        hc = ffn_sublayer(hc, mc, 0, norm_w[l, 0], ffn_w1[l, 0], ffn_w3[l, 0], ffn_w2[l, 0])
        y, yc = mixer(adaln(h, norm_w[l, 1], m, 1), adaln(hc, norm_w[l, 1], mc, 1), w_in[l], conv_w[l],
                      dn_a_log[l], dn_dt_bias[l], dn_norm_w[l], sgu_w[l], sgu_b[l], sgu_norm_w[l], na_rpb[l],
                      proj_a[l], proj_b[l], proj_c[l], w_out[l], not last)
        h = h + m[..., 1, 2, :] * y
        h = ffn_sublayer(h, m, 2, norm_w[l, 2], ffn_w1[l, 1], ffn_w3[l, 1], ffn_w2[l, 1])
        if not last:
            hc = hc + mc[1, 2] * yc
            hc = ffn_sublayer(hc, mc, 2, norm_w[l, 2], ffn_w1[l, 1], ffn_w3[l, 1], ffn_w2[l, 1])
    return rmsnorm(h, final_norm_w)
```

```python
import os
import numpy as np
from contextlib import ExitStack
import concourse.bass as bass
import concourse.mybir as mybir
from concourse.bass_utils import run_bass_kernel_spmd

F32 = mybir.dt.float32
BF16 = mybir.dt.bfloat16
AF = mybir.ActivationFunctionType
ALU = mybir.AluOpType
AX = mybir.AxisListType

NCORES = 8


class Buf:
    def __init__(self, prog, t, name):
        self.prog = prog
        self.t = t
        self.name = name
        self.w = None
        self.r = []
        self.dsem = None
        self.excl = False

    def __getitem__(self, idx):
        return self.t[idx]


class _Rec:
    def __init__(self):
        self.calls = []

    def __getattr__(self, name):
        def f(*a, **kw):
            self.calls.append((name, a, kw))
            return None
        return f


class Prog:
    ENG = ("pe", "act", "dve", "pool", "sp")

    def __init__(self, nc, stack):
        self.nc = nc
        self.stack = stack
        self.q = {e: [] for e in self.ENG}
        self.sems = {}
        self.val = {}
        self.seen = {e: {} for e in self.ENG}
        for e in self.ENG:
            self._newsem("E_" + e)
        self.nbuf = 0
        self.out_deps = []

    def _newsem(self, key):
        s = self.stack.enter_context(self.nc.semaphore(key))
        self.sems[key] = s
        self.val[key] = 0
        return key

    def sbuf(self, shape, dtype, name=None):
        self.nbuf += 1
        name = name or f"sb{self.nbuf}"
        t = self.stack.enter_context(self.nc.sbuf_tensor(name, list(shape), dtype))
        return Buf(self, t, name)

    def psum(self, shape, dtype=F32, name=None):
        self.nbuf += 1
        name = name or f"ps{self.nbuf}"
        t = self.stack.enter_context(self.nc.psum_tensor(name, list(shape), dtype))
        b = Buf(self, t, name)
        b.excl = True
        return b

    def dram(self, name, shape, dtype, kind="Internal"):
        t = self.nc.dram_tensor(name, list(shape), dtype, kind=kind)
        return Buf(self, t.ap(), name)

    def _waits(self, eng, reads, writes):
        deps = []
        for b in reads:
            if b.w is not None:
                deps.append(b.w)
        for b in writes:
            if b.w is not None:
                deps.append(b.w)
            deps.extend(b.r)
        need = {}
        for (k, v, e) in deps:
            if e == "pe" and eng == "pe":
                continue
            if self.seen[eng].get(k, 0) >= v:
                continue
            need[k] = max(need.get(k, 0), v)
        for k, v in need.items():
            self.seen[eng][k] = v
            self.q[eng].append(("wait", k, v))

    def _mark(self, dep, reads, writes):
        for b in writes:
            b.w = dep
            b.r = []
        for b in reads:
            if b not in writes:
                b.r.append(dep)
                if len(b.r) > 24:
                    m = {}
                    for (k, v, e) in b.r:
                        if k not in m or m[k][1] < v:
                            m[k] = (k, v, e)
                    b.r = list(m.values())

    def op(self, eng, fn, reads=(), writes=()):
        reads = [b for b in reads if b is not None]
        writes = [b for b in writes if b is not None]
        xr = [b for b in reads if getattr(b, "excl", False)]
        if xr:
            writes = writes + [b for b in xr if b not in writes]
        self._waits(eng, reads, writes)
        k = "E_" + eng
        self.val[k] += 1
        dep = (k, self.val[k], eng)
        rec = _Rec()
        fn(rec)
        assert len(rec.calls) == 1
        m_, a_, kw_ = rec.calls[0]
        self.q[eng].append(("op", (lambda e, m_=m_, a_=a_, kw_=kw_: getattr(e, m_)(*a_, **kw_)), k, 1))
        self._mark(dep, reads, writes)
        return dep

    def dma(self, eng, out_b, out_ap, in_b, in_ap, is_output=False, **kw):
        reads = [in_b]
        writes = [out_b]
        self._waits(eng, reads, writes)
        if out_b.dsem is None:
            out_b.dsem = self._newsem("D_" + out_b.name)
        k = out_b.dsem
        self.val[k] += 16
        dep = (k, self.val[k], "dma")
        self.q[eng].append(("op", (lambda e, o=out_ap, i=in_ap, kw=kw: e.dma_start(out=o, in_=i, **kw)), k, 16))
        self._mark(dep, reads, writes)
        if is_output:
            self.out_deps.append(dep)
        return dep

    def finish(self, eng="sp"):
        need = {}
        for (k, v, e) in self.out_deps:
            need[k] = max(need.get(k, 0), v)
        for k, v in need.items():
            self.q[eng].append(("wait", k, v))

    def emit(self):
        nc = self.nc
        with nc.Block() as block:
            def run(engine, name):
                for it in self.q[name]:
                    if it[0] == "wait":
                        engine.wait_ge(self.sems[it[1]], it[2])
                    else:
                        it[1](engine).then_inc(self.sems[it[2]], it[3])

            @block.sync
            def _(e):
                run(e, "sp")

            @block.tensor
            def _(e):
                run(e, "pe")

            @block.scalar
            def _(e):
                run(e, "act")

            @block.vector
            def _(e):
                run(e, "dve")

            @block.gpsimd
            def _(e):
                run(e, "pool")


def new_prog():
    nc = bass.Bass("TRN2", target_bir_lowering=False)
    stack = ExitStack()
    return nc, stack, Prog(nc, stack)


D = 4096
MODW = 9 * D
MODC = MODW // NCORES


def build_mod(nlayers=2):
    nc, stack, P = new_prog()
    with stack:
        cT = nc.dram_tensor("cT", [128, 96], F32, kind="ExternalInput").ap()
        adaw = nc.dram_tensor("adaw", [nlayers * D, MODC], F32, kind="ExternalInput").ap()
        adab = nc.dram_tensor("adab", [nlayers * 3, MODC], F32, kind="ExternalInput").ap()
        mod = nc.dram_tensor("mod", [nlayers * 3, MODC], F32, kind="ExternalOutput").ap()
        B_in = Buf(P, None, "dram_in")
        B_out = Buf(P, None, "dram_out")
        c_sb = P.sbuf([128, 96], F32, "c_sb")
        cs = P.sbuf([128, 96], F32, "cs")
        P.dma("sp", c_sb, c_sb[:, :], B_in, cT[:, :])
        P.op("act", lambda e: e.activation(out=cs[:, :], in_=c_sb[:, :], func=AF.Silu), [c_sb], [cs])
        HALF = MODC // 2
        wbufs = [P.sbuf([128, HALF], F32, f"w{i}") for i in range(3)]
        pss = [P.psum([128, 512], F32, f"acc{i}") for i in range(5)]
        bsb = P.sbuf([3, MODC], F32, "bsb")
        osb = P.sbuf([3, MODC], F32, "osb")
        tiles = [(o, min(512, HALF - o)) for o in range(0, HALF, 512)]
        it = 0
        for l in range(nlayers):
            P.dma("sp", bsb, bsb[:, :], B_in, adab[l * 3:(l + 1) * 3, :])
            for h in range(2):
                for kc in range(32):
                    wb = wbufs[it % 3]
                    it += 1
                    q = "sp" if kc % 2 == 0 else "act"
                    P.dma(q, wb, wb[:, :], B_in,
                          adaw[l * D + kc * 128:l * D + (kc + 1) * 128, h * HALF:(h + 1) * HALF])
                    for ti, (o, n) in enumerate(tiles):
                        P.op("pe", lambda e, ti=ti, o=o, n=n, kc=kc, wb=wb: e.matmul(
                            pss[ti][0:3, 0:n], lhsT=cs[:, kc * 3:(kc + 1) * 3], rhs=wb[:, o:o + n],
                            start=(kc == 0), stop=(kc == 31)), [cs, wb], [pss[ti]])
                for ti, (o, n) in enumerate(tiles):
                    c0 = h * HALF + o
                    P.op("dve", lambda e, ti=ti, n=n, c0=c0: e.tensor_tensor(
                        out=osb[:, c0:c0 + n], in0=pss[ti][0:3, 0:n], in1=bsb[:, c0:c0 + n], op=ALU.add),
                        [pss[ti], bsb], [osb])
            P.dma("sp", B_out, mod[l * 3:(l + 1) * 3, :], osb, osb[:, :], is_output=True)
        P.finish("sp")
        P.emit()
    return nc


def run_mod(c, c_ctx, ada_w, ada_b):
    L = ada_w.shape[0]
    call = np.concatenate([c, c_ctx[None, :]], axis=0).astype(np.float32)
    cT = np.ascontiguousarray(call.reshape(3, 32, 128).transpose(2, 1, 0)).reshape(128, 96)
    nc = build_mod(L)
    in_maps = []
    for i in range(NCORES):
        sl = slice(i * MODC, (i + 1) * MODC)
        in_maps.append({
            "cT": cT,
            "adaw": np.ascontiguousarray(ada_w[:, :, sl]).reshape(L * D, MODC),
            "adab": np.ascontiguousarray(np.repeat(ada_b[:, None, sl], 3, axis=1)).reshape(L * 3, MODC),
        })
    res = run_bass_kernel_spmd(nc, in_maps, core_ids=list(range(NCORES)))
    mod = np.concatenate([r["mod"].reshape(L, 3, MODC) for r in res.results], axis=2)
    return mod.reshape(L, 3, 3, 3, D)


def _prog_coll(self, kind, in_b, in_t, out_b, out_t, groups=None):
    eng = "pool"
    self._waits(eng, [in_b], [out_b])
    k = "C_" + out_b.name
    if k not in self.sems:
        self._newsem(k)
    self.val[k] += 1
    dep = (k, self.val[k], "dma")
    groups = groups or [list(range(NCORES))]
    self.q[eng].append(("op", (lambda e, o=out_t, i=in_t: e.collective_compute(
        kind, ALU.bypass, replica_groups=groups, ins=[i.opt()], outs=[o.opt()])), k, 1))
    self._mark(dep, [in_b], [out_b])
    return dep


Prog.coll = _prog_coll


def _prog_barrier(self):
    for e in self.ENG:
        for k, v in self.val.items():
            if v > 0 and self.seen[e].get(k, 0) < v:
                self.seen[e][k] = v
                self.q[e].append(("wait", k, v))


class _Scope:
    def __init__(self, prog):
        self.prog = prog
        self.stack = ExitStack()

    def __enter__(self):
        self.stack.__enter__()
        return self

    def __exit__(self, *a):
        self.prog.barrier()
        return self.stack.__exit__(*a)

    def sbuf(self, shape, dtype, name=None):
        P = self.prog
        P.nbuf += 1
        name = (name or "sb") + f"_{P.nbuf}"
        t = self.stack.enter_context(P.nc.sbuf_tensor(name, list(shape), dtype))
        return Buf(P, t, name)

    def psum(self, shape, dtype=F32, name=None):
        P = self.prog
        P.nbuf += 1
        name = (name or "ps") + f"_{P.nbuf}"
        t = self.stack.enter_context(P.nc.psum_tensor(name, list(shape), dtype))
        b = Buf(P, t, name)
        b.excl = True
        return b


def _prog_scope(self):
    return _Scope(self)


Prog.barrier = _prog_barrier
Prog.scope = _prog_scope


def _prog_dma(self, eng, out_b, out_ap, in_b, in_ap, is_output=False, **kw):
    reads = [in_b]
    writes = [out_b]
    self._waits(eng, reads, writes)
    if out_b.dsem is None:
        pool = getattr(self, "_dpool", None)
        if pool is None:
            pool = self._dpool = []
        if pool:
            out_b.dsem = pool.pop()
        else:
            out_b.dsem = self._newsem(f"D{len(self.sems)}")
    k = out_b.dsem
    self.val[k] += 16
    dep = (k, self.val[k], "dma")
    self.q[eng].append(("op", (lambda e, o=out_ap, i=in_ap, kw=kw: e.dma_start(out=o, in_=i, **kw)), k, 16))
    self._mark(dep, reads, writes)
    if is_output:
        self.out_deps.append(dep)
    return dep


Prog.dma = _prog_dma

_scope_sbuf0 = _Scope.sbuf
_scope_exit0 = _Scope.__exit__


def _scope_sbuf(self, shape, dtype, name=None):
    b = _scope_sbuf0(self, shape, dtype, name)
    if not hasattr(self, "bufs"):
        self.bufs = []
    self.bufs.append(b)
    return b


def _scope_exit(self, *a):
    r = _scope_exit0(self, *a)
    P = self.prog
    if not hasattr(P, "_dpool"):
        P._dpool = []
    for b in getattr(self, "bufs", []):
        if b.dsem is not None:
            P._dpool.append(b.dsem)
            b.dsem = None
    return r


_Scope.sbuf = _scope_sbuf
_Scope.__exit__ = _scope_exit


class Cfg:
    def __init__(self, D, FF, SEQ, CTX, GW, DNH, NAH, GMG, L, TB):
        self.D, self.FF, self.SEQ, self.CTX, self.GW = D, FF, SEQ, CTX, GW
        self.DNH, self.NAH, self.GMG, self.L, self.TB = DNH, NAH, GMG, L, TB
        self.KC = D // 128
        self.FC = FF // 128
        self.T = SEQ + CTX
        self.ROWS = SEQ // GW
        self.DNW, self.NAW, self.GMW = DNH * 128, NAH * 128, GMG * 128
        self.NCK = self.T // 64
        self.plan = [("dn", "fm", 3 * DNH), ("na", "fm", 2 * NAH), ("gu", "fm", GMG), ("g", "fm", 3 * self.KC),
                     ("nav", "tm", NAH), ("dg", "tm", DNH), ("gv", "tm", GMG), ("ab", "tm", 1)]
        self.NCH = sum(p[2] for p in self.plan)

    def blocks(self):
        out = []
        t = 0
        while t < self.T:
            n = min(self.TB, self.T - t)
            out.append((t, n))
            t += n
        return out

    def segs(self, t0, n):
        out = []
        if t0 < self.SEQ:
            m = min(n, self.SEQ - t0)
            out.append((0, m, 0))
            if m < n:
                out.append((m, n - m, 1))
        else:
            out.append((0, n, 1))
        return out


def tiles_of(n, w=512):
    out = []
    o = 0
    while o < n:
        m = min(w, n - o)
        out.append((o, m))
        o += m
    return out


FULL = Cfg(D=4096, FF=5632, SEQ=4096, CTX=256, GW=64, DNH=8, NAH=16, GMG=8, L=2, TB=1088)

GELU_C = 1.5957691216057308


class Main:
    def __init__(self, cfg, dbg=False, nlayers=None, stop_after=None):
        self.cfg = cfg
        self.dbg = dbg
        self.nl = nlayers or cfg.L
        self.stop_after = stop_after
        self.nc, self.stack, self.P = new_prog()
        self.qi = 0

    def din(self, name, shape, dtype=F32):
        ap = self.nc.dram_tensor(name, list(shape), dtype, kind="ExternalInput").ap()
        return Buf(self.P, ap, name)

    def dscr(self, name, shape, dtype=F32):
        kind = "ExternalOutput" if self.dbg else "Internal"
        ap = self.nc.dram_tensor(name, list(shape), dtype, kind=kind).ap()
        return Buf(self.P, ap, name)

    def dmaq(self):
        return "sp"

    def load_consts(self):
        P, c = self.P, self.cfg
        self.cst = self.din("consts", [128, CONST_W])
        self.csb = P.sbuf([128, CONST_W], F32, "csb")
        P.dma("sp", self.csb, self.csb[:, :], self.cst, self.cst[:, :])
        self.ones_bf = P.sbuf([128, 128], BF16, "ones_bf")
        P.op("dve", lambda e: e.tensor_copy(out=self.ones_bf[:, :], in_=self.csb[:, C_ONES:C_ONES + 128]),
             [self.csb], [self.ones_bf])
        self.ident_bf = P.sbuf([128, 128], BF16, "ident_bf")
        P.op("dve", lambda e: e.tensor_copy(out=self.ident_bf[:, :], in_=self.csb[:, C_ID:C_ID + 128]),
             [self.csb], [self.ident_bf])

    def mod_scalars(self, S, l, sub, half):
        P, c = self.P, self.cfg
        KC = c.KC
        a, sh, gt = [], [], []
        for grp in range(2):
            base = (((l * 2 + grp) * 3 + sub) * 3) * KC
            shift_ap = lambda b=base: self.modsb[:, b:b + KC]
            scale_ap = lambda b=base: self.modsb[:, b + KC:b + 2 * KC]
            gate_ap = lambda b=base: self.modsb[:, b + 2 * KC:b + 3 * KC]
            nb = (l * 3 + sub) * KC
            at = S.sbuf([128, KC], F32, "a_sc")
            P.op("dve", lambda e, at=at, sa=scale_ap, nb=nb: e.scalar_tensor_tensor(
                out=at[:, :], in0=sa(), scalar=1.0, in1=self.normsb[:, nb:nb + KC], op0=ALU.add, op1=ALU.mult),
                [self.modsb, self.normsb], [at])
            st = S.sbuf([128, KC], F32, "shift")
            P.op("dve", lambda e, st=st, sa=shift_ap: e.tensor_copy(out=st[:, :], in_=sa()), [self.modsb], [st])
            g = S.sbuf([128, KC], F32, "gate")
            P.op("dve", lambda e, g=g, ga=gate_ap: e.tensor_scalar(
                out=g[:, :], in0=ga(), scalar1=(0.5 if half else 1.0), scalar2=None, op0=ALU.mult),
                [self.modsb], [g])
            a.append(at)
            sh.append(st)
            gt.append(g)
        return a, sh, gt

    def norm_block(self, S, hsrc, t0, n, a, sh, xin, hst, sqt, pss, tmp, rstd_b):
        P, c = self.P, self.cfg
        KC, D = c.KC, c.D
        for si, (o, m) in enumerate(tiles_of(n, 128)):
            hs = hst[si % 2]
            P.dma(self.dmaq(), hs, hs[:, 0:KC * m].rearrange("p (k t) -> p k t", k=KC), hsrc,
                  hsrc.t[:, t0 + o:t0 + o + m].rearrange("(k p) t -> p k t", p=128))
            P.op("pool", lambda e, hs=hs, m=m: e.tensor_tensor(
                out=sqt[:, 0:KC * m], in0=hs[:, 0:KC * m], in1=hs[:, 0:KC * m], op=ALU.mult), [hs], [sqt])
            ps = pss[si % 2]
            for kc in range(KC):
                P.op("pe", lambda e, ps=ps, kc=kc, m=m: e.matmul(
                    ps[:, 0:m], lhsT=self.ones_bf[:, :], rhs=sqt[:, kc * m:(kc + 1) * m],
                    start=(kc == 0), stop=(kc == KC - 1)), [self.ones_bf, sqt], [ps])
            P.op("act", lambda e, ps=ps, m=m: e.activation(
                out=rstd_b[:, 0:m], in_=ps[:, 0:m], func=AF.Sqrt, bias=self.eps_ap(), scale=1.0 / D),
                [ps, self.csb], [rstd_b])
            P.op("dve", lambda e, m=m: e.reciprocal(out=rstd_b[:, 0:m], in_=rstd_b[:, 0:m]), [rstd_b], [rstd_b])
            for (so, sn, grp) in c.segs(t0 + o, m):
                for kc in range(KC):
                    tb = tmp[kc % 2]
                    P.op("dve", lambda e, tb=tb, hs=hs, kc=kc, m=m, so=so, sn=sn, grp=grp: e.scalar_tensor_tensor(
                        out=tb[:, 0:sn], in0=hs[:, kc * m + so:kc * m + so + sn], scalar=a[grp][:, kc:kc + 1],
                        in1=rstd_b[:, so:so + sn], op0=ALU.mult, op1=ALU.mult), [hs, a[grp], rstd_b], [tb])
                    P.op("act", lambda e, tb=tb, kc=kc, o=o, so=so, sn=sn, grp=grp: e.activation(
                        out=xin[:, kc, o + so:o + so + sn], in_=tb[:, 0:sn], func=AF.Identity,
                        bias=sh[grp][:, kc:kc + 1], scale=1.0), [tb, sh[grp]], [xin])

    def eps_ap(self):
        return self.csb[:, C_EPS:C_EPS + 1]

    def linear(self, S, xin, KCn, wsrc, chunks, n, w32, w16, psb, epi_fm=None, epi_tm=None):
        P = self.P
        tls = tiles_of(n, 512)
        subt = tiles_of(n, 128)
        cnt = getattr(self, "_lin_cnt", 0)
        for (ci, mode, user) in chunks:
            wb32 = w32[cnt % 2]
            wb16 = w16[cnt % 2]
            W = KCn * 128
            P.dma("sp", wb32, wb32[:, 0:W], wsrc, wsrc.t[ci, :, 0:W])
            hW = (W // 2)
            P.op("act", lambda e, a=wb16, b=wb32, hW=hW: e.activation(out=a[:, 0:hW], in_=b[:, 0:hW], func=AF.Copy),
                 [wb32], [wb16])
            P.op("pool", lambda e, a=wb16, b=wb32, hW=hW, W=W: e.tensor_copy(out=a[:, hW:W], in_=b[:, hW:W]),
                 [wb32], [wb16])
            if mode == "fm":
                half = (cnt % 2) * 4
                for ti, (o, m) in enumerate(tls):
                    ps = psb[half + ti]
                    for kc in range(KCn):
                        P.op("pe", lambda e, ps=ps, kc=kc, o=o, m=m, wb16=wb16: e.matmul(
                            ps[:, 0:m], lhsT=wb16[:, kc * 128:(kc + 1) * 128], rhs=xin[:, kc, o:o + m],
                            start=(kc == 0), stop=(kc == KCn - 1)), [wb16, xin], [ps])
                    epi_fm(user, ti, o, m, ps)
            else:
                for si, (o, m) in enumerate(subt):
                    ps = psb[(cnt % 2) * 4 + (si % 4)]
                    for kc in range(KCn):
                        P.op("pe", lambda e, ps=ps, kc=kc, o=o, m=m, wb16=wb16: e.matmul(
                            ps[0:m, 0:128], lhsT=xin[:, kc, o:o + m], rhs=wb16[:, kc * 128:(kc + 1) * 128],
                            start=(kc == 0), stop=(kc == KCn - 1)), [wb16, xin], [ps])
                    epi_tm(user, si, o, m, ps)
            cnt += 1
        self._lin_cnt = cnt

    def ffn(self, l, i, hsrc, hdst):
        P, c = self.P, self.cfg
        KC, FC = c.KC, c.FC
        sub = 0 if i == 0 else 2
        TB2 = c.TB // 2
        with P.scope() as S:
            a, sh, gt = self.mod_scalars(S, l, sub, True)
            WMAX = max(KC, FC) * 128
            XW = max(KC * c.TB, FC * TB2)
            xin_b = S.sbuf([128, XW], BF16, "xin")
            xin1 = xin_b[:, 0:KC * c.TB].rearrange("p (k t) -> p k t", k=KC)
            xin2 = xin_b[:, 0:FC * TB2].rearrange("p (k t) -> p k t", k=FC)
            X1 = Buf(P, xin1, "x1v"); X1 = _alias(xin_b, xin1)
            X2 = _alias(xin_b, xin2)
            w32 = [S.sbuf([128, WMAX], F32, f"w32_{j}") for j in range(2)]
            w16 = [S.sbuf([128, WMAX], BF16, f"w16_{j}") for j in range(2)]
            sqt = S.sbuf([128, KC * 128], BF16, "sqt")
            tmp = [S.sbuf([128, 512], F32, f"tmp{j}") for j in range(2)]
            rstd_b = S.sbuf([128, 128], F32, "rstd")
            s1 = [S.sbuf([128, 512], F32, f"s1_{j}") for j in range(len(tiles_of(c.TB)))]
            orow = [S.sbuf([128, c.TB], BF16, f"orow{j}") for j in range(2)]
            hrow = [S.sbuf([128, 512], F32, f"hrow{j}") for j in range(2)]
            orow32 = [S.sbuf([128, 512], F32, f"orow32{j}") for j in range(2)]
            psb = [S.psum([128, 512], F32, f"psb{j}") for j in range(8)]
            for (t0, n) in c.blocks():
                self.norm_block(S, hsrc, t0, n, a, sh, X1, w32, sqt, psb[0:2], tmp, rstd_b)

                def epi1(user, ti, o, m, ps, t0=t0, n=n):
                    f, which = user
                    if which == 0:
                        P.op("act", lambda e, ps=ps, m=m, ti=ti: e.activation(
                            out=s1[ti][:, 0:m], in_=ps[:, 0:m], func=AF.Silu), [ps], [s1[ti]])
                    else:
                        ob = orow[f % 2]
                        P.op("dve", lambda e, ps=ps, m=m, o=o, ob=ob, ti=ti: e.tensor_tensor(
                            out=ob[:, o:o + m], in0=ps[:, 0:m], in1=s1[ti][:, 0:m], op=ALU.mult),
                            [ps, s1[ti]], [ob])
                        if o + m == n:
                            P.dma("sp", self.gT, self.gT.t[f * 128:(f + 1) * 128, t0:t0 + n], ob, ob[:, 0:n])

                chunks = []
                for f in range(FC):
                    chunks.append((f, "fm", (f, 0)))
                    chunks.append((FC + f, "fm", (f, 1)))
                self.linear(S, X1, KC, self.w13[l][i], chunks, n, w32, w16, psb, epi_fm=epi1)
            cnt2 = [0]
            for (t0, n) in [(t, min(TB2, c.T - t)) for t in range(0, c.T, TB2)]:
                P.dma("sp", X2, X2[:, 0:FC, 0:n], self.gT,
                      self.gT.t[:, t0:t0 + n].rearrange("(k p) t -> p k t", p=128))

                def epi2(user, ti, o, m, ps, t0=t0, n=n):
                    d = user
                    j = cnt2[0] % 2
                    cnt2[0] += 1
                    hr, o32 = hrow[j], orow32[j]
                    P.dma("sp", hr, hr[:, 0:m], hsrc, hsrc.t[d * 128:(d + 1) * 128, t0 + o:t0 + o + m])
                    for (so, sn, grp) in c.segs(t0 + o, m):
                        P.op("dve", lambda e, ps=ps, so=so, sn=sn, grp=grp, d=d, hr=hr, o32=o32: e.scalar_tensor_tensor(
                            out=o32[:, so:so + sn], in0=ps[:, so:so + sn], scalar=gt[grp][:, d:d + 1],
                            in1=hr[:, so:so + sn], op0=ALU.mult, op1=ALU.add), [ps, gt[grp], hr], [o32])
                    P.dma("sp", hdst, hdst.t[d * 128:(d + 1) * 128, t0 + o:t0 + o + m], o32, o32[:, 0:m])

                self.linear(S, X2, FC, self.w2[l][i], [(d, "fm", d) for d in range(KC)], n, w32, w16, psb,
                            epi_fm=epi2)


class _AliasBuf:
    def __init__(self, parent, ap):
        object.__setattr__(self, "_p", parent)
        object.__setattr__(self, "_ap", ap)

    def __getattr__(self, k):
        return getattr(object.__getattribute__(self, "_p"), k)

    def __setattr__(self, k, v):
        setattr(object.__getattribute__(self, "_p"), k, v)

    def __getitem__(self, idx):
        return object.__getattribute__(self, "_ap")[idx]

    def __eq__(self, o):
        return _root(self) is _root(o)

    def __hash__(self):
        return id(_root(self))


def _root(b):
    while isinstance(b, _AliasBuf):
        b = object.__getattribute__(b, "_p")
    return b


def _alias(parent, ap):
    return _AliasBuf(parent, ap)


C_ONES, C_ID, C_NEG1, C_PERM = 0, 128, 256, 384
C_CUMF, C_CUMB, C_MSF, C_MSB, C_MITF, C_MITB = 512, 576, 640, 704, 768, 832
C_EPS = 896
CONST_W = 904
NEGBIG = -30000.0


def make_consts():
    c = np.zeros((128, CONST_W), np.float32)
    c[:, C_ONES:C_ONES + 128] = 1.0
    c[:, C_ID:C_ID + 128] = np.eye(128, dtype=np.float32)
    c[:, C_NEG1:C_NEG1 + 128] = -1.0
    pm = np.zeros((128, 128), np.float32)
    for m in range(128):
        q = m // 32
        partner = m + 32 if q % 2 == 0 else m - 32
        pm[partner, m] = 1.0
    c[:, C_PERM:C_PERM + 128] = pm
    i = np.arange(64)
    c[:64, C_CUMF:C_CUMF + 64] = (i[:, None] <= i[None, :])
    c[:64, C_CUMB:C_CUMB + 64] = (i[:, None] >= i[None, :])
    c[:64, C_MSF:C_MSF + 64] = np.where(i[None, :] < i[:, None], 0.0, NEGBIG)
    c[:64, C_MSB:C_MSB + 64] = np.where(i[None, :] > i[:, None], 0.0, NEGBIG)
    c[:64, C_MITF:C_MITF + 64] = np.where(i[:, None] <= i[None, :], 0.0, NEGBIG)
    c[:64, C_MITB:C_MITB + 64] = np.where(i[:, None] >= i[None, :], 0.0, NEGBIG)
    c[:, C_EPS] = 1e-6
    c[:, C_EPS + 1] = np.log(128.0 ** -0.5)
    return c


def _gelu(self, S, src, rows, m, dst_fn, reads, writes, k):
    P = self.P
    gx, g1 = self._gx[k % 2], self._g1[k % 2]
    P.op("act", lambda e: e.activation(out=gx[0:rows, 0:m], in_=src, func=AF.Copy), reads, [gx])
    P.op("dve", lambda e: e.tensor_tensor(out=g1[0:rows, 0:m], in0=gx[0:rows, 0:m], in1=gx[0:rows, 0:m], op=ALU.mult),
         [gx], [g1])
    P.op("pool", lambda e: e.tensor_scalar(out=g1[0:rows, 0:m], in0=g1[0:rows, 0:m], scalar1=0.044715, scalar2=1.0,
                                           op0=ALU.mult, op1=ALU.add), [g1], [g1])
    P.op("pool", lambda e: e.tensor_tensor(out=g1[0:rows, 0:m], in0=g1[0:rows, 0:m], in1=gx[0:rows, 0:m], op=ALU.mult),
         [g1, gx], [g1])
    P.op("act", lambda e: e.activation(out=g1[0:rows, 0:m], in_=g1[0:rows, 0:m], func=AF.Sigmoid, scale=GELU_C),
         [g1], [g1])
    P.op("dve", lambda e: e.tensor_tensor(out=dst_fn(), in0=gx[0:rows, 0:m], in1=g1[0:rows, 0:m], op=ALU.mult),
         [gx, g1], writes)


Main.gelu = _gelu


def _inproj(self, l, hsrc):
    P, c = self.P, self.cfg
    KC = c.KC
    with P.scope() as S:
        a, sh, gt = self.mod_scalars(S, l, 1, False)
        xin_b = S.sbuf([128, KC * c.TB], BF16, "xin")
        X1 = _alias(xin_b, xin_b[:, :].rearrange("p (k t) -> p k t", k=KC))
        w32 = [S.sbuf([128, KC * 128], F32, f"w32_{j}") for j in range(2)]
        w16 = [S.sbuf([128, KC * 128], BF16, f"w16_{j}") for j in range(2)]
        sqt = S.sbuf([128, KC * 128], BF16, "sqt")
        tmp = [S.sbuf([128, 512], F32, f"tmp{j}") for j in range(2)]
        rstd_b = S.sbuf([128, 128], F32, "rstd")
        self._gx = [S.sbuf([128, 512], F32, f"gx{j}") for j in range(2)]
        self._g1 = [S.sbuf([128, 512], F32, f"g1{j}") for j in range(2)]
        NS = len(tiles_of(c.TB, 128))
        orow32 = [S.sbuf([128, c.TB], F32, f"or32_{j}") for j in range(2)]
        orow16 = [S.sbuf([128, c.TB], BF16, f"or16_{j}") for j in range(2)]
        otm32 = [S.sbuf([128, NS * 128], F32, f"ot32_{j}") for j in range(2)]
        otm16 = [S.sbuf([128, NS * 128], BF16, f"ot16_{j}") for j in range(2)]
        psb = [S.psum([128, 512], F32, f"psb{j}") for j in range(8)]
        dst_fm = {"dn": self.z_dn, "na": self.z_na, "gu": self.z_gu, "g": self.z_g}
        dst_tm = {"nav": self.v_na, "dg": self.z_dg, "gv": self.z_gv, "ab": self.z_ab}
        chunks = []
        ci = 0
        for (name, mode, nch) in c.plan:
            for j in range(nch):
                chunks.append((ci, mode, (name, j, ci)))
                ci += 1
        gk = [0]
        for (t0, n) in c.blocks():
            self.norm_block(S, hsrc, t0, n, a, sh, X1, w32, sqt, psb[0:2], tmp, rstd_b)

            def epi_fm(user, ti, o, m, ps, t0=t0, n=n):
                name, j, ci = user
                ob = (orow16 if name == "na" else orow32)[ci % 2]
                if name in ("dn", "na"):
                    P.op("act", lambda e: e.activation(out=ob[:, o:o + m], in_=ps[:, 0:m], func=AF.Copy), [ps], [ob])
                elif name == "g":
                    P.op("act", lambda e: e.activation(out=ob[:, o:o + m], in_=ps[:, 0:m], func=AF.Sigmoid), [ps], [ob])
                else:
                    gk[0] += 1
                    self.gelu(S, ps[:, 0:m], 128, m, lambda: ob[:, o:o + m], [ps], [ob], gk[0])
                if o + m == n:
                    d = dst_fm[name]
                    P.dma("sp", d, d.t[j * 128:(j + 1) * 128, t0:t0 + n], ob, ob[:, 0:n])

            def epi_tm(user, si, o, m, ps, t0=t0, n=n):
                name, j, ci = user
                ob = (otm16 if name == "nav" else otm32)[ci % 2]
                if name in ("nav", "ab"):
                    P.op("act", lambda e: e.activation(out=ob[0:m, si * 128:(si + 1) * 128], in_=ps[0:m, 0:128],
                                                       func=AF.Copy), [ps], [ob])
                elif name == "dg":
                    P.op("act", lambda e: e.activation(out=ob[0:m, si * 128:(si + 1) * 128], in_=ps[0:m, 0:128],
                                                       func=AF.Silu), [ps], [ob])
                else:
                    gk[0] += 1
                    self.gelu(S, ps[0:m, 0:128], m, 128, lambda: ob[0:m, si * 128:(si + 1) * 128], [ps], [ob], gk[0])
                if o + m == n:
                    d = dst_tm[name]
                    nfull = n // 128
                    if nfull:
                        P.dma("sp", d, d.t[t0:t0 + nfull * 128, j * 128:(j + 1) * 128].rearrange("(s p) c -> p s c", p=128),
                              ob, ob[:, 0:nfull * 128].rearrange("p (s c) -> p s c", c=128))
                    rem = n - nfull * 128
                    if rem:
                        P.dma("sp", d, d.t[t0 + nfull * 128:t0 + n, j * 128:(j + 1) * 128],
                              ob, ob[0:rem, nfull * 128:(nfull + 1) * 128])

            self.linear(S, X1, KC, self.win[l], chunks, n, w32, w16, psb, epi_fm=epi_fm, epi_tm=epi_tm)


Main.inproj = _inproj


def _gmlp(self, l):
    P, c = self.P, self.cfg
    G = c.GMG
    GW_ = c.GMW
    with P.scope() as S:
        sgu = S.sbuf([128, G * 128], F32, "sgu32")
        P.dma("sp", sgu, sgu[:, :], self.sguT, self.sguT.t[:, l * G * 128:(l + 1) * G * 128])
        sgu16 = S.sbuf([128, G * 128], BF16, "sgu16")
        P.op("act", lambda e: e.activation(out=sgu16[:, :], in_=sgu[:, :], func=AF.Copy), [sgu], [sgu16])
        bbc = S.sbuf([128, G * 128], F32, "bbc")
        P.dma("sp", bbc, bbc[:, :], self.sgub, self.sgub.t[:, l * G * 128:(l + 1) * G * 128])
        nwb = S.sbuf([128, GW_], F32, "nwb")
        P.dma("sp", nwb, nwb[:, :], self.sgunw, self.sgunw.t[:, l * GW_:(l + 1) * GW_])
        vin = [S.sbuf([128, GW_], F32, f"vin{j}") for j in range(2)]
        uin = [S.sbuf([128, G * 128], F32, f"uin{j}") for j in range(2)]
        v16 = [S.sbuf([128, GW_], BF16, f"v16{j}") for j in range(2)]
        sq = S.sbuf([128, GW_], F32, "sq")
        st = [S.sbuf([128, 4], F32, f"st{j}") for j in range(2)]
        yo = [S.sbuf([128, G * 128], BF16, f"yo{j}") for j in range(2)]
        t32 = [S.sbuf([128, 128], F32, f"t32{j}") for j in range(2)]
        ps = [S.psum([128, 512], F32, f"gps{j}") for j in range(4)]
        for ck in range(c.T // 128):
            t0 = ck * 128
            j = ck % 2
            v, u, vb, s_, y = vin[j], uin[j], v16[j], st[j], yo[j]
            P.dma("sp", v, v[:, :], self.z_gv, self.z_gv.t[t0:t0 + 128, :])
            P.dma("sp", u, u[:, :].rearrange("p (g t) -> p g t", g=G), self.z_gu,
                  self.z_gu.t[:, t0:t0 + 128].rearrange("(g p) t -> p g t", p=128))
            P.op("dve", lambda e, v=v, s_=s_: e.tensor_reduce(out=s_[:, 0:1], in_=v[:, :], axis=AX.X, op=ALU.add),
                 [v], [s_])
            P.op("dve", lambda e, s_=s_: e.tensor_scalar(out=s_[:, 1:2], in0=s_[:, 0:1], scalar1=-1.0 / GW_, scalar2=None,
                                                         op0=ALU.mult), [s_], [s_])
            P.op("act", lambda e, v=v, s_=s_: e.activation(out=v[:, :], in_=v[:, :], func=AF.Identity, bias=s_[:, 1:2],
                                                           scale=1.0), [v, s_], [v])
            P.op("act", lambda e, v=v, s_=s_: e.activation(out=sq[:, :], in_=v[:, :], func=AF.Square,
                                                           accum_out=s_[:, 2:3]), [v], [sq, s_])
            P.op("act", lambda e, s_=s_: e.activation(out=s_[:, 3:4], in_=s_[:, 2:3], func=AF.Sqrt, bias=self.eps_ap(),
                                                      scale=1.0 / GW_), [s_, self.csb], [s_])
            P.op("dve", lambda e, s_=s_: e.reciprocal(out=s_[:, 3:4], in_=s_[:, 3:4]), [s_], [s_])
            P.op("dve", lambda e, v=v, vb=vb, s_=s_: e.scalar_tensor_tensor(
                out=vb[:, :], in0=v[:, :], scalar=s_[:, 3:4], in1=nwb[:, :], op0=ALU.mult, op1=ALU.mult),
                [v, s_, nwb], [vb])
            for g in range(G):
                p = ps[g % 4]
                tt = t32[g % 2]
                P.op("pe", lambda e, p=p, g=g, vb=vb: e.matmul(p[:, 0:128], lhsT=vb[:, g * 128:(g + 1) * 128],
                                                               rhs=sgu16[:, g * 128:(g + 1) * 128], start=True, stop=True),
                     [vb, sgu16], [p])
                P.op("dve", lambda e, p=p, g=g, tt=tt: e.tensor_tensor(out=tt[:, :], in0=p[:, 0:128],
                                                                       in1=bbc[:, g * 128:(g + 1) * 128], op=ALU.add),
                     [p, bbc], [tt])
                P.op("pool", lambda e, g=g, tt=tt, u=u, y=y: e.tensor_tensor(
                    out=y[:, g * 128:(g + 1) * 128], in0=tt[:, :], in1=u[:, g * 128:(g + 1) * 128], op=ALU.mult),
                    [tt, u], [y])
            P.dma("sp", self.yaT, self.yaT.t[:, t0:t0 + 128].rearrange("(g p) t -> p g t", p=128),
                  y, y[:, :].rearrange("p (g t) -> p g t", g=G))


Main.gmlp = _gmlp


def _merge(self, l, hsrc, hdst):
    P, c = self.P, self.cfg
    KC = c.KC
    ka, kb, kc_ = c.GMW // 128, c.DNW // 128, c.NAW // 128
    KY = ka + kb + kc_
    with P.scope() as S:
        a, sh, gt = self.mod_scalars(S, l, 1, False)
        TBm = c.TB // 2
        yin_b = S.sbuf([128, KY * TBm], BF16, "yin")
        YIN = _alias(yin_b, yin_b[:, :].rearrange("p (k t) -> p k t", k=KY))
        yy_b = S.sbuf([128, KC * TBm], BF16, "yy")
        YY = _alias(yy_b, yy_b[:, :].rearrange("p (k t) -> p k t", k=KC))
        w32 = [S.sbuf([128, KC * 128], F32, f"w32_{j}") for j in range(2)]
        w16 = [S.sbuf([128, KC * 128], BF16, f"w16_{j}") for j in range(2)]
        grow = [S.sbuf([128, 512], F32, f"grow{j}") for j in range(3)]
        accs = [S.sbuf([128, 512], F32, f"acc{j}") for j in range(3)]
        hrow = [S.sbuf([128, 512], F32, f"hrow{j}") for j in range(2)]
        o32 = [S.sbuf([128, 512], F32, f"o32{j}") for j in range(2)]
        psb = [S.psum([128, 512], F32, f"psb{j}") for j in range(8)]
        cnt = [0]
        for (t0, n) in [(t, min(TBm, c.T - t)) for t in range(0, c.T, TBm)]:
            P.dma("sp", YIN, YIN[:, 0:ka, 0:n], self.yaT, self.yaT.t[:, t0:t0 + n].rearrange("(k p) t -> p k t", p=128))
            P.dma("sp", YIN, YIN[:, ka:ka + kb, 0:n], self.ybT, self.ybT.t[:, t0:t0 + n].rearrange("(k p) t -> p k t", p=128))
            P.dma("sp", YIN, YIN[:, ka + kb:KY, 0:n], self.ycT, self.ycT.t[:, t0:t0 + n].rearrange("(k p) t -> p k t", p=128))
            tls = tiles_of(n, 512)
            for d in range(KC):
                for bi, (koff, kn, wsrc) in enumerate(((0, ka, self.pa[l]), (ka, kb, self.pb[l]), (ka + kb, kc_, self.pc[l]))):
                    k2 = cnt[0]
                    cnt[0] += 1
                    wb32, wb16 = w32[k2 % 2], w16[k2 % 2]
                    W = kn * 128
                    P.dma("sp", wb32, wb32[:, 0:W], wsrc, wsrc.t[d, :, 0:W])
                    P.op("act", lambda e, wb16=wb16, wb32=wb32, W=W: e.activation(out=wb16[:, 0:W], in_=wb32[:, 0:W],
                                                                                  func=AF.Copy), [wb32], [wb16])
                    for ti, (o, m) in enumerate(tls):
                        ps = psb[(k2 % 2) * 4 + ti]
                        for kk in range(kn):
                            P.op("pe", lambda e, ps=ps, kk=kk, o=o, m=m, wb16=wb16, koff=koff, kn=kn: e.matmul(
                                ps[:, 0:m], lhsT=wb16[:, kk * 128:(kk + 1) * 128], rhs=YIN[:, koff + kk, o:o + m],
                                start=(kk == 0), stop=(kk == kn - 1)), [wb16, YIN], [ps])
                        gr = grow[bi]
                        P.dma("sp", gr, gr[:, 0:m], self.z_g,
                              self.z_g.t[(bi * KC + d) * 128:(bi * KC + d + 1) * 128, t0 + o:t0 + o + m])
                        ac = accs[ti]
                        if bi == 0:
                            P.op("dve", lambda e, ps=ps, m=m, gr=gr, ac=ac: e.tensor_tensor(
                                out=ac[:, 0:m], in0=ps[:, 0:m], in1=gr[:, 0:m], op=ALU.mult), [ps, gr], [ac])
                        else:
                            P.op("dve", lambda e, ps=ps, m=m, gr=gr: e.tensor_tensor(
                                out=gr[:, 0:m], in0=ps[:, 0:m], in1=gr[:, 0:m], op=ALU.mult), [ps, gr], [gr])
                            if bi == 1:
                                P.op("pool", lambda e, m=m, gr=gr, ac=ac: e.tensor_tensor(
                                    out=ac[:, 0:m], in0=ac[:, 0:m], in1=gr[:, 0:m], op=ALU.add), [ac, gr], [ac])
                            else:
                                P.op("pool", lambda e, m=m, gr=gr, ac=ac, d=d, o=o: e.tensor_tensor(
                                    out=YY[:, d, o:o + m], in0=ac[:, 0:m], in1=gr[:, 0:m], op=ALU.add), [ac, gr], [YY])
            kcnt = [0]

            def epi(user, ti, o, m, ps, t0=t0, n=n):
                d = user
                j = kcnt[0] % 2
                kcnt[0] += 1
                hr, ob = hrow[j], o32[j]
                P.dma("sp", hr, hr[:, 0:m], hsrc, hsrc.t[d * 128:(d + 1) * 128, t0 + o:t0 + o + m])
                for (so, sn, grp) in c.segs(t0 + o, m):
                    P.op("dve", lambda e, ps=ps, so=so, sn=sn, grp=grp, d=d, hr=hr, ob=ob: e.scalar_tensor_tensor(
                        out=ob[:, so:so + sn], in0=ps[:, so:so + sn], scalar=gt[grp][:, d:d + 1],
                        in1=hr[:, so:so + sn], op0=ALU.mult, op1=ALU.add), [ps, gt[grp], hr], [ob])
                P.dma("sp", hdst, hdst.t[d * 128:(d + 1) * 128, t0 + o:t0 + o + m], ob, ob[:, 0:m])

            self.linear(S, YY, KC, self.wo[l], [(d, "fm", d) for d in range(KC)], n, w32, w16, psb, epi_fm=epi)


Main.merge = _merge


def _natt(self, l, last):
    P, c = self.P, self.cfg
    GW, SEQ, CTX, T, ROWS = c.GW, c.SEQ, c.CTX, c.T, c.ROWS
    nloc = 8 * GW
    NK = nloc + CTX
    NKC = NK // 128
    NTC = T // 128
    scale = 128.0 ** -0.5
    with P.scope() as S:
        qT = [S.sbuf([128, T], BF16, f"qT{j}") for j in range(2)]
        kT = [S.sbuf([128, T], BF16, f"kT{j}") for j in range(2)]
        VA = [S.sbuf([128, NTC * 128], BF16, f"VA{j}") for j in range(2)]
        VB = [S.sbuf([128, NTC * 128], BF16, f"VB{j}") for j in range(2)]
        bias = [S.sbuf([GW, 15 * GW], F32, f"bias{j}") for j in range(2)]
        yT = [S.sbuf([128, T], BF16, f"yT{j}") for j in range(2)]
        sc = [S.sbuf([128, NK], F32, f"sc{j}") for j in range(2)]
        pn = [S.sbuf([128, NK], BF16, f"pn{j}") for j in range(2)]
        st = [S.sbuf([128, 4], F32, f"st{j}") for j in range(2)]
        pT = [S.sbuf([128, NKC * 128], BF16, f"pT{j}") for j in range(2)]
        ps_s = [S.psum([128, 512], F32, f"pss{j}") for j in range(2)]
        ps_c = [S.psum([128, 512], F32, f"psc{j}") for j in range(2)]
        ps_t = [S.psum([128, 1024], BF16, f"pst{j}") for j in range(2)]
        ps_o = [S.psum([128, 512], F32, f"pso{j}") for j in range(2)]
        it = 0
        for h in range(c.NAH):
            hb = h % 2
            q, k, va, vb, bs, y = qT[hb], kT[hb], VA[hb], VB[hb], bias[hb], yT[hb]
            P.dma("sp", q, q[:, :], self.z_na, self.z_na.t[h * 128:(h + 1) * 128, :])
            P.dma("sp", k, k[:, :], self.z_na, self.z_na.t[(c.NAH + h) * 128:(c.NAH + h + 1) * 128, :])
            P.dma("sp", va, va[:, :].rearrange("p (c d) -> p c d", d=128), self.v_na,
                  self.v_na.t[:, h * 128:(h + 1) * 128].rearrange("(c p) d -> p c d", p=128))
            P.dma("sp", vb, vb[:, 0:(NTC - 1) * 128].rearrange("p (c d) -> p c d", d=128), self.v_na,
                  self.v_na.t[64:T - 64, h * 128:(h + 1) * 128].rearrange("(c p) d -> p c d", p=128))
            P.dma("sp", bs, bs[:, :], self.rpbm, self.rpbm.t[l * c.NAH + h, :, :])

            def softmax_pv(rows_n, nk, s_i, key_chunks, ydst, it):
                s_, p_, t_, pt = sc[s_i], pn[s_i], st[s_i], pT[s_i]
                P.op("dve", lambda e: e.tensor_reduce(out=t_[0:rows_n, 0:1], in_=s_[0:rows_n, 0:nk], axis=AX.X, op=ALU.max),
                     [s_], [t_])
                P.op("dve", lambda e: e.tensor_scalar(out=t_[0:rows_n, 1:2], in0=t_[0:rows_n, 0:1], scalar1=-1.0, scalar2=None,
                                                      op0=ALU.mult), [t_], [t_])
                P.op("act", lambda e: e.activation(out=s_[0:rows_n, 0:nk], in_=s_[0:rows_n, 0:nk], func=AF.Exp,
                                                   bias=t_[0:rows_n, 1:2], scale=1.0, accum_out=t_[0:rows_n, 2:3]),
                     [s_, t_], [s_, t_])
                P.op("dve", lambda e: e.reciprocal(out=t_[0:rows_n, 3:4], in_=t_[0:rows_n, 2:3]), [t_], [t_])
                P.op("dve", lambda e: e.tensor_scalar(out=p_[0:rows_n, 0:nk], in0=s_[0:rows_n, 0:nk], scalar1=t_[0:rows_n, 3:4],
                                                      scalar2=None, op0=ALU.mult), [s_, t_], [p_])
                nkc = nk // 128
                pst = ps_t[it % 2]
                for j in range(nkc):
                    P.op("pe", lambda e, j=j: e.transpose(pst[:, j * 128:j * 128 + rows_n], p_[0:rows_n, j * 128:(j + 1) * 128],
                                                          self.ident_bf[0:rows_n, 0:rows_n]), [p_, self.ident_bf], [pst])
                P.op("act", lambda e: e.activation(
                    out=pt[:, 0:nkc * 128].rearrange("p (j c) -> p j c", c=128)[:, :, 0:rows_n],
                    in_=pst[:, 0:nkc * 128].rearrange("p (j c) -> p j c", c=128)[:, :, 0:rows_n], func=AF.Copy), [pst], [pt])
                po = ps_o[it % 2]
                for j, vch in enumerate(key_chunks):
                    P.op("pe", lambda e, j=j, vch=vch: e.matmul(po[:, 0:rows_n], lhsT=vch(), rhs=pt[:, j * 128:j * 128 + rows_n],
                                                                start=(j == 0), stop=(j == nkc - 1)), [va, vb, pt], [po])
                P.op("act", lambda e: e.activation(out=ydst(), in_=po[:, 0:rows_n], func=AF.Copy), [po], [y])

            for r in range(ROWS):
                r0 = min(max(r - 4, 0), ROWS - 8)
                kst = r0 * GW
                s_i = it % 2
                pss, psc, s_ = ps_s[it % 2], ps_c[it % 2], sc[s_i]
                P.op("pe", lambda e, r=r, kst=kst, pss=pss: e.matmul(pss[0:GW, 0:nloc], lhsT=q[:, r * GW:(r + 1) * GW],
                                                                     rhs=k[:, kst:kst + nloc], start=True, stop=True), [q, k], [pss])
                P.op("pe", lambda e, r=r, psc=psc: e.matmul(psc[0:GW, 0:CTX], lhsT=q[:, r * GW:(r + 1) * GW],
                                                            rhs=k[:, SEQ:T], start=True, stop=True), [q, k], [psc])
                bo = (r0 - r + 7) * GW
                P.op("dve", lambda e, pss=pss, s_=s_, bo=bo: e.scalar_tensor_tensor(
                    out=s_[0:GW, 0:nloc], in0=pss[0:GW, 0:nloc], scalar=scale, in1=bs[:, bo:bo + nloc],
                    op0=ALU.mult, op1=ALU.add), [pss, bs], [s_])
                P.op("act", lambda e, psc=psc, s_=s_: e.activation(out=s_[0:GW, nloc:NK], in_=psc[0:GW, 0:CTX],
                                                                   func=AF.Copy, scale=scale), [psc], [s_])
                kch = []
                for j in range(nloc // 128):
                    tk = kst + j * 128
                    if tk % 128 == 0:
                        kch.append(lambda tk=tk: va[:, tk:tk + 128])
                    else:
                        kch.append(lambda tk=tk: vb[:, tk - 64:tk + 64])
                for j in range(CTX // 128):
                    kch.append(lambda j=j: va[:, SEQ + j * 128:SEQ + (j + 1) * 128])
                softmax_pv(GW, NK, s_i, kch, lambda r=r: y[:, r * GW:(r + 1) * GW], it)
                it += 1
            if not last:
                for qt in range(CTX // 128):
                    s_i = it % 2
                    psc, s_ = ps_c[it % 2], sc[s_i]
                    P.op("pe", lambda e, qt=qt, psc=psc: e.matmul(psc[:, 0:CTX], lhsT=q[:, SEQ + qt * 128:SEQ + (qt + 1) * 128],
                                                                  rhs=k[:, SEQ:T], start=True, stop=True), [q, k], [psc])
                    P.op("act", lambda e, psc=psc, s_=s_: e.activation(out=s_[:, 0:CTX], in_=psc[:, 0:CTX], func=AF.Copy,
                                                                       scale=scale), [psc], [s_])
                    kch = [(lambda j=j: va[:, SEQ + j * 128:SEQ + (j + 1) * 128]) for j in range(CTX // 128)]
                    softmax_pv(128, CTX, s_i, kch, lambda qt=qt: y[:, SEQ + qt * 128:SEQ + (qt + 1) * 128], it)
                    it += 1
            else:
                P.op("pool", lambda e: e.memset(y[:, SEQ:T], 0.0), [], [y])
            P.dma("sp", self.ycT, self.ycT.t[h * 128:(h + 1) * 128, :], y, y[:, :])


Main.natt = _natt


def _final_norm(self, hsrc):
    P, c = self.P, self.cfg
    KC, D = c.KC, c.D
    with P.scope() as S:
        hst = [S.sbuf([128, KC * 128], F32, f"hst{j}") for j in range(2)]
        sqt = S.sbuf([128, KC * 128], BF16, "sqt")
        rstd_b = S.sbuf([128, 128], F32, "rstd")
        ob = [S.sbuf([128, KC * 128], F32, f"ob{j}") for j in range(2)]
        pss = [S.psum([128, 512], F32, f"fps{j}") for j in range(2)]
        for si, (o, m) in enumerate(tiles_of(c.SEQ, 128)):
            hs, ot, ps = hst[si % 2], ob[si % 2], pss[si % 2]
            P.dma("sp", hs, hs[:, 0:KC * m].rearrange("p (k t) -> p k t", k=KC), hsrc,
                  hsrc.t[:, o:o + m].rearrange("(k p) t -> p k t", p=128))
            P.op("pool", lambda e, hs=hs, m=m: e.tensor_tensor(out=sqt[:, 0:KC * m], in0=hs[:, 0:KC * m],
                                                               in1=hs[:, 0:KC * m], op=ALU.mult), [hs], [sqt])
            for kc in range(KC):
                P.op("pe", lambda e, ps=ps, kc=kc, m=m: e.matmul(ps[:, 0:m], lhsT=self.ones_bf[:, :],
                                                                 rhs=sqt[:, kc * m:(kc + 1) * m], start=(kc == 0),
                                                                 stop=(kc == KC - 1)), [self.ones_bf, sqt], [ps])
            P.op("act", lambda e, ps=ps, m=m: e.activation(out=rstd_b[:, 0:m], in_=ps[:, 0:m], func=AF.Sqrt,
                                                           bias=self.eps_ap(), scale=1.0 / D), [ps, self.csb], [rstd_b])
            P.op("dve", lambda e, m=m: e.reciprocal(out=rstd_b[:, 0:m], in_=rstd_b[:, 0:m]), [rstd_b], [rstd_b])
            for kc in range(KC):
                P.op("dve", lambda e, hs=hs, ot=ot, kc=kc, m=m: e.scalar_tensor_tensor(
                    out=ot[:, kc * m:(kc + 1) * m], in0=hs[:, kc * m:(kc + 1) * m], scalar=self.fnsb[:, kc:kc + 1],
                    in1=rstd_b[:, 0:m], op0=ALU.mult, op1=ALU.mult), [hs, self.fnsb, rstd_b], [ot])
            P.dma("sp", self.outT, self.outT.t[:, o:o + m].rearrange("(k p) t -> p k t", p=128),
                  ot, ot[:, 0:KC * m].rearrange("p (k t) -> p k t", k=KC), is_output=True)


Main.final_norm = _final_norm


def _dn_prep(self, l):
    P, c = self.P, self.cfg
    T, SEQ, H = c.T, c.SEQ, c.DNH
    NCK = c.NCK
    with P.scope() as S:
        cos = S.sbuf([128, SEQ], F32, "cos")
        sin = S.sbuf([128, SEQ], F32, "sin")
        P.dma("sp", cos, cos[:, :], self.ropec, self.ropec.t[:, :])
        P.dma("sp", sin, sin[:, :], self.ropes, self.ropes.t[:, :])
        cw = S.sbuf([128, 3 * H * 5], F32, "cw")
        P.dma("sp", cw, cw[:, :], self.convw, self.convw.t[:, l * 3 * H * 5:(l + 1) * 3 * H * 5])
        X = [S.sbuf([128, T], F32, f"X{j}") for j in range(2)]
        A = [S.sbuf([128, T], F32, f"A{j}") for j in range(2)]
        tm = [S.sbuf([64, NCK * 128], F32, f"tm{j}") for j in range(2)]
        t1 = [S.sbuf([128, 512], F32, f"t1{j}") for j in range(2)]
        t2 = [S.sbuf([128, 512], F32, f"t2{j}") for j in range(2)]
        psn = [S.psum([128, 512], F32, f"psn{j}") for j in range(2)]
        psr = [S.psum([128, 512], F32, f"psr{j}") for j in range(2)]
        pst = [S.psum([128, 512], F32, f"pst{j}") for j in range(2)]
        ones32 = self.csb[:, C_ONES:C_ONES + 128]
        perm = self.csb[:, C_PERM:C_PERM + 128]
        id32 = self.csb[:, C_ID:C_ID + 128]
        it = 0
        tcount = 0
        for h in range(H):
            for kind in range(3):
                ch = kind * H + h
                x, a = X[it % 2], A[it % 2]
                it += 1
                P.dma("sp", x, x[:, :], self.z_dn, self.z_dn.t[ch * 128:(ch + 1) * 128, :])
                wc = lambda j, ch=ch: cw[:, ch * 5 + j:ch * 5 + j + 1]
                for (r0, r1) in ((0, SEQ), (SEQ, T)):
                    P.op("act", lambda e: e.activation(out=a[:, r0:r1], in_=x[:, r0:r1], func=AF.Copy, scale=wc(2)),
                         [x, cw], [a])
                    for j, (d0, d1, s0, s1) in ((0, (r0 + 2, r1, r0, r1 - 2)), (1, (r0 + 1, r1, r0, r1 - 1)),
                                                (3, (r0, r1 - 1, r0 + 1, r1)), (4, (r0, r1 - 2, r0 + 2, r1))):
                        P.op("dve", lambda e: e.scalar_tensor_tensor(out=a[:, d0:d1], in0=x[:, s0:s1], scalar=wc(j),
                                                                     in1=a[:, d0:d1], op0=ALU.mult, op1=ALU.add),
                             [x, cw, a], [a])
                P.op("act", lambda e: e.activation(out=a[:, :], in_=a[:, :], func=AF.Silu), [a], [a])
                if kind != 1:
                    for (o, m) in tiles_of(T, 512):
                        k2 = tcount % 2
                        tcount += 1
                        ta, tb, pn_, pr_ = t1[k2], t2[k2], psn[k2], psr[k2]
                        P.op("pool", lambda e: e.tensor_tensor(out=ta[:, 0:m], in0=a[:, o:o + m], in1=a[:, o:o + m],
                                                               op=ALU.mult), [a], [ta])
                        P.op("pe", lambda e: e.matmul(pn_[:, 0:m], lhsT=ones32, rhs=ta[:, 0:m], start=True, stop=True),
                             [self.csb, ta], [pn_])
                        P.op("act", lambda e: e.activation(out=tb[:, 0:m], in_=pn_[:, 0:m], func=AF.Sqrt,
                                                           bias=self.eps_ap(), scale=1.0), [pn_, self.csb], [tb])
                        P.op("dve", lambda e: e.reciprocal(out=tb[:, 0:m], in_=tb[:, 0:m]), [tb], [tb])
                        P.op("dve", lambda e: e.tensor_tensor(out=a[:, o:o + m], in0=a[:, o:o + m], in1=tb[:, 0:m],
                                                              op=ALU.mult), [a, tb], [a])
                        if o < SEQ:
                            mm_ = min(m, SEQ - o)
                            P.op("pe", lambda e: e.matmul(pr_[:, 0:mm_], lhsT=perm, rhs=a[:, o:o + mm_], start=True, stop=True),
                                 [self.csb, a], [pr_])
                            P.op("dve", lambda e: e.tensor_tensor(out=ta[:, 0:mm_], in0=a[:, o:o + mm_], in1=cos[:, o:o + mm_],
                                                                  op=ALU.mult), [a, cos], [ta])
                            P.op("dve", lambda e: e.tensor_tensor(out=tb[:, 0:mm_], in0=pr_[:, 0:mm_], in1=sin[:, o:o + mm_],
                                                                  op=ALU.mult), [pr_, sin], [tb])
                            P.op("pool", lambda e: e.tensor_tensor(out=a[:, o:o + mm_], in0=ta[:, 0:mm_], in1=tb[:, 0:mm_],
                                                                   op=ALU.add), [ta, tb], [a])
                if kind == 0:
                    P.dma("sp", self.dn_kT, self.dn_kT.t[h * 128:(h + 1) * 128, :], a, a[:, :])
                if kind == 2:
                    P.dma("sp", self.dn_qT, self.dn_qT.t[h * 128:(h + 1) * 128, :], a, a[:, :])
                if kind in (0, 1):
                    tmb = tm[kind]
                    for g0 in range(0, NCK, 4):
                        gn = min(4, NCK - g0)
                        pt = pst[(g0 // 4) % 2]
                        for j in range(gn):
                            ck = g0 + j
                            P.op("pe", lambda e: e.transpose(pt[0:64, j * 128:(j + 1) * 128], a[:, ck * 64:(ck + 1) * 64], id32),
                                 [a, self.csb], [pt])
                        P.op("act", lambda e: e.activation(out=tmb[:, g0 * 128:(g0 + gn) * 128], in_=pt[0:64, 0:gn * 128],
                                                           func=AF.Copy), [pt], [tmb])
                    dst = self.dn_ktm if kind == 0 else self.dn_vtm
                    P.dma("sp", dst, dst.t[:, h * 128:(h + 1) * 128].rearrange("(c p) d -> p c d", p=64),
                          tmb, tmb[:, :].rearrange("p (c d) -> p c d", d=128))


Main.dn_prep = _dn_prep


def _dn_scan(self, l):
    P, c = self.P, self.cfg
    T, SEQ, H, NCK = c.T, c.SEQ, c.DNH, c.NCK
    H2 = 2 * H
    sq = 128.0 ** -0.5
    lns = float(np.log(sq))
    csb = self.csb
    ones64 = csb[0:64, C_ONES:C_ONES + 64]
    ones64w = csb[0:64, C_ONES:C_ONES + 128]
    neg64 = csb[0:64, C_NEG1:C_NEG1 + 64]
    id64 = csb[0:64, C_ID:C_ID + 64]
    CUM = [csb[0:64, C_CUMF:C_CUMF + 64], csb[0:64, C_CUMB:C_CUMB + 64]]
    MS = [csb[0:64, C_MSF:C_MSF + 64], csb[0:64, C_MSB:C_MSB + 64]]
    MIT = [csb[0:64, C_MITF:C_MITF + 64], csb[0:64, C_MITB:C_MITB + 64]]
    nlat = SEQ // 64
    lat = list(range(nlat))
    ctx = list(range(nlat, NCK))
    order = [ctx + lat, ctx[::-1] + lat[::-1]]
    with P.scope() as S:
        gall = S.sbuf([64, NCK * H2], F32, "gall")
        ball = S.sbuf([64, NCK * H2], F32, "ball")
        with P.scope() as S2:
            ab = S2.sbuf([64, NCK * 2 * H2], F32, "ab")
            ab3 = ab[:, :].rearrange("p (c n) -> p c n", n=2 * H2)
            P.dma("sp", ab, ab3, self.z_ab, self.z_ab.t[:, 0:2 * H2].rearrange("(c p) n -> p c n", p=64))
            dtb = S2.sbuf([64, NCK * H2], F32, "dtb")
            nal = S2.sbuf([64, NCK * H2], F32, "nal")
            P.dma("sp", dtb, dtb[:, :], self.dnc, self.dnc.t[:, (l * 2) * NCK * H2:(l * 2 + 1) * NCK * H2])
            P.dma("sp", nal, nal[:, :], self.dnc, self.dnc.t[:, (l * 2 + 1) * NCK * H2:(l * 2 + 2) * NCK * H2])
            g3 = gall[:, :].rearrange("p (c n) -> p c n", n=H2)
            b3 = ball[:, :].rearrange("p (c n) -> p c n", n=H2)
            d3 = dtb[:, :].rearrange("p (c n) -> p c n", n=H2)
            P.op("dve", lambda e: e.tensor_tensor(out=g3, in0=ab3[:, :, 0:H2], in1=d3, op=ALU.add), [ab, dtb], [gall])
            P.op("act", lambda e: e.activation(out=gall[:, :], in_=gall[:, :], func=AF.Exp), [gall], [gall])
            P.op("act", lambda e: e.activation(out=gall[:, :], in_=gall[:, :], func=AF.Ln, bias=self.one_ap(64), scale=1.0),
                 [gall, csb], [gall])
            P.op("act", lambda e: e.activation(out=nal[:, :], in_=nal[:, :], func=AF.Exp), [nal], [nal])
            P.op("dve", lambda e: e.scalar_tensor_tensor(out=gall[:, :], in0=gall[:, :], scalar=-1.0, in1=nal[:, :],
                                                         op0=ALU.mult, op1=ALU.mult), [gall, nal], [gall])
            P.op("act", lambda e: e.activation(out=b3, in_=ab3[:, :, H2:2 * H2], func=AF.Sigmoid), [ab], [ball])
        nwb = S.sbuf([64, 128], F32, "nwb")
        P.dma("sp", nwb, nwb[:, :], self.dnnw, self.dnnw.t[:, l * 128:(l + 1) * 128])
        kT = S.sbuf([128, T], F32, "kT")
        qT = S.sbuf([128, T], F32, "qT")
        ktm = S.sbuf([64, NCK * 128], F32, "ktm")
        vtm = S.sbuf([64, NCK * 128], F32, "vtm")
        O = S.sbuf([64, NCK * 128], F32, "O")
        ysb = S.sbuf([128, T], BF16, "ysb")
        ms = S.sbuf([64, NCK], F32, "ms")
        junk2 = [S.sbuf([64, 128], F32, f"junk{j}") for j in range(2)]
        Sst = [S.sbuf([128, 128], F32, f"S{d}") for d in range(2)]
        banks = [S.psum([128, 512], F32, f"bk{j}") for j in range(8)]

        def view(bank, rows, c0, c1, name):
            return _alias(banks[bank], banks[bank][0:rows, c0:c1])

        def mk(d, j):
            t = {}
            for nm, shp in (("gcum", [64, 64]), ("decS", [64, 64]), ("decIT", [64, 64]), ("N0", [64, 64]), ("qkT", [64, 64]),
                            ("sm", [128, 8]), ("Y0", [64, 256]), ("Y1", [64, 256]), ("kd", [64, 128]),
                            ("PA", [64, 64]), ("PB", [64, 64]), ("TA", [64, 64]), ("TB", [64, 64]), ("wT", [128, 64])):
                t[nm] = S.sbuf(shp, F32, f"{nm}{d}{j}")
            return t

        tmp = [[mk(d, j) for j in range(2)] for d in range(2)]
        vn = [S.sbuf([64, 128], F32, f"vn{d}") for d in range(2)]
        t3 = [S.sbuf([64, 128], F32, f"t3{d}") for d in range(2)]
        ot = [S.sbuf([64, 128], F32, f"ot{d}") for d in range(2)]
        pv = []
        for d in range(2):
            b0 = d * 3
            pv.append({
                "diff": view(b0, 64, 0, 64, f"diff{d}"), "KK": view(b0, 64, 64, 128, f"KK{d}"),
                "QK": view(b0, 64, 128, 192, f"QK{d}"), "tp0": view(b0, 64, 192, 256, f"tp0{d}"),
                "tp1": view(b0, 64, 256, 320, f"tp1{d}"), "wTp": view(b0, 128, 320, 384, f"wTp{d}"),
                "gc": view(b0, 64, 384, 386, f"gc{d}"), "gt": view(b0, 128, 386, 388, f"gt{d}"),
                "ap0": view(b0 + 1, 64, 0, 256, f"ap0{d}"), "ap1": view(b0 + 1, 64, 256, 512, f"ap1{d}"),
                "ps1": view(b0 + 2, 64, 0, 128, f"ps1{d}"), "ps2": view(b0 + 2, 64, 128, 256, f"ps2{d}"),
                "ps3": view(b0 + 2, 64, 256, 384, f"ps3{d}"), "ps4": view(b0 + 2, 128, 384, 512, f"ps4{d}"),
            })
        ytp = [banks[6], banks[7]]

        PRE_N = int(os.environ.get("PRE_N", "1000"))
        pcount = [0]

        def pop(*a, **kw):
            pcount[0] += 1
            if pcount[0] <= PRE_N:
                return P.op(*a, **kw)

        def pre(h, d, ck, t):
            pcount[0] = 0
            p = pv[d]
            gi = ck * H2 + d * H + h
            gcol = gall[:, gi:gi + 1]
            bcol = ball[:, gi:gi + 1]
            kc_ = kT[:, ck * 64:(ck + 1) * 64]
            qc_ = qT[:, ck * 64:(ck + 1) * 64]
            pop("dve", lambda e: e.tensor_scalar(out=t["gcum"][:, :], in0=CUM[d], scalar1=gcol, scalar2=None, op0=ALU.mult),
                 [csb, gall], [t["gcum"]])
            pop("pe", lambda e: e.matmul(p["diff"][:, :], lhsT=t["gcum"][:, :], rhs=ones64, start=True, stop=False),
                 [t["gcum"], csb], [p["diff"]])
            pop("pe", lambda e: e.matmul(p["diff"][:, :], lhsT=neg64, rhs=t["gcum"][:, :], start=False, stop=True),
                 [t["gcum"], csb], [p["diff"]])
            pop("pe", lambda e: e.matmul(p["gc"][:, 0:1], lhsT=CUM[d], rhs=gcol, start=True, stop=True), [csb, gall], [p["gc"]])
            pop("pe", lambda e: e.matmul(p["gt"][:, 0:1], lhsT=ones64w, rhs=gcol, start=True, stop=True), [csb, gall], [p["gt"]])
            pop("dve", lambda e: e.tensor_tensor(out=t["decS"][:, :], in0=p["diff"][:, :], in1=MS[d], op=ALU.add),
                 [p["diff"], csb], [t["decS"]])
            pop("act", lambda e: e.activation(out=t["decS"][:, :], in_=t["decS"][:, :], func=AF.Exp), [t["decS"]], [t["decS"]])
            pop("dve", lambda e: e.scalar_tensor_tensor(out=t["decIT"][:, :], in0=p["diff"][:, :], scalar=-1.0, in1=MIT[d],
                                                         op0=ALU.mult, op1=ALU.add), [p["diff"], csb], [t["decIT"]])
            pop("act", lambda e: e.activation(out=t["decIT"][:, :], in_=t["decIT"][:, :], func=AF.Exp), [t["decIT"]], [t["decIT"]])
            pop("pe", lambda e: e.matmul(p["KK"][:, :], lhsT=kc_, rhs=kc_, start=True, stop=True), [kT], [p["KK"]])
            pop("pe", lambda e: e.matmul(p["QK"][:, :], lhsT=kc_, rhs=qc_, start=True, stop=True), [kT, qT], [p["QK"]])
            pop("dve", lambda e: e.scalar_tensor_tensor(out=t["N0"][:, :], in0=p["KK"][:, :], scalar=bcol, in1=t["decS"][:, :],
                                                         op0=ALU.mult, op1=ALU.mult), [p["KK"], ball, t["decS"]], [t["N0"]])
            pop("dve", lambda e: e.tensor_tensor(out=t["qkT"][:, :], in0=p["QK"][:, :], in1=t["decIT"][:, :], op=ALU.mult),
                 [p["QK"], t["decIT"]], [t["qkT"]])
            sm = t["sm"]
            pop("act", lambda e: e.activation(out=sm[:, 0:1], in_=p["gt"][:, 0:1], func=AF.Copy), [p["gt"]], [sm])
            pop("act", lambda e: e.activation(out=sm[0:64, 5:6], in_=p["gc"][:, 0:1], func=AF.Copy), [p["gc"]], [sm])
            pop("act", lambda e: e.activation(out=sm[:, 1:2], in_=sm[:, 0:1], func=AF.Exp), [sm], [sm])
            pop("act", lambda e: e.activation(out=sm[0:64, 2:3], in_=sm[0:64, 5:6], func=AF.Exp), [sm], [sm])
            pop("act", lambda e: e.activation(out=sm[0:64, 3:4], in_=sm[0:64, 5:6], func=AF.Exp, bias=sm[0:64, 0:1],
                                              scale=-1.0), [sm], [sm])
            pop("act", lambda e: e.activation(out=sm[0:64, 4:5], in_=sm[0:64, 5:6], func=AF.Exp, bias=self.lns_ap(64),
                                              scale=1.0), [sm, csb], [sm])
            Y = t["Y0"]
            pop("dve", lambda e: e.tensor_scalar(out=Y[:, 0:128], in0=vtm[:, ck * 128:(ck + 1) * 128], scalar1=bcol,
                                                  scalar2=None, op0=ALU.mult), [vtm, ball], [Y])
            pop("dve", lambda e: e.tensor_scalar(out=Y[:, 128:256], in0=ktm[:, ck * 128:(ck + 1) * 128], scalar1=bcol,
                                                  scalar2=sm[0:64, 2:3], op0=ALU.mult, op1=ALU.mult), [ktm, ball, sm], [Y])
            pop("act", lambda e: e.activation(out=t["kd"][:, :], in_=ktm[:, ck * 128:(ck + 1) * 128], func=AF.Copy,
                                               scale=sm[0:64, 3:4]), [ktm, sm], [t["kd"]])
            Pc, PTc = t["N0"], t["TA"]
            pop("pe", lambda e: e.transpose(p["tp0"][:, :], t["N0"][:, :], id64), [t["N0"], csb], [p["tp0"]])
            pop("act", lambda e: e.activation(out=PTc[:, :], in_=p["tp0"][:, :], func=AF.Copy), [p["tp0"]], [PTc])
            Ys = [t["Y0"], t["Y1"]]
            yi = 0
            pop("pe", lambda e: e.matmul(p["ap0"][:, :], lhsT=PTc[:, :], rhs=Ys[0][:, :], start=True, stop=True),
                 [PTc, Ys[0]], [p["ap0"]])
            pop("dve", lambda e: e.tensor_tensor(out=Ys[1][:, :], in0=Ys[0][:, :], in1=p["ap0"][:, :], op=ALU.subtract),
                 [Ys[0], p["ap0"]], [Ys[1]])
            yi = 1
            pbufs = [(t["PA"], t["TB"]), (t["PB"], t["TA"])]
            for k in range(5):
                Pn, PTn = pbufs[k % 2]
                if k % 2 == 1:
                    PTn = t["TA"]
                pop("pe", lambda e: e.matmul(p["tp0"][:, :], lhsT=PTc[:, :], rhs=Pc[:, :], start=True, stop=True),
                     [PTc, Pc], [p["tp0"]])
                pop("pe", lambda e: e.matmul(p["tp1"][:, :], lhsT=Pc[:, :], rhs=PTc[:, :], start=True, stop=True),
                     [PTc, Pc], [p["tp1"]])
                Pn = t["PA"] if Pc is not t["PA"] else t["PB"]
                PTn = t["TB"] if PTc is not t["TB"] else t["TA"]
                pop("act", lambda e: e.activation(out=Pn[:, :], in_=p["tp0"][:, :], func=AF.Copy), [p["tp0"]], [Pn])
                pop("act", lambda e: e.activation(out=PTn[:, :], in_=p["tp1"][:, :], func=AF.Copy), [p["tp1"]], [PTn])
                Pc, PTc = Pn, PTn
                apx = p["ap1"] if k % 2 == 0 else p["ap0"]
                Yc, Yn = Ys[yi], Ys[1 - yi]
                pop("pe", lambda e: e.matmul(apx[:, :], lhsT=PTc[:, :], rhs=Yc[:, :], start=True, stop=True), [PTc, Yc], [apx])
                pop("dve", lambda e: e.tensor_tensor(out=Yn[:, :], in0=Yc[:, :], in1=apx[:, :], op=ALU.add), [Yc, apx], [Yn])
                yi = 1 - yi
            Yf = Ys[yi]
            pop("pe", lambda e: e.transpose(p["wTp"][:, :], Yf[:, 128:256], id64), [Yf, csb], [p["wTp"]])
            pop("act", lambda e: e.activation(out=t["wT"][:, :], in_=p["wTp"][:, :], func=AF.Copy), [p["wTp"]], [t["wT"]])
            return Yf

        SEQ_N = int(os.environ.get("SEQ_N", "1000"))
        scount = [0]

        def sop(*a, **kw):
            scount[0] += 1
            if scount[0] <= SEQ_N:
                return P.op(*a, **kw)

        def seq(h, d, ck, t, Yf):
            scount[0] = 0
            p = pv[d]
            Sd = Sst[d]
            sm = t["sm"]
            qc_ = qT[:, ck * 64:(ck + 1) * 64]
            sop("pe", lambda e: e.matmul(p["ps1"][:, :], lhsT=t["wT"][:, :], rhs=Sd[:, :], start=True, stop=True),
                 [t["wT"], Sd], [p["ps1"]])
            sop("dve", lambda e: e.tensor_tensor(out=vn[d][:, :], in0=Yf[:, 0:128], in1=p["ps1"][:, :], op=ALU.subtract),
                 [Yf, p["ps1"]], [vn[d]])
            sop("pe", lambda e: e.matmul(p["ps2"][:, :], lhsT=qc_, rhs=Sd[:, :], start=True, stop=True), [qT, Sd], [p["ps2"]])
            sop("pe", lambda e: e.matmul(p["ps3"][:, :], lhsT=t["qkT"][:, :], rhs=vn[d][:, :], start=True, stop=True),
                 [t["qkT"], vn[d]], [p["ps3"]])
            sop("pe", lambda e: e.matmul(p["ps4"][:, :], lhsT=t["kd"][:, :], rhs=vn[d][:, :], start=True, stop=True),
                 [t["kd"], vn[d]], [p["ps4"]])
            sop("dve", lambda e: e.scalar_tensor_tensor(out=Sd[:, :], in0=Sd[:, :], scalar=sm[:, 1:2], in1=p["ps4"][:, :],
                                                         op0=ALU.mult, op1=ALU.add), [Sd, sm, p["ps4"]], [Sd])
            sop("dve", lambda e: e.tensor_scalar(out=t3[d][:, :], in0=p["ps3"][:, :], scalar1=sq, scalar2=None, op0=ALU.mult),
                [p["ps3"]], [t3[d]])
            sop("dve", lambda e: e.scalar_tensor_tensor(out=ot[d][:, :], in0=p["ps2"][:, :], scalar=sm[0:64, 4:5], in1=t3[d][:, :],
                                                         op0=ALU.mult, op1=ALU.add), [p["ps2"], sm, t3[d]], [ot[d]])
            sop("pool", lambda e: e.tensor_tensor(out=O[:, ck * 128:(ck + 1) * 128], in0=O[:, ck * 128:(ck + 1) * 128],
                                                   in1=ot[d][:, :], op=ALU.add), [O, ot[d]], [O])

        dz = [S.sbuf([64, 128], F32, f"dz{j}") for j in range(2)]
        yb = [S.sbuf([64, 128], F32, f"yb{j}") for j in range(2)]
        DBG = float(os.environ.get("DN_DBG", "9"))
        for h in range(H if DBG > 1 else 0):
            P.dma("sp", kT, kT[:, :], self.dn_kT, self.dn_kT.t[h * 128:(h + 1) * 128, :])
            P.dma("sp", qT, qT[:, :], self.dn_qT, self.dn_qT.t[h * 128:(h + 1) * 128, :])
            P.dma("sp", ktm, ktm[:, :].rearrange("p (c d) -> p c d", d=128), self.dn_ktm,
                  self.dn_ktm.t[:, h * 128:(h + 1) * 128].rearrange("(c p) d -> p c d", p=64))
            P.dma("sp", vtm, vtm[:, :].rearrange("p (c d) -> p c d", d=128), self.dn_vtm,
                  self.dn_vtm.t[:, h * 128:(h + 1) * 128].rearrange("(c p) d -> p c d", p=64))
            P.op("pool", lambda e: e.memset(O[:, :], 0.0), [], [O])
            for d in range(2):
                P.op("pool", lambda e: e.memset(Sst[d][:, :], 0.0), [], [Sst[d]])
            Yfs = [[None, None], [None, None]]
            for d in range(2 if DBG > 1.7 else 0):
                Yfs[d][0] = pre(h, d, order[d][0], tmp[d][0])
            for step in range(min(NCK, int(os.environ.get('NSTEP', '1000'))) if DBG > 2 else 0):
                j = step % 2
                if step + 1 < NCK:
                    for d in range(2):
                        Yfs[d][1 - j] = pre(h, d, order[d][step + 1], tmp[d][1 - j])
                for d in range(2):
                    seq(h, d, order[d][step], tmp[d][j], Yfs[d][j])
            if self.dbg:
                P.dma("sp", self.dbgO, self.dbgO.t[:, :], O, O[:, :])
                P.dma("sp", self.dbgS, self.dbgS.t[:, 0:128], Sst[0], Sst[0][:, :])
                P.dma("sp", self.dbgS, self.dbgS.t[:, 128:256], Sst[1], Sst[1][:, :])
            if DBG < 4:
                continue
            for ck in range(NCK):
                jk = junk2[ck % 2]
                P.op("act", lambda e: e.activation(out=jk[:, :], in_=O[:, ck * 128:(ck + 1) * 128], func=AF.Square), [O], [jk])
                P.op("dve", lambda e: e.tensor_reduce(out=ms[:, ck:ck + 1], in_=jk[:, :], axis=AX.X, op=ALU.add), [jk], [ms])
            P.op("act", lambda e: e.activation(out=ms[:, :], in_=ms[:, :], func=AF.Sqrt, bias=csb[0:64, C_EPS:C_EPS + 1],
                                               scale=1.0 / 128), [ms, csb], [ms])
            P.op("dve", lambda e: e.reciprocal(out=ms[:, :], in_=ms[:, :]), [ms], [ms])
            for g0 in range(0, NCK if DBG > 5 else 0, 8):
                gn = min(8, NCK - g0)
                pt = ytp[(g0 // 8) % 2]
                for jj in range(gn):
                    ck = g0 + jj
                    dzb, ybb = dz[ck % 2], yb[ck % 2]
                    P.dma("sp", dzb, dzb[:, :], self.z_dg, self.z_dg.t[ck * 64:(ck + 1) * 64, h * 128:(h + 1) * 128])
                    P.op("dve", lambda e: e.scalar_tensor_tensor(out=ybb[:, :], in0=O[:, ck * 128:(ck + 1) * 128],
                                                                 scalar=ms[:, ck:ck + 1], in1=nwb[:, :], op0=ALU.mult,
                                                                 op1=ALU.mult), [O, ms, nwb], [ybb])
                    P.op("pool", lambda e: e.tensor_tensor(out=ybb[:, :], in0=ybb[:, :], in1=dzb[:, :], op=ALU.mult),
                         [ybb, dzb], [ybb])
                    if DBG > 6:
                        P.op("pe", lambda e: e.transpose(pt[:, jj * 64:(jj + 1) * 64], ybb[:, :], id64), [ybb, csb], [pt])
                if DBG > 7:
                    P.op("act", lambda e: e.activation(out=ysb[:, g0 * 64:(g0 + gn) * 64], in_=pt[:, 0:gn * 64], func=AF.Copy),
                         [pt], [ysb])
            if DBG > 7:
                P.dma("sp", self.ybT, self.ybT.t[h * 128:(h + 1) * 128, :], ysb, ysb[:, :])


Main.dn_scan = _dn_scan


class _V:
    def __init__(self, buf, rows, c0, c1):
        self.ap = buf[0:rows, c0:c1]

    def __getitem__(self, idx):
        return self.ap[idx]


C_ONE1 = C_ONES


def _one_ap(self, rows):
    return self.csb[0:rows, C_ONES:C_ONES + 1]


def _lns_ap(self, rows):
    return self.csb[0:rows, C_EPS + 1:C_EPS + 2]


Main.one_ap = _one_ap
Main.lns_ap = _lns_ap


def _mod_phase(self):
    P, c = self.P, self.cfg
    KC = c.KC
    NQ = 9 * KC
    with P.scope() as S:
        craw = S.sbuf([128, KC * 2], F32, "craw")
        cs = S.sbuf([128, KC * 2], F32, "cs")
        P.dma("sp", craw, craw[:, :], self.cT, self.cT.t[:, :])
        P.op("act", lambda e: e.activation(out=cs[:, :], in_=craw[:, :], func=AF.Silu), [craw], [cs])
        bsb = S.sbuf([128, c.L * NQ], F32, "adab")
        P.dma("sp", bsb, bsb[:, :], self.adabT, self.adabT.t[:, :])
        wt = [S.sbuf([128, KC * 128], F32, f"aw{j}") for j in range(3)]
        ps = [S.psum([128, 512], F32, f"mps{j}") for j in range(4)]
        it = 0
        for l in range(self.nl):
            mv = self.modsb[:, l * 2 * NQ:(l + 1) * 2 * NQ].rearrange("p (g q) -> p g q", g=2)
            for q in range(NQ):
                w = wt[it % 3]
                p_ = ps[it % 4]
                it += 1
                P.dma("sp", w, w[:, :], self.adaw[l], self.adaw[l].t[q, :, :])
                for kc in range(KC):
                    P.op("pe", lambda e: e.matmul(p_[:, 0:2], lhsT=w[:, kc * 128:(kc + 1) * 128], rhs=cs[:, kc * 2:(kc + 1) * 2],
                                                  start=(kc == 0), stop=(kc == KC - 1)), [w, cs], [p_])
                P.op("dve", lambda e: e.tensor_scalar(out=mv[:, :, q], in0=p_[:, 0:2], scalar1=bsb[:, l * NQ + q:l * NQ + q + 1],
                                                      scalar2=None, op0=ALU.add), [p_, bsb], [self.modsb])


Main.mod_phase = _mod_phase


def _build(self):
    P, c = self.P, self.cfg
    L, KC, FC, T = c.L, c.KC, c.FC, c.T
    with self.stack:
        self.xT = self.din("xT", [c.D, T])
        self.cT = self.din("cT", [128, KC * 2])
        self.adaw = [self.din(f"adaw_{l}", [9 * KC, 128, KC * 128]) for l in range(L)]
        self.adabT = self.din("adabT", [128, L * 9 * KC])
        self.normT = self.din("normT", [128, L * 3 * KC])
        self.fnT = self.din("fnT", [128, KC])
        self.w13 = [[self.din(f"w13_{l}_{i}", [2 * FC, 128, KC * 128]) for i in range(2)] for l in range(L)]
        self.w2 = [[self.din(f"w2_{l}_{i}", [KC, 128, FC * 128]) for i in range(2)] for l in range(L)]
        self.win = [self.din(f"win_{l}", [c.NCH, 128, KC * 128]) for l in range(L)]
        self.pa = [self.din(f"pa_{l}", [KC, 128, c.GMW]) for l in range(L)]
        self.pb = [self.din(f"pb_{l}", [KC, 128, c.DNW]) for l in range(L)]
        self.pc = [self.din(f"pc_{l}", [KC, 128, c.NAW]) for l in range(L)]
        self.wo = [self.din(f"wo_{l}", [KC, 128, KC * 128]) for l in range(L)]
        self.sguT = self.din("sguT", [128, L * c.GMG * 128])
        self.sgub = self.din("sgub", [128, L * c.GMG * 128])
        self.sgunw = self.din("sgunw", [128, L * c.GMW])
        self.rpbm = self.din("rpbm", [L * c.NAH, c.GW, 15 * c.GW])
        self.ropec = self.din("ropec", [128, c.SEQ])
        self.ropes = self.din("ropes", [128, c.SEQ])
        self.convw = self.din("convw", [128, L * 3 * c.DNH * 5])
        self.dnc = self.din("dnc", [64, L * 2 * c.NCK * 2 * c.DNH])
        self.dnnw = self.din("dnnw", [64, L * 128])
        self.hs = [self.dscr(f"h{j}", [c.D, T]) for j in range(3 * L)]
        self.gT = self.dscr("gT", [c.FF, T], BF16)
        self.z_dn = self.dscr("z_dn", [3 * c.DNW, T])
        self.z_na = self.dscr("z_na", [2 * c.NAW, T], BF16)
        self.z_gu = self.dscr("z_gu", [c.GMW, T])
        self.z_g = self.dscr("z_g", [3 * c.D, T])
        self.v_na = self.dscr("v_na", [T, c.NAW], BF16)
        self.z_dg = self.dscr("z_dg", [T, c.DNW])
        self.z_gv = self.dscr("z_gv", [T, c.GMW])
        self.z_ab = self.dscr("z_ab", [T, 128])
        self.yaT = self.dscr("yaT", [c.GMW, T], BF16)
        self.ybT = self.dscr("ybT", [c.DNW, T], BF16)
        self.ycT = self.dscr("ycT", [c.NAW, T], BF16)
        self.dn_kT = self.dscr("dn_kT", [c.DNW, T])
        self.dn_qT = self.dscr("dn_qT", [c.DNW, T])
        self.dn_ktm = self.dscr("dn_ktm", [T, c.DNW])
        self.dn_vtm = self.dscr("dn_vtm", [T, c.DNW])
        if self.dbg:
            self.dbgO = self.dscr("dbgO", [64, c.NCK * 128])
            self.dbgS = self.dscr("dbgS", [128, 256])
        ap = self.nc.dram_tensor("outT", [c.D, c.SEQ], F32, kind="ExternalOutput").ap()
        self.outT = Buf(P, ap, "outT")
        self.load_consts()
        self.modsb = P.sbuf([128, L * 2 * 9 * KC], F32, "modsb")
        self.mod_phase()
        self.normsb = P.sbuf([128, L * 3 * KC], F32, "normsb")
        P.dma("sp", self.normsb, self.normsb[:, :], self.normT, self.normT.t[:, :])
        self.fnsb = P.sbuf([128, KC], F32, "fnsb")
        P.dma("sp", self.fnsb, self.fnsb[:, :], self.fnT, self.fnT.t[:, :])
        h = self.xT
        stop = self.stop_after
        done = False
        for l in range(self.nl):
            last = (l == c.L - 1)
            self.ffn(l, 0, h, self.hs[3 * l]); h = self.hs[3 * l]
            if stop == (l, "ffn0"): break
            self.inproj(l, h)
            if stop == (l, "inproj"): break
            self.gmlp(l)
            if stop == (l, "gmlp"): break
            self.natt(l, last)
            if stop == (l, "natt"): break
            self.dn_prep(l)
            if stop == (l, "dnprep"): break
            self.dn_scan(l)
            if stop == (l, "dnscan"): break
            self.merge(l, h, self.hs[3 * l + 1]); h = self.hs[3 * l + 1]
            if stop == (l, "merge"): break
            self.ffn(l, 1, h, self.hs[3 * l + 2]); h = self.hs[3 * l + 2]
        self.final_norm(h)
        P.finish("sp")
        P.emit()
    return self.nc


Main.build = _build


def tile_w(W, ncols_chunk=128):
    K, N = W.shape
    KCn, G = K // 128, N // 128
    return np.ascontiguousarray(W.reshape(KCn, 128, G, 128).transpose(2, 1, 0, 3)).reshape(G, 128, KCn * 128)


def vecT(v):
    sh = v.shape
    KCn = sh[-1] // 128
    x = v.reshape(*sh[:-1], KCn, 128)
    x = np.moveaxis(x, -1, 0)
    return np.ascontiguousarray(x).reshape(128, -1)


def rope_tables(c):
    nf = 32
    freqs = (10000.0 ** (-np.arange(nf, dtype=np.float32) / nf)).astype(np.float32)
    t = np.arange(c.SEQ)
    rows = (t // c.GW).astype(np.float32)
    cols = (t % c.GW).astype(np.float32)
    cosT = np.zeros((128, c.SEQ), np.float32)
    sinT = np.zeros((128, c.SEQ), np.float32)
    for p in range(128):
        pos = rows if p < 64 else cols
        ang = pos * freqs[p % 32]
        cosT[p] = np.cos(ang)
        sinT[p] = np.sin(ang) * (-1.0 if (p % 64) < 32 else 1.0)
    return cosT, sinT


def in_col_order(c):
    DNW, NAW, GMW, D, H = c.DNW, c.NAW, c.GMW, c.D, c.DNH
    sizes = [2 * DNW, 2 * H, 2 * H, NAW, NAW, DNW, NAW, DNW, GMW, GMW, D, D, D]
    offs = np.concatenate([[0], np.cumsum(sizes)])
    o = {n: offs[i] for i, n in enumerate(["kv", "al", "be", "nak", "nav", "dnq", "naq", "dng", "gmu", "gmv", "ga", "gb", "gc"])}
    r = lambda a, n: list(range(a, a + n))
    cols = (r(o["kv"], DNW) + r(o["kv"] + DNW, DNW) + r(o["dnq"], DNW) + r(o["naq"], NAW) + r(o["nak"], NAW)
            + r(o["gmu"], GMW) + r(o["ga"], D) + r(o["gb"], D) + r(o["gc"], D)
            + r(o["nav"], NAW) + r(o["dng"], DNW) + r(o["gmv"], GMW) + r(o["al"], 2 * H) + r(o["be"], 2 * H))
    return np.array(cols), int(offs[-1])


def prep_core(c, b, inp, adaw_tiled):
    L = c.L
    f32 = np.float32
    m = {}
    m["consts"] = make_consts()
    m["xT"] = np.ascontiguousarray(np.concatenate([inp["x"][b], inp["ctx"][b]], axis=0).T)
    cc = np.stack([inp["c"][b], inp["c_ctx"]], axis=0)
    m["cT"] = np.ascontiguousarray(cc.reshape(2, c.KC, 128).transpose(2, 1, 0)).reshape(128, c.KC * 2)
    for l in range(L):
        m[f"adaw_{l}"] = adaw_tiled[l]
    m["adabT"] = vecT(inp["ada_b"])
    m["normT"] = vecT(inp["norm_w"])
    m["fnT"] = vecT(inp["final_norm_w"])
    cols, inw = in_col_order(c)
    for l in range(L):
        for i in range(2):
            m[f"w13_{l}_{i}"] = np.concatenate([tile_w(inp["ffn_w1"][l, i]), tile_w(inp["ffn_w3"][l, i])], axis=0)
            m[f"w2_{l}_{i}"] = tile_w(inp["ffn_w2"][l, i])
        W = inp["w_in"][l][:, cols]
        pad = c.NCH * 128 - W.shape[1]
        W = np.concatenate([W, np.zeros((c.D, pad), f32)], axis=1)
        m[f"win_{l}"] = tile_w(W)
        m[f"pa_{l}"] = tile_w(inp["proj_a"][l])
        m[f"pb_{l}"] = tile_w(inp["proj_b"][l])
        m[f"pc_{l}"] = tile_w(inp["proj_c"][l])
        m[f"wo_{l}"] = tile_w(inp["w_out"][l])
    G = c.GMG
    m["sguT"] = np.ascontiguousarray(inp["sgu_w"].transpose(3, 0, 1, 2)).reshape(128, L * G * 128)
    m["sgub"] = np.ascontiguousarray(np.broadcast_to(inp["sgu_b"].reshape(1, L * G * 128), (128, L * G * 128)))
    m["sgunw"] = np.ascontiguousarray(np.broadcast_to(inp["sgu_norm_w"].reshape(1, L * c.GMW), (128, L * c.GMW)))
    GW = c.GW
    qc = np.arange(GW)
    col_start = np.clip(qc - 8, 0, GW - 16)
    col_ok = (qc[None, :] >= col_start[:, None]) & (qc[None, :] < col_start[:, None] + 16)
    dc = np.clip(qc[None, :] - qc[:, None] + 15, 0, 30)
    rp = inp["na_rpb"]
    g = rp[:, :, :, dc]
    g = np.where(col_ok[None, None, None], g, f32(NEGBIG)).astype(f32)
    m["rpbm"] = np.ascontiguousarray(g.transpose(0, 1, 3, 2, 4)).reshape(L * c.NAH, GW, 15 * GW)
    cosT, sinT = rope_tables(c)
    m["ropec"], m["ropes"] = cosT, sinT
    H = c.DNH
    cw = inp["conv_w"]
    m["convw"] = np.ascontiguousarray(cw.reshape(L, 5, 3 * H, 128).transpose(3, 0, 2, 1)).reshape(128, L * 3 * H * 5)
    dt = inp["dn_dt_bias"].reshape(L, 1, 2 * H)
    al = inp["dn_a_log"].reshape(L, 1, 2 * H)
    dnc = np.stack([np.broadcast_to(dt, (L, c.NCK, 2 * H)), np.broadcast_to(al, (L, c.NCK, 2 * H))], axis=1)
    m["dnc"] = np.ascontiguousarray(np.broadcast_to(dnc.reshape(1, -1), (64, dnc.size))).astype(f32)
    m["dnnw"] = np.ascontiguousarray(np.broadcast_to(inp["dn_norm_w"].reshape(1, L * 128), (64, L * 128))).astype(f32)
    return {k: np.ascontiguousarray(v, dtype=f32) for k, v in m.items()}


def kernel(x, c, ctx, c_ctx, ada_w, ada_b, norm_w, ffn_w1, ffn_w3, ffn_w2, w_in, conv_w, dn_a_log,
           dn_dt_bias, dn_norm_w, sgu_w, sgu_b, sgu_norm_w, na_rpb, proj_a, proj_b, proj_c, w_out,
           final_norm_w):
    inp = dict(x=x, c=c, ctx=ctx, c_ctx=c_ctx, ada_w=ada_w, ada_b=ada_b, norm_w=norm_w, ffn_w1=ffn_w1,
               ffn_w3=ffn_w3, ffn_w2=ffn_w2, w_in=w_in, conv_w=conv_w, dn_a_log=dn_a_log, dn_dt_bias=dn_dt_bias,
               dn_norm_w=dn_norm_w, sgu_w=sgu_w, sgu_b=sgu_b, sgu_norm_w=sgu_norm_w, na_rpb=na_rpb,
               proj_a=proj_a, proj_b=proj_b, proj_c=proj_c, w_out=w_out, final_norm_w=final_norm_w)
    inp = {k: np.asarray(v, dtype=np.float32) for k, v in inp.items()}
    cfg = FULL
    B = inp["x"].shape[0]
    M = Main(cfg)
    nc = M.build()
    adaw_tiled = [tile_w(inp["ada_w"][l]) for l in range(cfg.L)]
    in_maps = [prep_core(cfg, b, inp, adaw_tiled) for b in range(B)]
    res = run_bass_kernel_spmd(nc, in_maps, core_ids=list(range(B)))
    out = np.stack([np.ascontiguousarray(res.results[b]["outT"].T) for b in range(B)], axis=0)
    return out.astype(np.float32)
```

```python
import os
import numpy as np
from contextlib import ExitStack
import concourse.bass as bass
import concourse.mybir as mybir
from concourse.bass_utils import run_bass_kernel_spmd

F32 = mybir.dt.float32
BF16 = mybir.dt.bfloat16
AF = mybir.ActivationFunctionType
ALU = mybir.AluOpType
AX = mybir.AxisListType

NCORES = 8


class Buf:
    def __init__(self, prog, t, name):
        self.prog = prog
        self.t = t
        self.name = name
        self.w = None
        self.r = []
        self.dsem = None
        self.excl = False

    def __getitem__(self, idx):
        return self.t[idx]


class _Rec:
    def __init__(self):
        self.calls = []

    def __getattr__(self, name):
        def f(*a, **kw):
            self.calls.append((name, a, kw))
            return None
        return f


class Prog:
    ENG = ("pe", "act", "dve", "pool", "sp")

    def __init__(self, nc, stack):
        self.nc = nc
        self.stack = stack
        self.q = {e: [] for e in self.ENG}
        self.sems = {}
        self.val = {}
        self.seen = {e: {} for e in self.ENG}
        for e in self.ENG:
            self._newsem("E_" + e)
        self.nbuf = 0
        self.out_deps = []

    def _newsem(self, key):
        s = self.stack.enter_context(self.nc.semaphore(key))
        self.sems[key] = s
        self.val[key] = 0
        return key

    def sbuf(self, shape, dtype, name=None):
        self.nbuf += 1
        name = name or f"sb{self.nbuf}"
        t = self.stack.enter_context(self.nc.sbuf_tensor(name, list(shape), dtype))
        return Buf(self, t, name)

    def psum(self, shape, dtype=F32, name=None):
        self.nbuf += 1
        name = name or f"ps{self.nbuf}"
        t = self.stack.enter_context(self.nc.psum_tensor(name, list(shape), dtype))
        b = Buf(self, t, name)
        b.excl = True
        return b

    def dram(self, name, shape, dtype, kind="Internal"):
        t = self.nc.dram_tensor(name, list(shape), dtype, kind=kind)
        return Buf(self, t.ap(), name)

    def _waits(self, eng, reads, writes):
        deps = []
        for b in reads:
            if b.w is not None:
                deps.append(b.w)
        for b in writes:
            if b.w is not None:
                deps.append(b.w)
            deps.extend(b.r)
        need = {}
        for (k, v, e) in deps:
            if e == "pe" and eng == "pe":
                continue
            if self.seen[eng].get(k, 0) >= v:
                continue
            need[k] = max(need.get(k, 0), v)
        for k, v in need.items():
            self.seen[eng][k] = v
            self.q[eng].append(("wait", k, v))

    def _mark(self, dep, reads, writes):
        for b in writes:
            b.w = dep
            b.r = []
        for b in reads:
            if b not in writes:
                b.r.append(dep)
                if len(b.r) > 24:
                    m = {}
                    for (k, v, e) in b.r:
                        if k not in m or m[k][1] < v:
                            m[k] = (k, v, e)
                    b.r = list(m.values())

    def op(self, eng, fn, reads=(), writes=()):
        reads = [b for b in reads if b is not None]
        writes = [b for b in writes if b is not None]
        xr = [b for b in reads if getattr(b, "excl", False)]
        if xr:
            writes = writes + [b for b in xr if b not in writes]
        self._waits(eng, reads, writes)
        k = "E_" + eng
        self.val[k] += 1
        dep = (k, self.val[k], eng)
        rec = _Rec()
        fn(rec)
        assert len(rec.calls) == 1
        m_, a_, kw_ = rec.calls[0]
        self.q[eng].append(("op", (lambda e, m_=m_, a_=a_, kw_=kw_: getattr(e, m_)(*a_, **kw_)), k, 1))
        self._mark(dep, reads, writes)
        return dep

    def dma(self, eng, out_b, out_ap, in_b, in_ap, is_output=False, **kw):
        reads = [in_b]
        writes = [out_b]
        self._waits(eng, reads, writes)
        if out_b.dsem is None:
            out_b.dsem = self._newsem("D_" + out_b.name)
        k = out_b.dsem
        self.val[k] += 16
        dep = (k, self.val[k], "dma")
        self.q[eng].append(("op", (lambda e, o=out_ap, i=in_ap, kw=kw: e.dma_start(out=o, in_=i, **kw)), k, 16))
        self._mark(dep, reads, writes)
        if is_output:
            self.out_deps.append(dep)
        return dep

    def finish(self, eng="sp"):
        need = {}
        for (k, v, e) in self.out_deps:
            need[k] = max(need.get(k, 0), v)
        for k, v in need.items():
            self.q[eng].append(("wait", k, v))

    def emit(self):
        nc = self.nc
        with nc.Block() as block:
            def run(engine, name):
                for it in self.q[name]:
                    if it[0] == "wait":
                        engine.wait_ge(self.sems[it[1]], it[2])
                    else:
                        it[1](engine).then_inc(self.sems[it[2]], it[3])

            @block.sync
            def _(e):
                run(e, "sp")

            @block.tensor
            def _(e):
                run(e, "pe")

            @block.scalar
            def _(e):
                run(e, "act")

            @block.vector
            def _(e):
                run(e, "dve")

            @block.gpsimd
            def _(e):
                run(e, "pool")


def new_prog():
    nc = bass.Bass("TRN2", target_bir_lowering=False)
    stack = ExitStack()
    return nc, stack, Prog(nc, stack)


D = 4096
MODW = 9 * D
MODC = MODW // NCORES


def build_mod(nlayers=2):
    nc, stack, P = new_prog()
    with stack:
        cT = nc.dram_tensor("cT", [128, 96], F32, kind="ExternalInput").ap()
        adaw = nc.dram_tensor("adaw", [nlayers * D, MODC], F32, kind="ExternalInput").ap()
        adab = nc.dram_tensor("adab", [nlayers * 3, MODC], F32, kind="ExternalInput").ap()
        mod = nc.dram_tensor("mod", [nlayers * 3, MODC], F32, kind="ExternalOutput").ap()
        B_in = Buf(P, None, "dram_in")
        B_out = Buf(P, None, "dram_out")
        c_sb = P.sbuf([128, 96], F32, "c_sb")
        cs = P.sbuf([128, 96], F32, "cs")
        P.dma("sp", c_sb, c_sb[:, :], B_in, cT[:, :])
        P.op("act", lambda e: e.activation(out=cs[:, :], in_=c_sb[:, :], func=AF.Silu), [c_sb], [cs])
        HALF = MODC // 2
        wbufs = [P.sbuf([128, HALF], F32, f"w{i}") for i in range(3)]
        pss = [P.psum([128, 512], F32, f"acc{i}") for i in range(5)]
        bsb = P.sbuf([3, MODC], F32, "bsb")
        osb = P.sbuf([3, MODC], F32, "osb")
        tiles = [(o, min(512, HALF - o)) for o in range(0, HALF, 512)]
        it = 0
        for l in range(nlayers):
            P.dma("sp", bsb, bsb[:, :], B_in, adab[l * 3:(l + 1) * 3, :])
            for h in range(2):
                for kc in range(32):
                    wb = wbufs[it % 3]
                    it += 1
                    q = "sp" if kc % 2 == 0 else "act"
                    P.dma(q, wb, wb[:, :], B_in,
                          adaw[l * D + kc * 128:l * D + (kc + 1) * 128, h * HALF:(h + 1) * HALF])
                    for ti, (o, n) in enumerate(tiles):
                        P.op("pe", lambda e, ti=ti, o=o, n=n, kc=kc, wb=wb: e.matmul(
                            pss[ti][0:3, 0:n], lhsT=cs[:, kc * 3:(kc + 1) * 3], rhs=wb[:, o:o + n],
                            start=(kc == 0), stop=(kc == 31)), [cs, wb], [pss[ti]])
                for ti, (o, n) in enumerate(tiles):
                    c0 = h * HALF + o
                    P.op("dve", lambda e, ti=ti, n=n, c0=c0: e.tensor_tensor(
                        out=osb[:, c0:c0 + n], in0=pss[ti][0:3, 0:n], in1=bsb[:, c0:c0 + n], op=ALU.add),
                        [pss[ti], bsb], [osb])
            P.dma("sp", B_out, mod[l * 3:(l + 1) * 3, :], osb, osb[:, :], is_output=True)
        P.finish("sp")
        P.emit()
    return nc


def run_mod(c, c_ctx, ada_w, ada_b):
    L = ada_w.shape[0]
    call = np.concatenate([c, c_ctx[None, :]], axis=0).astype(np.float32)
    cT = np.ascontiguousarray(call.reshape(3, 32, 128).transpose(2, 1, 0)).reshape(128, 96)
    nc = build_mod(L)
    in_maps = []
    for i in range(NCORES):
        sl = slice(i * MODC, (i + 1) * MODC)
        in_maps.append({
            "cT": cT,
            "adaw": np.ascontiguousarray(ada_w[:, :, sl]).reshape(L * D, MODC),
            "adab": np.ascontiguousarray(np.repeat(ada_b[:, None, sl], 3, axis=1)).reshape(L * 3, MODC),
        })
    res = run_bass_kernel_spmd(nc, in_maps, core_ids=list(range(NCORES)))
    mod = np.concatenate([r["mod"].reshape(L, 3, MODC) for r in res.results], axis=2)
    return mod.reshape(L, 3, 3, 3, D)


def _prog_coll(self, kind, in_b, in_t, out_b, out_t, groups=None):
    eng = "pool"
    self._waits(eng, [in_b], [out_b])
    k = "C_" + out_b.name
    if k not in self.sems:
        self._newsem(k)
    self.val[k] += 1
    dep = (k, self.val[k], "dma")
    groups = groups or [list(range(NCORES))]
    self.q[eng].append(("op", (lambda e, o=out_t, i=in_t: e.collective_compute(
        kind, ALU.bypass, replica_groups=groups, ins=[i.opt()], outs=[o.opt()])), k, 1))
    self._mark(dep, [in_b], [out_b])
    return dep


Prog.coll = _prog_coll


def _prog_barrier(self):
    for e in self.ENG:
        for k, v in self.val.items():
            if v > 0 and self.seen[e].get(k, 0) < v:
                self.seen[e][k] = v
                self.q[e].append(("wait", k, v))


class _Scope:
    def __init__(self, prog):
        self.prog = prog
        self.stack = ExitStack()

    def __enter__(self):
        self.stack.__enter__()
        return self

    def __exit__(self, *a):
        self.prog.barrier()
        return self.stack.__exit__(*a)

    def sbuf(self, shape, dtype, name=None):
        P = self.prog
        P.nbuf += 1
        name = (name or "sb") + f"_{P.nbuf}"
        t = self.stack.enter_context(P.nc.sbuf_tensor(name, list(shape), dtype))
        return Buf(P, t, name)

    def psum(self, shape, dtype=F32, name=None):
        P = self.prog
        P.nbuf += 1
        name = (name or "ps") + f"_{P.nbuf}"
        t = self.stack.enter_context(P.nc.psum_tensor(name, list(shape), dtype))
        b = Buf(P, t, name)
        b.excl = True
        return b


def _prog_scope(self):
    return _Scope(self)


Prog.barrier = _prog_barrier
Prog.scope = _prog_scope


def _prog_dma(self, eng, out_b, out_ap, in_b, in_ap, is_output=False, **kw):
    reads = [in_b]
    writes = [out_b]
    self._waits(eng, reads, writes)
    if out_b.dsem is None:
        pool = getattr(self, "_dpool", None)
        if pool is None:
            pool = self._dpool = []
        if pool:
            out_b.dsem = pool.pop()
        else:
            out_b.dsem = self._newsem(f"D{len(self.sems)}")
    k = out_b.dsem
    self.val[k] += 16
    dep = (k, self.val[k], "dma")
    self.q[eng].append(("op", (lambda e, o=out_ap, i=in_ap, kw=kw: e.dma_start(out=o, in_=i, **kw)), k, 16))
    self._mark(dep, reads, writes)
    if is_output:
        self.out_deps.append(dep)
    return dep


Prog.dma = _prog_dma

_scope_sbuf0 = _Scope.sbuf
_scope_exit0 = _Scope.__exit__


def _scope_sbuf(self, shape, dtype, name=None):
    b = _scope_sbuf0(self, shape, dtype, name)
    if not hasattr(self, "bufs"):
        self.bufs = []
    self.bufs.append(b)
    return b


def _scope_exit(self, *a):
    r = _scope_exit0(self, *a)
    P = self.prog
    if not hasattr(P, "_dpool"):
        P._dpool = []
    for b in getattr(self, "bufs", []):
        if b.dsem is not None:
            P._dpool.append(b.dsem)
            b.dsem = None
    return r


_Scope.sbuf = _scope_sbuf
_Scope.__exit__ = _scope_exit


class Cfg:
    def __init__(self, D, FF, SEQ, CTX, GW, DNH, NAH, GMG, L, TB):
        self.D, self.FF, self.SEQ, self.CTX, self.GW = D, FF, SEQ, CTX, GW
        self.DNH, self.NAH, self.GMG, self.L, self.TB = DNH, NAH, GMG, L, TB
        self.KC = D // 128
        self.FC = FF // 128
        self.T = SEQ + CTX
        self.ROWS = SEQ // GW
        self.DNW, self.NAW, self.GMW = DNH * 128, NAH * 128, GMG * 128
        self.NCK = self.T // 64
        self.plan = [("dn", "fm", 3 * DNH), ("na", "fm", 2 * NAH), ("gu", "fm", GMG), ("g", "fm", 3 * self.KC),
                     ("nav", "tm", NAH), ("dg", "tm", DNH), ("gv", "tm", GMG), ("ab", "tm", 1)]
        self.NCH = sum(p[2] for p in self.plan)

    def blocks(self):
        out = []
        t = 0
        while t < self.T:
            n = min(self.TB, self.T - t)
            out.append((t, n))
            t += n
        return out

    def segs(self, t0, n):
        out = []
        if t0 < self.SEQ:
            m = min(n, self.SEQ - t0)
            out.append((0, m, 0))
            if m < n:
                out.append((m, n - m, 1))
        else:
            out.append((0, n, 1))
        return out


def tiles_of(n, w=512):
    out = []
    o = 0
    while o < n:
        m = min(w, n - o)
        out.append((o, m))
        o += m
    return out


FULL = Cfg(D=4096, FF=5632, SEQ=4096, CTX=256, GW=64, DNH=8, NAH=16, GMG=8, L=2, TB=1088)

GELU_C = 1.5957691216057308


class Main:
    def __init__(self, cfg, dbg=False, nlayers=None, stop_after=None):
        self.cfg = cfg
        self.dbg = dbg
        self.nl = nlayers or cfg.L
        self.stop_after = stop_after
        self.nc, self.stack, self.P = new_prog()
        self.qi = 0

    def din(self, name, shape, dtype=F32):
        ap = self.nc.dram_tensor(name, list(shape), dtype, kind="ExternalInput").ap()
        return Buf(self.P, ap, name)

    def dscr(self, name, shape, dtype=F32):
        kind = "ExternalOutput" if self.dbg else "Internal"
        ap = self.nc.dram_tensor(name, list(shape), dtype, kind=kind).ap()
        return Buf(self.P, ap, name)

    def dmaq(self):
        return "sp"

    def load_consts(self):
        P, c = self.P, self.cfg
        self.cst = self.din("consts", [128, CONST_W])
        self.csb = P.sbuf([128, CONST_W], F32, "csb")
        P.dma("sp", self.csb, self.csb[:, :], self.cst, self.cst[:, :])
        self.ones_bf = P.sbuf([128, 128], BF16, "ones_bf")
        P.op("dve", lambda e: e.tensor_copy(out=self.ones_bf[:, :], in_=self.csb[:, C_ONES:C_ONES + 128]),
             [self.csb], [self.ones_bf])
        self.ident_bf = P.sbuf([128, 128], BF16, "ident_bf")
        P.op("dve", lambda e: e.tensor_copy(out=self.ident_bf[:, :], in_=self.csb[:, C_ID:C_ID + 128]),
             [self.csb], [self.ident_bf])

    def mod_scalars(self, S, l, sub, half):
        P, c = self.P, self.cfg
        KC = c.KC
        a, sh, gt = [], [], []
        for grp in range(2):
            base = (((l * 2 + grp) * 3 + sub) * 3) * KC
            shift_ap = lambda b=base: self.modsb[:, b:b + KC]
            scale_ap = lambda b=base: self.modsb[:, b + KC:b + 2 * KC]
            gate_ap = lambda b=base: self.modsb[:, b + 2 * KC:b + 3 * KC]
            nb = (l * 3 + sub) * KC
            at = S.sbuf([128, KC], F32, "a_sc")
            P.op("dve", lambda e, at=at, sa=scale_ap, nb=nb: e.scalar_tensor_tensor(
                out=at[:, :], in0=sa(), scalar=1.0, in1=self.normsb[:, nb:nb + KC], op0=ALU.add, op1=ALU.mult),
                [self.modsb, self.normsb], [at])
            st = S.sbuf([128, KC], F32, "shift")
            P.op("dve", lambda e, st=st, sa=shift_ap: e.tensor_copy(out=st[:, :], in_=sa()), [self.modsb], [st])
            g = S.sbuf([128, KC], F32, "gate")
            P.op("dve", lambda e, g=g, ga=gate_ap: e.tensor_scalar(
                out=g[:, :], in0=ga(), scalar1=(0.5 if half else 1.0), scalar2=None, op0=ALU.mult),
                [self.modsb], [g])
            a.append(at)
            sh.append(st)
            gt.append(g)
        return a, sh, gt

    def norm_block(self, S, hsrc, t0, n, a, sh, xin, hst, sqt, pss, tmp, rstd_b):
        P, c = self.P, self.cfg
        KC, D = c.KC, c.D
        for si, (o, m) in enumerate(tiles_of(n, 128)):
            hs = hst[si % 2]
            P.dma(self.dmaq(), hs, hs[:, 0:KC * m].rearrange("p (k t) -> p k t", k=KC), hsrc,
                  hsrc.t[:, t0 + o:t0 + o + m].rearrange("(k p) t -> p k t", p=128))
            P.op("pool", lambda e, hs=hs, m=m: e.tensor_tensor(
                out=sqt[:, 0:KC * m], in0=hs[:, 0:KC * m], in1=hs[:, 0:KC * m], op=ALU.mult), [hs], [sqt])
            ps = pss[si % 2]
            for kc in range(KC):
                P.op("pe", lambda e, ps=ps, kc=kc, m=m: e.matmul(
                    ps[:, 0:m], lhsT=self.ones_bf[:, :], rhs=sqt[:, kc * m:(kc + 1) * m],
                    start=(kc == 0), stop=(kc == KC - 1)), [self.ones_bf, sqt], [ps])
            P.op("act", lambda e, ps=ps, m=m: e.activation(
                out=rstd_b[:, 0:m], in_=ps[:, 0:m], func=AF.Sqrt, bias=self.eps_ap(), scale=1.0 / D),
                [ps, self.csb], [rstd_b])
            P.op("dve", lambda e, m=m: e.reciprocal(out=rstd_b[:, 0:m], in_=rstd_b[:, 0:m]), [rstd_b], [rstd_b])
            for (so, sn, grp) in c.segs(t0 + o, m):
                for kc in range(KC):
                    tb = tmp[kc % 2]
                    P.op("dve", lambda e, tb=tb, hs=hs, kc=kc, m=m, so=so, sn=sn, grp=grp: e.scalar_tensor_tensor(
                        out=tb[:, 0:sn], in0=hs[:, kc * m + so:kc * m + so + sn], scalar=a[grp][:, kc:kc + 1],
                        in1=rstd_b[:, so:so + sn], op0=ALU.mult, op1=ALU.mult), [hs, a[grp], rstd_b], [tb])
                    P.op("act", lambda e, tb=tb, kc=kc, o=o, so=so, sn=sn, grp=grp: e.activation(
                        out=xin[:, kc, o + so:o + so + sn], in_=tb[:, 0:sn], func=AF.Identity,
                        bias=sh[grp][:, kc:kc + 1], scale=1.0), [tb, sh[grp]], [xin])

    def eps_ap(self):
        return self.csb[:, C_EPS:C_EPS + 1]

    def linear(self, S, xin, KCn, wsrc, chunks, n, w32, w16, psb, epi_fm=None, epi_tm=None):
        P = self.P
        tls = tiles_of(n, 512)
        subt = tiles_of(n, 128)
        cnt = getattr(self, "_lin_cnt", 0)
        W = KCn * 128
        hW = (W // 2)
        base = cnt
        nchunks = len(chunks)

        def issue_dma(i):
            wb = w32[(base + i) % 2]
            P.dma("sp", wb, wb[:, 0:W], wsrc, wsrc.t[chunks[i][0], :, 0:W])

        def issue_cast(i):
            a, b = w16[(base + i) % 2], w32[(base + i) % 2]
            P.op("act", lambda e: e.activation(out=a[:, 0:hW], in_=b[:, 0:hW], func=AF.Copy), [b], [a])
            P.op("pool", lambda e: e.tensor_copy(out=a[:, hW:W], in_=b[:, hW:W]), [b], [a])

        issue_dma(0)
        if nchunks > 1:
            issue_dma(1)
        issue_cast(0)
        for idx, (ci, mode, user) in enumerate(chunks):
            wb16 = w16[cnt % 2]
            if idx + 1 < nchunks:
                issue_cast(idx + 1)
            if idx + 2 < nchunks:
                issue_dma(idx + 2)
            if mode == "fm":
                half = (cnt % 2) * 4
                for ti, (o, m) in enumerate(tls):
                    ps = psb[half + ti]
                    for kc in range(KCn):
                        P.op("pe", lambda e, ps=ps, kc=kc, o=o, m=m, wb16=wb16: e.matmul(
                            ps[:, 0:m], lhsT=wb16[:, kc * 128:(kc + 1) * 128], rhs=xin[:, kc, o:o + m],
                            start=(kc == 0), stop=(kc == KCn - 1)), [wb16, xin], [ps])
                    epi_fm(user, ti, o, m, ps)
            else:
                for si, (o, m) in enumerate(subt):
                    ps = psb[(cnt % 2) * 4 + (si % 4)]
                    for kc in range(KCn):
                        P.op("pe", lambda e, ps=ps, kc=kc, o=o, m=m, wb16=wb16: e.matmul(
                            ps[0:m, 0:128], lhsT=xin[:, kc, o:o + m], rhs=wb16[:, kc * 128:(kc + 1) * 128],
                            start=(kc == 0), stop=(kc == KCn - 1)), [wb16, xin], [ps])
                    epi_tm(user, si, o, m, ps)
            cnt += 1
        self._lin_cnt = cnt

    def ffn(self, l, i, hsrc, hdst):
        P, c = self.P, self.cfg
        KC, FC = c.KC, c.FC
        sub = 0 if i == 0 else 2
        TB2 = c.TB // 2
        with P.scope() as S:
            a, sh, gt = self.mod_scalars(S, l, sub, True)
            WMAX = max(KC, FC) * 128
            XW = max(KC * c.TB, FC * TB2)
            xin_b = S.sbuf([128, XW], BF16, "xin")
            xin1 = xin_b[:, 0:KC * c.TB].rearrange("p (k t) -> p k t", k=KC)
            xin2 = xin_b[:, 0:FC * TB2].rearrange("p (k t) -> p k t", k=FC)
            X1 = Buf(P, xin1, "x1v"); X1 = _alias(xin_b, xin1)
            X2 = _alias(xin_b, xin2)
            w32 = [S.sbuf([128, WMAX], F32, f"w32_{j}") for j in range(2)]
            w16 = [S.sbuf([128, WMAX], BF16, f"w16_{j}") for j in range(2)]
            sqt = S.sbuf([128, KC * 128], BF16, "sqt")
            tmp = [S.sbuf([128, 512], F32, f"tmp{j}") for j in range(2)]
            rstd_b = S.sbuf([128, 128], F32, "rstd")
            s1 = [S.sbuf([128, 512], F32, f"s1_{j}") for j in range(len(tiles_of(c.TB)))]
            orow = [S.sbuf([128, c.TB], BF16, f"orow{j}") for j in range(2)]
            hrow = [S.sbuf([128, 512], F32, f"hrow{j}") for j in range(2)]
            orow32 = [S.sbuf([128, 512], F32, f"orow32{j}") for j in range(2)]
            psb = [S.psum([128, 512], F32, f"psb{j}") for j in range(8)]
            for (t0, n) in c.blocks():
                self.norm_block(S, hsrc, t0, n, a, sh, X1, w32, sqt, psb[0:2], tmp, rstd_b)

                def epi1(user, ti, o, m, ps, t0=t0, n=n):
                    f, which = user
                    if which == 0:
                        P.op("act", lambda e, ps=ps, m=m, ti=ti: e.activation(
                            out=s1[ti][:, 0:m], in_=ps[:, 0:m], func=AF.Silu), [ps], [s1[ti]])
                    else:
                        ob = orow[f % 2]
                        P.op("dve", lambda e, ps=ps, m=m, o=o, ob=ob, ti=ti: e.tensor_tensor(
                            out=ob[:, o:o + m], in0=ps[:, 0:m], in1=s1[ti][:, 0:m], op=ALU.mult),
                            [ps, s1[ti]], [ob])
                        if o + m == n:
                            P.dma("sp", self.gT, self.gT.t[f * 128:(f + 1) * 128, t0:t0 + n], ob, ob[:, 0:n])

                chunks = []
                for f in range(FC):
                    chunks.append((f, "fm", (f, 0)))
                    chunks.append((FC + f, "fm", (f, 1)))
                self.linear(S, X1, KC, self.w13[l][i], chunks, n, w32, w16, psb, epi_fm=epi1)
            cnt2 = [0]
            for (t0, n) in [(t, min(TB2, c.T - t)) for t in range(0, c.T, TB2)]:
                P.dma("sp", X2, X2[:, 0:FC, 0:n], self.gT,
                      self.gT.t[:, t0:t0 + n].rearrange("(k p) t -> p k t", p=128))

                def epi2(user, ti, o, m, ps, t0=t0, n=n):
                    d = user
                    j = cnt2[0] % 2
                    cnt2[0] += 1
                    hr, o32 = hrow[j], orow32[j]
                    P.dma("sp", hr, hr[:, 0:m], hsrc, hsrc.t[d * 128:(d + 1) * 128, t0 + o:t0 + o + m])
                    for (so, sn, grp) in c.segs(t0 + o, m):
                        P.op("dve", lambda e, ps=ps, so=so, sn=sn, grp=grp, d=d, hr=hr, o32=o32: e.scalar_tensor_tensor(
                            out=o32[:, so:so + sn], in0=ps[:, so:so + sn], scalar=gt[grp][:, d:d + 1],
                            in1=hr[:, so:so + sn], op0=ALU.mult, op1=ALU.add), [ps, gt[grp], hr], [o32])
                    P.dma("sp", hdst, hdst.t[d * 128:(d + 1) * 128, t0 + o:t0 + o + m], o32, o32[:, 0:m])

                self.linear(S, X2, FC, self.w2[l][i], [(d, "fm", d) for d in range(KC)], n, w32, w16, psb,
                            epi_fm=epi2)


class _AliasBuf:
    def __init__(self, parent, ap):
        object.__setattr__(self, "_p", parent)
        object.__setattr__(self, "_ap", ap)

    def __getattr__(self, k):
        return getattr(object.__getattribute__(self, "_p"), k)

    def __setattr__(self, k, v):
        setattr(object.__getattribute__(self, "_p"), k, v)

    def __getitem__(self, idx):
        return object.__getattribute__(self, "_ap")[idx]

    def __eq__(self, o):
        return _root(self) is _root(o)

    def __hash__(self):
        return id(_root(self))


def _root(b):
    while isinstance(b, _AliasBuf):
        b = object.__getattribute__(b, "_p")
    return b


def _alias(parent, ap):
    return _AliasBuf(parent, ap)


C_ONES, C_ID, C_NEG1, C_PERM = 0, 128, 256, 384
C_CUMF, C_CUMB, C_MSF, C_MSB, C_MITF, C_MITB = 512, 576, 640, 704, 768, 832
C_EPS = 896
CONST_W = 904
NEGBIG = -30000.0


def make_consts():
    c = np.zeros((128, CONST_W), np.float32)
    c[:, C_ONES:C_ONES + 128] = 1.0
    c[:, C_ID:C_ID + 128] = np.eye(128, dtype=np.float32)
    c[:, C_NEG1:C_NEG1 + 128] = -1.0
    pm = np.zeros((128, 128), np.float32)
    for m in range(128):
        q = m // 32
        partner = m + 32 if q % 2 == 0 else m - 32
        pm[partner, m] = 1.0
    c[:, C_PERM:C_PERM + 128] = pm
    i = np.arange(64)
    c[:64, C_CUMF:C_CUMF + 64] = (i[:, None] <= i[None, :])
    c[:64, C_CUMB:C_CUMB + 64] = (i[:, None] >= i[None, :])
    c[:64, C_MSF:C_MSF + 64] = np.where(i[None, :] < i[:, None], 0.0, NEGBIG)
    c[:64, C_MSB:C_MSB + 64] = np.where(i[None, :] > i[:, None], 0.0, NEGBIG)
    c[:64, C_MITF:C_MITF + 64] = np.where(i[:, None] <= i[None, :], 0.0, NEGBIG)
    c[:64, C_MITB:C_MITB + 64] = np.where(i[:, None] >= i[None, :], 0.0, NEGBIG)
    c[:, C_EPS] = 1e-6
    c[:, C_EPS + 1] = np.log(128.0 ** -0.5)
    return c


def _gelu(self, S, src, rows, m, dst_fn, reads, writes, k):
    P = self.P
    gx, g1 = self._gx[k % 2], self._g1[k % 2]
    P.op("act", lambda e: e.activation(out=gx[0:rows, 0:m], in_=src, func=AF.Copy), reads, [gx])
    P.op("dve", lambda e: e.tensor_tensor(out=g1[0:rows, 0:m], in0=gx[0:rows, 0:m], in1=gx[0:rows, 0:m], op=ALU.mult),
         [gx], [g1])
    P.op("pool", lambda e: e.tensor_scalar(out=g1[0:rows, 0:m], in0=g1[0:rows, 0:m], scalar1=0.044715, scalar2=1.0,
                                           op0=ALU.mult, op1=ALU.add), [g1], [g1])
    P.op("pool", lambda e: e.tensor_tensor(out=g1[0:rows, 0:m], in0=g1[0:rows, 0:m], in1=gx[0:rows, 0:m], op=ALU.mult),
         [g1, gx], [g1])
    P.op("act", lambda e: e.activation(out=g1[0:rows, 0:m], in_=g1[0:rows, 0:m], func=AF.Sigmoid, scale=GELU_C),
         [g1], [g1])
    P.op("dve", lambda e: e.tensor_tensor(out=dst_fn(), in0=gx[0:rows, 0:m], in1=g1[0:rows, 0:m], op=ALU.mult),
         [gx, g1], writes)


Main.gelu = _gelu


def _inproj(self, l, hsrc):
    P, c = self.P, self.cfg
    KC = c.KC
    with P.scope() as S:
        a, sh, gt = self.mod_scalars(S, l, 1, False)
        xin_b = S.sbuf([128, KC * c.TB], BF16, "xin")
        X1 = _alias(xin_b, xin_b[:, :].rearrange("p (k t) -> p k t", k=KC))
        w32 = [S.sbuf([128, KC * 128], F32, f"w32_{j}") for j in range(2)]
        w16 = [S.sbuf([128, KC * 128], BF16, f"w16_{j}") for j in range(2)]
        sqt = S.sbuf([128, KC * 128], BF16, "sqt")
        tmp = [S.sbuf([128, 512], F32, f"tmp{j}") for j in range(2)]
        rstd_b = S.sbuf([128, 128], F32, "rstd")
        self._gx = [S.sbuf([128, 512], F32, f"gx{j}") for j in range(2)]
        self._g1 = [S.sbuf([128, 512], F32, f"g1{j}") for j in range(2)]
        NS = len(tiles_of(c.TB, 128))
        orow32 = [S.sbuf([128, c.TB], F32, f"or32_{j}") for j in range(2)]
        orow16 = [S.sbuf([128, c.TB], BF16, f"or16_{j}") for j in range(2)]
        otm32 = [S.sbuf([128, NS * 128], F32, f"ot32_{j}") for j in range(2)]
        otm16 = [S.sbuf([128, NS * 128], BF16, f"ot16_{j}") for j in range(2)]
        psb = [S.psum([128, 512], F32, f"psb{j}") for j in range(8)]
        dst_fm = {"dn": self.z_dn, "na": self.z_na, "gu": self.z_gu, "g": self.z_g}
        dst_tm = {"nav": self.v_na, "dg": self.z_dg, "gv": self.z_gv, "ab": self.z_ab}
        chunks = []
        ci = 0
        for (name, mode, nch) in c.plan:
            for j in range(nch):
                chunks.append((ci, mode, (name, j, ci)))
                ci += 1
        gk = [0]
        for (t0, n) in c.blocks():
            self.norm_block(S, hsrc, t0, n, a, sh, X1, w32, sqt, psb[0:2], tmp, rstd_b)

            def epi_fm(user, ti, o, m, ps, t0=t0, n=n):
                name, j, ci = user
                ob = (orow16 if name == "na" else orow32)[ci % 2]
                if name in ("dn", "na"):
                    P.op("act", lambda e: e.activation(out=ob[:, o:o + m], in_=ps[:, 0:m], func=AF.Copy), [ps], [ob])
                elif name == "g":
                    P.op("act", lambda e: e.activation(out=ob[:, o:o + m], in_=ps[:, 0:m], func=AF.Sigmoid), [ps], [ob])
                else:
                    gk[0] += 1
                    self.gelu(S, ps[:, 0:m], 128, m, lambda: ob[:, o:o + m], [ps], [ob], gk[0])
                if o + m == n:
                    d = dst_fm[name]
                    P.dma("sp", d, d.t[j * 128:(j + 1) * 128, t0:t0 + n], ob, ob[:, 0:n])

            def epi_tm(user, si, o, m, ps, t0=t0, n=n):
                name, j, ci = user
                ob = (otm16 if name == "nav" else otm32)[ci % 2]
                if name in ("nav", "ab"):
                    P.op("act", lambda e: e.activation(out=ob[0:m, si * 128:(si + 1) * 128], in_=ps[0:m, 0:128],
                                                       func=AF.Copy), [ps], [ob])
                elif name == "dg":
                    P.op("act", lambda e: e.activation(out=ob[0:m, si * 128:(si + 1) * 128], in_=ps[0:m, 0:128],
                                                       func=AF.Silu), [ps], [ob])
                else:
                    gk[0] += 1
                    self.gelu(S, ps[0:m, 0:128], m, 128, lambda: ob[0:m, si * 128:(si + 1) * 128], [ps], [ob], gk[0])
                if o + m == n:
                    d = dst_tm[name]
                    nfull = n // 128
                    if nfull:
                        P.dma("sp", d, d.t[t0:t0 + nfull * 128, j * 128:(j + 1) * 128].rearrange("(s p) c -> p s c", p=128),
                              ob, ob[:, 0:nfull * 128].rearrange("p (s c) -> p s c", c=128))
                    rem = n - nfull * 128
                    if rem:
                        P.dma("sp", d, d.t[t0 + nfull * 128:t0 + n, j * 128:(j + 1) * 128],
                              ob, ob[0:rem, nfull * 128:(nfull + 1) * 128])

            self.linear(S, X1, KC, self.win[l], chunks, n, w32, w16, psb, epi_fm=epi_fm, epi_tm=epi_tm)


Main.inproj = _inproj


def _gmlp(self, l):
    P, c = self.P, self.cfg
    G = c.GMG
    GW_ = c.GMW
    with P.scope() as S:
        sgu = S.sbuf([128, G * 128], F32, "sgu32")
        P.dma("sp", sgu, sgu[:, :], self.sguT, self.sguT.t[:, l * G * 128:(l + 1) * G * 128])
        sgu16 = S.sbuf([128, G * 128], BF16, "sgu16")
        P.op("act", lambda e: e.activation(out=sgu16[:, :], in_=sgu[:, :], func=AF.Copy), [sgu], [sgu16])
        bbc = S.sbuf([128, G * 128], F32, "bbc")
        P.dma("sp", bbc, bbc[:, :], self.sgub, self.sgub.t[:, l * G * 128:(l + 1) * G * 128])
        nwb = S.sbuf([128, GW_], F32, "nwb")
        P.dma("sp", nwb, nwb[:, :], self.sgunw, self.sgunw.t[:, l * GW_:(l + 1) * GW_])
        vin = [S.sbuf([128, GW_], F32, f"vin{j}") for j in range(2)]
        uin = [S.sbuf([128, G * 128], F32, f"uin{j}") for j in range(2)]
        v16 = [S.sbuf([128, GW_], BF16, f"v16{j}") for j in range(2)]
        sq = S.sbuf([128, GW_], F32, "sq")
        st = [S.sbuf([128, 4], F32, f"st{j}") for j in range(2)]
        yo = [S.sbuf([128, G * 128], BF16, f"yo{j}") for j in range(2)]
        t32 = [S.sbuf([128, 128], F32, f"t32{j}") for j in range(2)]
        ps = [S.psum([128, 512], F32, f"gps{j}") for j in range(4)]
        for ck in range(c.T // 128):
            t0 = ck * 128
            j = ck % 2
            v, u, vb, s_, y = vin[j], uin[j], v16[j], st[j], yo[j]
            P.dma("sp", v, v[:, :], self.z_gv, self.z_gv.t[t0:t0 + 128, :])
            P.dma("sp", u, u[:, :].rearrange("p (g t) -> p g t", g=G), self.z_gu,
                  self.z_gu.t[:, t0:t0 + 128].rearrange("(g p) t -> p g t", p=128))
            P.op("dve", lambda e, v=v, s_=s_: e.tensor_reduce(out=s_[:, 0:1], in_=v[:, :], axis=AX.X, op=ALU.add),
                 [v], [s_])
            P.op("dve", lambda e, s_=s_: e.tensor_scalar(out=s_[:, 1:2], in0=s_[:, 0:1], scalar1=-1.0 / GW_, scalar2=None,
                                                         op0=ALU.mult), [s_], [s_])
            P.op("act", lambda e, v=v, s_=s_: e.activation(out=v[:, :], in_=v[:, :], func=AF.Identity, bias=s_[:, 1:2],
                                                           scale=1.0), [v, s_], [v])
            P.op("act", lambda e, v=v, s_=s_: e.activation(out=sq[:, :], in_=v[:, :], func=AF.Square,
                                                           accum_out=s_[:, 2:3]), [v], [sq, s_])
            P.op("act", lambda e, s_=s_: e.activation(out=s_[:, 3:4], in_=s_[:, 2:3], func=AF.Sqrt, bias=self.eps_ap(),
                                                      scale=1.0 / GW_), [s_, self.csb], [s_])
            P.op("dve", lambda e, s_=s_: e.reciprocal(out=s_[:, 3:4], in_=s_[:, 3:4]), [s_], [s_])
            P.op("dve", lambda e, v=v, vb=vb, s_=s_: e.scalar_tensor_tensor(
                out=vb[:, :], in0=v[:, :], scalar=s_[:, 3:4], in1=nwb[:, :], op0=ALU.mult, op1=ALU.mult),
                [v, s_, nwb], [vb])
            for g in range(G):
                p = ps[g % 4]
                tt = t32[g % 2]
                P.op("pe", lambda e, p=p, g=g, vb=vb: e.matmul(p[:, 0:128], lhsT=vb[:, g * 128:(g + 1) * 128],
                                                               rhs=sgu16[:, g * 128:(g + 1) * 128], start=True, stop=True),
                     [vb, sgu16], [p])
                P.op("dve", lambda e, p=p, g=g, tt=tt: e.tensor_tensor(out=tt[:, :], in0=p[:, 0:128],
                                                                       in1=bbc[:, g * 128:(g + 1) * 128], op=ALU.add),
                     [p, bbc], [tt])
                P.op("pool", lambda e, g=g, tt=tt, u=u, y=y: e.tensor_tensor(
                    out=y[:, g * 128:(g + 1) * 128], in0=tt[:, :], in1=u[:, g * 128:(g + 1) * 128], op=ALU.mult),
                    [tt, u], [y])
            P.dma("sp", self.yaT, self.yaT.t[:, t0:t0 + 128].rearrange("(g p) t -> p g t", p=128),
                  y, y[:, :].rearrange("p (g t) -> p g t", g=G))


Main.gmlp = _gmlp


def _merge(self, l, hsrc, hdst):
    P, c = self.P, self.cfg
    KC = c.KC
    ka, kb, kc_ = c.GMW // 128, c.DNW // 128, c.NAW // 128
    KY = ka + kb + kc_
    with P.scope() as S:
        a, sh, gt = self.mod_scalars(S, l, 1, False)
        TBm = c.TB // 2
        yin_b = S.sbuf([128, KY * TBm], BF16, "yin")
        YIN = _alias(yin_b, yin_b[:, :].rearrange("p (k t) -> p k t", k=KY))
        yy_b = S.sbuf([128, KC * TBm], BF16, "yy")
        YY = _alias(yy_b, yy_b[:, :].rearrange("p (k t) -> p k t", k=KC))
        w32 = [S.sbuf([128, KC * 128], F32, f"w32_{j}") for j in range(2)]
        w16 = [S.sbuf([128, KC * 128], BF16, f"w16_{j}") for j in range(2)]
        grow = [S.sbuf([128, 512], F32, f"grow{j}") for j in range(3)]
        accs = [S.sbuf([128, 512], F32, f"acc{j}") for j in range(3)]
        hrow = [S.sbuf([128, 512], F32, f"hrow{j}") for j in range(2)]
        o32 = [S.sbuf([128, 512], F32, f"o32{j}") for j in range(2)]
        psb = [S.psum([128, 512], F32, f"psb{j}") for j in range(8)]
        cnt = [0]
        for (t0, n) in [(t, min(TBm, c.T - t)) for t in range(0, c.T, TBm)]:
            P.dma("sp", YIN, YIN[:, 0:ka, 0:n], self.yaT, self.yaT.t[:, t0:t0 + n].rearrange("(k p) t -> p k t", p=128))
            P.dma("sp", YIN, YIN[:, ka:ka + kb, 0:n], self.ybT, self.ybT.t[:, t0:t0 + n].rearrange("(k p) t -> p k t", p=128))
            P.dma("sp", YIN, YIN[:, ka + kb:KY, 0:n], self.ycT, self.ycT.t[:, t0:t0 + n].rearrange("(k p) t -> p k t", p=128))
            tls = tiles_of(n, 512)
            for d in range(KC):
                for bi, (koff, kn, wsrc) in enumerate(((0, ka, self.pa[l]), (ka, kb, self.pb[l]), (ka + kb, kc_, self.pc[l]))):
                    k2 = cnt[0]
                    cnt[0] += 1
                    wb32, wb16 = w32[k2 % 2], w16[k2 % 2]
                    W = kn * 128
                    P.dma("sp", wb32, wb32[:, 0:W], wsrc, wsrc.t[d, :, 0:W])
                    P.op("act", lambda e, wb16=wb16, wb32=wb32, W=W: e.activation(out=wb16[:, 0:W], in_=wb32[:, 0:W],
                                                                                  func=AF.Copy), [wb32], [wb16])
                    for ti, (o, m) in enumerate(tls):
                        ps = psb[(k2 % 2) * 4 + ti]
                        for kk in range(kn):
                            P.op("pe", lambda e, ps=ps, kk=kk, o=o, m=m, wb16=wb16, koff=koff, kn=kn: e.matmul(
                                ps[:, 0:m], lhsT=wb16[:, kk * 128:(kk + 1) * 128], rhs=YIN[:, koff + kk, o:o + m],
                                start=(kk == 0), stop=(kk == kn - 1)), [wb16, YIN], [ps])
                        gr = grow[bi]
                        P.dma("sp", gr, gr[:, 0:m], self.z_g,
                              self.z_g.t[(bi * KC + d) * 128:(bi * KC + d + 1) * 128, t0 + o:t0 + o + m])
                        ac = accs[ti]
                        if bi == 0:
                            P.op("dve", lambda e, ps=ps, m=m, gr=gr, ac=ac: e.tensor_tensor(
                                out=ac[:, 0:m], in0=ps[:, 0:m], in1=gr[:, 0:m], op=ALU.mult), [ps, gr], [ac])
                        else:
                            P.op("dve", lambda e, ps=ps, m=m, gr=gr: e.tensor_tensor(
                                out=gr[:, 0:m], in0=ps[:, 0:m], in1=gr[:, 0:m], op=ALU.mult), [ps, gr], [gr])
                            if bi == 1:
                                P.op("pool", lambda e, m=m, gr=gr, ac=ac: e.tensor_tensor(
                                    out=ac[:, 0:m], in0=ac[:, 0:m], in1=gr[:, 0:m], op=ALU.add), [ac, gr], [ac])
                            else:
                                P.op("pool", lambda e, m=m, gr=gr, ac=ac, d=d, o=o: e.tensor_tensor(
                                    out=YY[:, d, o:o + m], in0=ac[:, 0:m], in1=gr[:, 0:m], op=ALU.add), [ac, gr], [YY])
            kcnt = [0]

            def epi(user, ti, o, m, ps, t0=t0, n=n):
                d = user
                j = kcnt[0] % 2
                kcnt[0] += 1
                hr, ob = hrow[j], o32[j]
                P.dma("sp", hr, hr[:, 0:m], hsrc, hsrc.t[d * 128:(d + 1) * 128, t0 + o:t0 + o + m])
                for (so, sn, grp) in c.segs(t0 + o, m):
                    P.op("dve", lambda e, ps=ps, so=so, sn=sn, grp=grp, d=d, hr=hr, ob=ob: e.scalar_tensor_tensor(
                        out=ob[:, so:so + sn], in0=ps[:, so:so + sn], scalar=gt[grp][:, d:d + 1],
                        in1=hr[:, so:so + sn], op0=ALU.mult, op1=ALU.add), [ps, gt[grp], hr], [ob])
                P.dma("sp", hdst, hdst.t[d * 128:(d + 1) * 128, t0 + o:t0 + o + m], ob, ob[:, 0:m])

            self.linear(S, YY, KC, self.wo[l], [(d, "fm", d) for d in range(KC)], n, w32, w16, psb, epi_fm=epi)


Main.merge = _merge


def _natt(self, l, last):
    P, c = self.P, self.cfg
    GW, SEQ, CTX, T, ROWS = c.GW, c.SEQ, c.CTX, c.T, c.ROWS
    nloc = 8 * GW
    NK = nloc + CTX
    NKC = NK // 128
    NTC = T // 128
    scale = 128.0 ** -0.5
    with P.scope() as S:
        qT = [S.sbuf([128, T], BF16, f"qT{j}") for j in range(2)]
        kT = [S.sbuf([128, T], BF16, f"kT{j}") for j in range(2)]
        VA = [S.sbuf([128, NTC * 128], BF16, f"VA{j}") for j in range(2)]
        VB = [S.sbuf([128, NTC * 128], BF16, f"VB{j}") for j in range(2)]
        bias = [S.sbuf([GW, 15 * GW], F32, f"bias{j}") for j in range(2)]
        yT = [S.sbuf([128, T], BF16, f"yT{j}") for j in range(2)]
        sc = [S.sbuf([128, NK], F32, f"sc{j}") for j in range(2)]
        pn = [S.sbuf([128, NK], BF16, f"pn{j}") for j in range(2)]
        st = [S.sbuf([128, 4], F32, f"st{j}") for j in range(2)]
        pT = [S.sbuf([128, NKC * 128], BF16, f"pT{j}") for j in range(2)]
        ps_s = [S.psum([128, 512], F32, f"pss{j}") for j in range(2)]
        ps_c = [S.psum([128, 512], F32, f"psc{j}") for j in range(2)]
        ps_t = [S.psum([128, 1024], BF16, f"pst{j}") for j in range(2)]
        ps_o = [S.psum([128, 512], F32, f"pso{j}") for j in range(2)]
        it = 0
        for h in range(c.NAH):
            hb = h % 2
            q, k, va, vb, bs, y = qT[hb], kT[hb], VA[hb], VB[hb], bias[hb], yT[hb]
            P.dma("sp", q, q[:, :], self.z_na, self.z_na.t[h * 128:(h + 1) * 128, :])
            P.dma("sp", k, k[:, :], self.z_na, self.z_na.t[(c.NAH + h) * 128:(c.NAH + h + 1) * 128, :])
            P.dma("sp", va, va[:, :].rearrange("p (c d) -> p c d", d=128), self.v_na,
                  self.v_na.t[:, h * 128:(h + 1) * 128].rearrange("(c p) d -> p c d", p=128))
            P.dma("sp", vb, vb[:, 0:(NTC - 1) * 128].rearrange("p (c d) -> p c d", d=128), self.v_na,
                  self.v_na.t[64:T - 64, h * 128:(h + 1) * 128].rearrange("(c p) d -> p c d", p=128))
            P.dma("sp", bs, bs[:, :], self.rpbm, self.rpbm.t[l * c.NAH + h, :, :])

            def softmax_pv(rows_n, nk, s_i, key_chunks, ydst, it):
                s_, p_, t_, pt = sc[s_i], pn[s_i], st[s_i], pT[s_i]
                P.op("dve", lambda e: e.tensor_reduce(out=t_[0:rows_n, 0:1], in_=s_[0:rows_n, 0:nk], axis=AX.X, op=ALU.max),
                     [s_], [t_])
                P.op("dve", lambda e: e.tensor_scalar(out=t_[0:rows_n, 1:2], in0=t_[0:rows_n, 0:1], scalar1=-1.0, scalar2=None,
                                                      op0=ALU.mult), [t_], [t_])
                P.op("act", lambda e: e.activation(out=s_[0:rows_n, 0:nk], in_=s_[0:rows_n, 0:nk], func=AF.Exp,
                                                   bias=t_[0:rows_n, 1:2], scale=1.0, accum_out=t_[0:rows_n, 2:3]),
                     [s_, t_], [s_, t_])
                P.op("dve", lambda e: e.reciprocal(out=t_[0:rows_n, 3:4], in_=t_[0:rows_n, 2:3]), [t_], [t_])
                P.op("dve", lambda e: e.tensor_scalar(out=p_[0:rows_n, 0:nk], in0=s_[0:rows_n, 0:nk], scalar1=t_[0:rows_n, 3:4],
                                                      scalar2=None, op0=ALU.mult), [s_, t_], [p_])
                nkc = nk // 128
                pst = ps_t[it % 2]
                for j in range(nkc):
                    P.op("pe", lambda e, j=j: e.transpose(pst[:, j * 128:j * 128 + rows_n], p_[0:rows_n, j * 128:(j + 1) * 128],
                                                          self.ident_bf[0:rows_n, 0:rows_n]), [p_, self.ident_bf], [pst])
                P.op("act", lambda e: e.activation(
                    out=pt[:, 0:nkc * 128].rearrange("p (j c) -> p j c", c=128)[:, :, 0:rows_n],
                    in_=pst[:, 0:nkc * 128].rearrange("p (j c) -> p j c", c=128)[:, :, 0:rows_n], func=AF.Copy), [pst], [pt])
                po = ps_o[it % 2]
                for j, vch in enumerate(key_chunks):
                    P.op("pe", lambda e, j=j, vch=vch: e.matmul(po[:, 0:rows_n], lhsT=vch(), rhs=pt[:, j * 128:j * 128 + rows_n],
                                                                start=(j == 0), stop=(j == nkc - 1)), [va, vb, pt], [po])
                P.op("act", lambda e: e.activation(out=ydst(), in_=po[:, 0:rows_n], func=AF.Copy), [po], [y])

            for r in range(ROWS):
                r0 = min(max(r - 4, 0), ROWS - 8)
                kst = r0 * GW
                s_i = it % 2
                pss, psc, s_ = ps_s[it % 2], ps_c[it % 2], sc[s_i]
                P.op("pe", lambda e, r=r, kst=kst, pss=pss: e.matmul(pss[0:GW, 0:nloc], lhsT=q[:, r * GW:(r + 1) * GW],
                                                                     rhs=k[:, kst:kst + nloc], start=True, stop=True), [q, k], [pss])
                P.op("pe", lambda e, r=r, psc=psc: e.matmul(psc[0:GW, 0:CTX], lhsT=q[:, r * GW:(r + 1) * GW],
                                                            rhs=k[:, SEQ:T], start=True, stop=True), [q, k], [psc])
                bo = (r0 - r + 7) * GW
                P.op("dve", lambda e, pss=pss, s_=s_, bo=bo: e.scalar_tensor_tensor(
                    out=s_[0:GW, 0:nloc], in0=pss[0:GW, 0:nloc], scalar=scale, in1=bs[:, bo:bo + nloc],
                    op0=ALU.mult, op1=ALU.add), [pss, bs], [s_])
                P.op("act", lambda e, psc=psc, s_=s_: e.activation(out=s_[0:GW, nloc:NK], in_=psc[0:GW, 0:CTX],
                                                                   func=AF.Copy, scale=scale), [psc], [s_])
                kch = []
                for j in range(nloc // 128):
                    tk = kst + j * 128
                    if tk % 128 == 0:
                        kch.append(lambda tk=tk: va[:, tk:tk + 128])
                    else:
                        kch.append(lambda tk=tk: vb[:, tk - 64:tk + 64])
                for j in range(CTX // 128):
                    kch.append(lambda j=j: va[:, SEQ + j * 128:SEQ + (j + 1) * 128])
                softmax_pv(GW, NK, s_i, kch, lambda r=r: y[:, r * GW:(r + 1) * GW], it)
                it += 1
            if not last:
                for qt in range(CTX // 128):
                    s_i = it % 2
                    psc, s_ = ps_c[it % 2], sc[s_i]
                    P.op("pe", lambda e, qt=qt, psc=psc: e.matmul(psc[:, 0:CTX], lhsT=q[:, SEQ + qt * 128:SEQ + (qt + 1) * 128],
                                                                  rhs=k[:, SEQ:T], start=True, stop=True), [q, k], [psc])
                    P.op("act", lambda e, psc=psc, s_=s_: e.activation(out=s_[:, 0:CTX], in_=psc[:, 0:CTX], func=AF.Copy,
                                                                       scale=scale), [psc], [s_])
                    kch = [(lambda j=j: va[:, SEQ + j * 128:SEQ + (j + 1) * 128]) for j in range(CTX // 128)]
                    softmax_pv(128, CTX, s_i, kch, lambda qt=qt: y[:, SEQ + qt * 128:SEQ + (qt + 1) * 128], it)
                    it += 1
            else:
                P.op("pool", lambda e: e.memset(y[:, SEQ:T], 0.0), [], [y])
            P.dma("sp", self.ycT, self.ycT.t[h * 128:(h + 1) * 128, :], y, y[:, :])


Main.natt = _natt


def _final_norm(self, hsrc):
    P, c = self.P, self.cfg
    KC, D = c.KC, c.D
    with P.scope() as S:
        hst = [S.sbuf([128, KC * 128], F32, f"hst{j}") for j in range(2)]
        sqt = S.sbuf([128, KC * 128], BF16, "sqt")
        rstd_b = S.sbuf([128, 128], F32, "rstd")
        ob = [S.sbuf([128, KC * 128], F32, f"ob{j}") for j in range(2)]
        pss = [S.psum([128, 512], F32, f"fps{j}") for j in range(2)]
        for si, (o, m) in enumerate(tiles_of(c.SEQ, 128)):
            hs, ot, ps = hst[si % 2], ob[si % 2], pss[si % 2]
            P.dma("sp", hs, hs[:, 0:KC * m].rearrange("p (k t) -> p k t", k=KC), hsrc,
                  hsrc.t[:, o:o + m].rearrange("(k p) t -> p k t", p=128))
            P.op("pool", lambda e, hs=hs, m=m: e.tensor_tensor(out=sqt[:, 0:KC * m], in0=hs[:, 0:KC * m],
                                                               in1=hs[:, 0:KC * m], op=ALU.mult), [hs], [sqt])
            for kc in range(KC):
                P.op("pe", lambda e, ps=ps, kc=kc, m=m: e.matmul(ps[:, 0:m], lhsT=self.ones_bf[:, :],
                                                                 rhs=sqt[:, kc * m:(kc + 1) * m], start=(kc == 0),
                                                                 stop=(kc == KC - 1)), [self.ones_bf, sqt], [ps])
            P.op("act", lambda e, ps=ps, m=m: e.activation(out=rstd_b[:, 0:m], in_=ps[:, 0:m], func=AF.Sqrt,
                                                           bias=self.eps_ap(), scale=1.0 / D), [ps, self.csb], [rstd_b])
            P.op("dve", lambda e, m=m: e.reciprocal(out=rstd_b[:, 0:m], in_=rstd_b[:, 0:m]), [rstd_b], [rstd_b])
            for kc in range(KC):
                P.op("dve", lambda e, hs=hs, ot=ot, kc=kc, m=m: e.scalar_tensor_tensor(
                    out=ot[:, kc * m:(kc + 1) * m], in0=hs[:, kc * m:(kc + 1) * m], scalar=self.fnsb[:, kc:kc + 1],
                    in1=rstd_b[:, 0:m], op0=ALU.mult, op1=ALU.mult), [hs, self.fnsb, rstd_b], [ot])
            P.dma("sp", self.outT, self.outT.t[:, o:o + m].rearrange("(k p) t -> p k t", p=128),
                  ot, ot[:, 0:KC * m].rearrange("p (k t) -> p k t", k=KC), is_output=True)


Main.final_norm = _final_norm


def _dn_prep(self, l):
    P, c = self.P, self.cfg
    T, SEQ, H = c.T, c.SEQ, c.DNH
    NCK = c.NCK
    with P.scope() as S:
        cos = S.sbuf([128, SEQ], F32, "cos")
        sin = S.sbuf([128, SEQ], F32, "sin")
        P.dma("sp", cos, cos[:, :], self.ropec, self.ropec.t[:, :])
        P.dma("sp", sin, sin[:, :], self.ropes, self.ropes.t[:, :])
        cw = S.sbuf([128, 3 * H * 5], F32, "cw")
        P.dma("sp", cw, cw[:, :], self.convw, self.convw.t[:, l * 3 * H * 5:(l + 1) * 3 * H * 5])
        X = [S.sbuf([128, T], F32, f"X{j}") for j in range(2)]
        A = [S.sbuf([128, T], F32, f"A{j}") for j in range(2)]
        tm = [S.sbuf([64, NCK * 128], F32, f"tm{j}") for j in range(2)]
        t1 = [S.sbuf([128, 512], F32, f"t1{j}") for j in range(2)]
        t2 = [S.sbuf([128, 512], F32, f"t2{j}") for j in range(2)]
        psn = [S.psum([128, 512], F32, f"psn{j}") for j in range(2)]
        psr = [S.psum([128, 512], F32, f"psr{j}") for j in range(2)]
        pst = [S.psum([128, 512], F32, f"pst{j}") for j in range(2)]
        ones32 = self.csb[:, C_ONES:C_ONES + 128]
        perm = self.csb[:, C_PERM:C_PERM + 128]
        id32 = self.csb[:, C_ID:C_ID + 128]
        it = 0
        tcount = 0
        for h in range(H):
            for kind in range(3):
                ch = kind * H + h
                x, a = X[it % 2], A[it % 2]
                it += 1
                P.dma("sp", x, x[:, :], self.z_dn, self.z_dn.t[ch * 128:(ch + 1) * 128, :])
                wc = lambda j, ch=ch: cw[:, ch * 5 + j:ch * 5 + j + 1]
                for (r0, r1) in ((0, SEQ), (SEQ, T)):
                    P.op("act", lambda e: e.activation(out=a[:, r0:r1], in_=x[:, r0:r1], func=AF.Copy, scale=wc(2)),
                         [x, cw], [a])
                    for j, (d0, d1, s0, s1) in ((0, (r0 + 2, r1, r0, r1 - 2)), (1, (r0 + 1, r1, r0, r1 - 1)),
                                                (3, (r0, r1 - 1, r0 + 1, r1)), (4, (r0, r1 - 2, r0 + 2, r1))):
                        P.op("dve", lambda e: e.scalar_tensor_tensor(out=a[:, d0:d1], in0=x[:, s0:s1], scalar=wc(j),
                                                                     in1=a[:, d0:d1], op0=ALU.mult, op1=ALU.add),
                             [x, cw, a], [a])
                P.op("act", lambda e: e.activation(out=a[:, :], in_=a[:, :], func=AF.Silu), [a], [a])
                if kind != 1:
                    for (o, m) in tiles_of(T, 512):
                        k2 = tcount % 2
                        tcount += 1
                        ta, tb, pn_, pr_ = t1[k2], t2[k2], psn[k2], psr[k2]
                        P.op("pool", lambda e: e.tensor_tensor(out=ta[:, 0:m], in0=a[:, o:o + m], in1=a[:, o:o + m],
                                                               op=ALU.mult), [a], [ta])
                        P.op("pe", lambda e: e.matmul(pn_[:, 0:m], lhsT=ones32, rhs=ta[:, 0:m], start=True, stop=True),
                             [self.csb, ta], [pn_])
                        P.op("act", lambda e: e.activation(out=tb[:, 0:m], in_=pn_[:, 0:m], func=AF.Sqrt,
                                                           bias=self.eps_ap(), scale=1.0), [pn_, self.csb], [tb])
                        P.op("dve", lambda e: e.reciprocal(out=tb[:, 0:m], in_=tb[:, 0:m]), [tb], [tb])
                        P.op("dve", lambda e: e.tensor_tensor(out=a[:, o:o + m], in0=a[:, o:o + m], in1=tb[:, 0:m],
                                                              op=ALU.mult), [a, tb], [a])
                        if o < SEQ:
                            mm_ = min(m, SEQ - o)
                            P.op("pe", lambda e: e.matmul(pr_[:, 0:mm_], lhsT=perm, rhs=a[:, o:o + mm_], start=True, stop=True),
                                 [self.csb, a], [pr_])
                            P.op("dve", lambda e: e.tensor_tensor(out=ta[:, 0:mm_], in0=a[:, o:o + mm_], in1=cos[:, o:o + mm_],
                                                                  op=ALU.mult), [a, cos], [ta])
                            P.op("dve", lambda e: e.tensor_tensor(out=tb[:, 0:mm_], in0=pr_[:, 0:mm_], in1=sin[:, o:o + mm_],
                                                                  op=ALU.mult), [pr_, sin], [tb])
                            P.op("pool", lambda e: e.tensor_tensor(out=a[:, o:o + mm_], in0=ta[:, 0:mm_], in1=tb[:, 0:mm_],
                                                                   op=ALU.add), [ta, tb], [a])
                if kind == 0:
                    P.dma("sp", self.dn_kT, self.dn_kT.t[h * 128:(h + 1) * 128, :], a, a[:, :])
                if kind == 2:
                    P.dma("sp", self.dn_qT, self.dn_qT.t[h * 128:(h + 1) * 128, :], a, a[:, :])
                if kind in (0, 1):
                    tmb = tm[kind]
                    for g0 in range(0, NCK, 4):
                        gn = min(4, NCK - g0)
                        pt = pst[(g0 // 4) % 2]
                        for j in range(gn):
                            ck = g0 + j
                            P.op("pe", lambda e: e.transpose(pt[0:64, j * 128:(j + 1) * 128], a[:, ck * 64:(ck + 1) * 64], id32),
                                 [a, self.csb], [pt])
                        P.op("act", lambda e: e.activation(out=tmb[:, g0 * 128:(g0 + gn) * 128], in_=pt[0:64, 0:gn * 128],
                                                           func=AF.Copy), [pt], [tmb])
                    dst = self.dn_ktm if kind == 0 else self.dn_vtm
                    P.dma("sp", dst, dst.t[:, h * 128:(h + 1) * 128].rearrange("(c p) d -> p c d", p=64),
                          tmb, tmb[:, :].rearrange("p (c d) -> p c d", d=128))


Main.dn_prep = _dn_prep


def _dn_scan(self, l):
    P, c = self.P, self.cfg
    T, SEQ, H, NCK = c.T, c.SEQ, c.DNH, c.NCK
    H2 = 2 * H
    sq = 128.0 ** -0.5
    lns = float(np.log(sq))
    csb = self.csb
    ones64 = csb[0:64, C_ONES:C_ONES + 64]
    ones64w = csb[0:64, C_ONES:C_ONES + 128]
    neg64 = csb[0:64, C_NEG1:C_NEG1 + 64]
    id64 = csb[0:64, C_ID:C_ID + 64]
    CUM = [csb[0:64, C_CUMF:C_CUMF + 64], csb[0:64, C_CUMB:C_CUMB + 64]]
    MS = [csb[0:64, C_MSF:C_MSF + 64], csb[0:64, C_MSB:C_MSB + 64]]
    MIT = [csb[0:64, C_MITF:C_MITF + 64], csb[0:64, C_MITB:C_MITB + 64]]
    nlat = SEQ // 64
    lat = list(range(nlat))
    ctx = list(range(nlat, NCK))
    order = [ctx + lat, ctx[::-1] + lat[::-1]]
    with P.scope() as S:
        gall = S.sbuf([64, NCK * H2], F32, "gall")
        ball = S.sbuf([64, NCK * H2], F32, "ball")
        with P.scope() as S2:
            ab = S2.sbuf([64, NCK * 2 * H2], F32, "ab")
            ab3 = ab[:, :].rearrange("p (c n) -> p c n", n=2 * H2)
            P.dma("sp", ab, ab3, self.z_ab, self.z_ab.t[:, 0:2 * H2].rearrange("(c p) n -> p c n", p=64))
            dtb = S2.sbuf([64, NCK * H2], F32, "dtb")
            nal = S2.sbuf([64, NCK * H2], F32, "nal")
            P.dma("sp", dtb, dtb[:, :], self.dnc, self.dnc.t[:, (l * 2) * NCK * H2:(l * 2 + 1) * NCK * H2])
            P.dma("sp", nal, nal[:, :], self.dnc, self.dnc.t[:, (l * 2 + 1) * NCK * H2:(l * 2 + 2) * NCK * H2])
            g3 = gall[:, :].rearrange("p (c n) -> p c n", n=H2)
            b3 = ball[:, :].rearrange("p (c n) -> p c n", n=H2)
            d3 = dtb[:, :].rearrange("p (c n) -> p c n", n=H2)
            P.op("dve", lambda e: e.tensor_tensor(out=g3, in0=ab3[:, :, 0:H2], in1=d3, op=ALU.add), [ab, dtb], [gall])
            P.op("act", lambda e: e.activation(out=gall[:, :], in_=gall[:, :], func=AF.Exp), [gall], [gall])
            P.op("act", lambda e: e.activation(out=gall[:, :], in_=gall[:, :], func=AF.Ln, bias=self.one_ap(64), scale=1.0),
                 [gall, csb], [gall])
            P.op("act", lambda e: e.activation(out=nal[:, :], in_=nal[:, :], func=AF.Exp), [nal], [nal])
            P.op("dve", lambda e: e.scalar_tensor_tensor(out=gall[:, :], in0=gall[:, :], scalar=-1.0, in1=nal[:, :],
                                                         op0=ALU.mult, op1=ALU.mult), [gall, nal], [gall])
            P.op("act", lambda e: e.activation(out=b3, in_=ab3[:, :, H2:2 * H2], func=AF.Sigmoid), [ab], [ball])
        nwb = S.sbuf([64, 128], F32, "nwb")
        P.dma("sp", nwb, nwb[:, :], self.dnnw, self.dnnw.t[:, l * 128:(l + 1) * 128])
        kT = S.sbuf([128, T], F32, "kT")
        qT = S.sbuf([128, T], F32, "qT")
        ktm = S.sbuf([64, NCK * 128], F32, "ktm")
        vtm = S.sbuf([64, NCK * 128], F32, "vtm")
        O = S.sbuf([64, NCK * 128], F32, "O")
        ysb = S.sbuf([128, T], BF16, "ysb")
        ms = S.sbuf([64, NCK], F32, "ms")
        junk2 = [S.sbuf([64, 128], F32, f"junk{j}") for j in range(2)]
        Sst = [S.sbuf([128, 128], F32, f"S{d}") for d in range(2)]
        banks = [S.psum([128, 512], F32, f"bk{j}") for j in range(8)]

        def view(bank, rows, c0, c1, name):
            return _alias(banks[bank], banks[bank][0:rows, c0:c1])

        def mk(d, j):
            t = {}
            for nm, shp in (("gcum", [64, 64]), ("decS", [64, 64]), ("decIT", [64, 64]), ("N0", [64, 64]), ("qkT", [64, 64]),
                            ("sm", [128, 8]), ("Y0", [64, 256]), ("Y1", [64, 256]), ("kd", [64, 128]),
                            ("PA", [64, 64]), ("PB", [64, 64]), ("TA", [64, 64]), ("TB", [64, 64]), ("wT", [128, 64])):
                t[nm] = S.sbuf(shp, F32, f"{nm}{d}{j}")
            return t

        tmp = [[mk(d, j) for j in range(2)] for d in range(2)]
        vn = [S.sbuf([64, 128], F32, f"vn{d}") for d in range(2)]
        t3 = [S.sbuf([64, 128], F32, f"t3{d}") for d in range(2)]
        ot = [S.sbuf([64, 128], F32, f"ot{d}") for d in range(2)]
        pv = []
        for d in range(2):
            b0 = d * 3
            pv.append({
                "diff": view(b0, 64, 0, 64, f"diff{d}"), "KK": view(b0, 64, 64, 128, f"KK{d}"),
                "QK": view(b0, 64, 128, 192, f"QK{d}"), "tp0": view(b0, 64, 192, 256, f"tp0{d}"),
                "tp1": view(b0, 64, 256, 320, f"tp1{d}"), "wTp": view(b0, 128, 320, 384, f"wTp{d}"),
                "gc": view(b0, 64, 384, 386, f"gc{d}"), "gt": view(b0, 128, 386, 388, f"gt{d}"),
                "ap0": view(b0 + 1, 64, 0, 256, f"ap0{d}"), "ap1": view(b0 + 1, 64, 256, 512, f"ap1{d}"),
                "ps1": view(b0 + 2, 64, 0, 128, f"ps1{d}"), "ps2": view(b0 + 2, 64, 128, 256, f"ps2{d}"),
                "ps3": view(b0 + 2, 64, 256, 384, f"ps3{d}"), "ps4": view(b0 + 2, 128, 384, 512, f"ps4{d}"),
            })
        ytp = [banks[6], banks[7]]

        PRE_N = int(os.environ.get("PRE_N", "1000"))
        pcount = [0]

        def pop(*a, **kw):
            pcount[0] += 1
            if pcount[0] <= PRE_N:
                return P.op(*a, **kw)

        def pre(h, d, ck, t):
            pcount[0] = 0
            p = pv[d]
            gi = ck * H2 + d * H + h
            gcol = gall[:, gi:gi + 1]
            bcol = ball[:, gi:gi + 1]
            kc_ = kT[:, ck * 64:(ck + 1) * 64]
            qc_ = qT[:, ck * 64:(ck + 1) * 64]
            pop("dve", lambda e: e.tensor_scalar(out=t["gcum"][:, :], in0=CUM[d], scalar1=gcol, scalar2=None, op0=ALU.mult),
                 [csb, gall], [t["gcum"]])
            pop("pe", lambda e: e.matmul(p["diff"][:, :], lhsT=t["gcum"][:, :], rhs=ones64, start=True, stop=False),
                 [t["gcum"], csb], [p["diff"]])
            pop("pe", lambda e: e.matmul(p["diff"][:, :], lhsT=neg64, rhs=t["gcum"][:, :], start=False, stop=True),
                 [t["gcum"], csb], [p["diff"]])
            pop("pe", lambda e: e.matmul(p["gc"][:, 0:1], lhsT=CUM[d], rhs=gcol, start=True, stop=True), [csb, gall], [p["gc"]])
            pop("pe", lambda e: e.matmul(p["gt"][:, 0:1], lhsT=ones64w, rhs=gcol, start=True, stop=True), [csb, gall], [p["gt"]])
            pop("dve", lambda e: e.tensor_tensor(out=t["decS"][:, :], in0=p["diff"][:, :], in1=MS[d], op=ALU.add),
                 [p["diff"], csb], [t["decS"]])
            pop("act", lambda e: e.activation(out=t["decS"][:, :], in_=t["decS"][:, :], func=AF.Exp), [t["decS"]], [t["decS"]])
            pop("dve", lambda e: e.scalar_tensor_tensor(out=t["decIT"][:, :], in0=p["diff"][:, :], scalar=-1.0, in1=MIT[d],
                                                         op0=ALU.mult, op1=ALU.add), [p["diff"], csb], [t["decIT"]])
            pop("act", lambda e: e.activation(out=t["decIT"][:, :], in_=t["decIT"][:, :], func=AF.Exp), [t["decIT"]], [t["decIT"]])
            pop("pe", lambda e: e.matmul(p["KK"][:, :], lhsT=kc_, rhs=kc_, start=True, stop=True), [kT], [p["KK"]])
            pop("pe", lambda e: e.matmul(p["QK"][:, :], lhsT=kc_, rhs=qc_, start=True, stop=True), [kT, qT], [p["QK"]])
            pop("dve", lambda e: e.scalar_tensor_tensor(out=t["N0"][:, :], in0=p["KK"][:, :], scalar=bcol, in1=t["decS"][:, :],
                                                         op0=ALU.mult, op1=ALU.mult), [p["KK"], ball, t["decS"]], [t["N0"]])
            pop("dve", lambda e: e.tensor_tensor(out=t["qkT"][:, :], in0=p["QK"][:, :], in1=t["decIT"][:, :], op=ALU.mult),
                 [p["QK"], t["decIT"]], [t["qkT"]])
            sm = t["sm"]
            pop("act", lambda e: e.activation(out=sm[:, 0:1], in_=p["gt"][:, 0:1], func=AF.Copy), [p["gt"]], [sm])
            pop("act", lambda e: e.activation(out=sm[0:64, 5:6], in_=p["gc"][:, 0:1], func=AF.Copy), [p["gc"]], [sm])
            pop("act", lambda e: e.activation(out=sm[:, 1:2], in_=sm[:, 0:1], func=AF.Exp), [sm], [sm])
            pop("act", lambda e: e.activation(out=sm[0:64, 2:3], in_=sm[0:64, 5:6], func=AF.Exp), [sm], [sm])
            pop("act", lambda e: e.activation(out=sm[0:64, 3:4], in_=sm[0:64, 5:6], func=AF.Exp, bias=sm[0:64, 0:1],
                                              scale=-1.0), [sm], [sm])
            pop("act", lambda e: e.activation(out=sm[0:64, 4:5], in_=sm[0:64, 5:6], func=AF.Exp, bias=self.lns_ap(64),
                                              scale=1.0), [sm, csb], [sm])
            Y = t["Y0"]
            pop("dve", lambda e: e.tensor_scalar(out=Y[:, 0:128], in0=vtm[:, ck * 128:(ck + 1) * 128], scalar1=bcol,
                                                  scalar2=None, op0=ALU.mult), [vtm, ball], [Y])
            pop("dve", lambda e: e.tensor_scalar(out=Y[:, 128:256], in0=ktm[:, ck * 128:(ck + 1) * 128], scalar1=bcol,
                                                  scalar2=sm[0:64, 2:3], op0=ALU.mult, op1=ALU.mult), [ktm, ball, sm], [Y])
            pop("act", lambda e: e.activation(out=t["kd"][:, :], in_=ktm[:, ck * 128:(ck + 1) * 128], func=AF.Copy,
                                               scale=sm[0:64, 3:4]), [ktm, sm], [t["kd"]])
            Pc, PTc = t["N0"], t["TA"]
            pop("pe", lambda e: e.transpose(p["tp0"][:, :], t["N0"][:, :], id64), [t["N0"], csb], [p["tp0"]])
            pop("act", lambda e: e.activation(out=PTc[:, :], in_=p["tp0"][:, :], func=AF.Copy), [p["tp0"]], [PTc])
            Ys = [t["Y0"], t["Y1"]]
            yi = 0
            pop("pe", lambda e: e.matmul(p["ap0"][:, :], lhsT=PTc[:, :], rhs=Ys[0][:, :], start=True, stop=True),
                 [PTc, Ys[0]], [p["ap0"]])
            pop("dve", lambda e: e.tensor_tensor(out=Ys[1][:, :], in0=Ys[0][:, :], in1=p["ap0"][:, :], op=ALU.subtract),
                 [Ys[0], p["ap0"]], [Ys[1]])
            yi = 1
            pbufs = [(t["PA"], t["TB"]), (t["PB"], t["TA"])]
            for k in range(5):
                Pn, PTn = pbufs[k % 2]
                if k % 2 == 1:
                    PTn = t["TA"]
                pop("pe", lambda e: e.matmul(p["tp0"][:, :], lhsT=PTc[:, :], rhs=Pc[:, :], start=True, stop=True),
                     [PTc, Pc], [p["tp0"]])
                pop("pe", lambda e: e.matmul(p["tp1"][:, :], lhsT=Pc[:, :], rhs=PTc[:, :], start=True, stop=True),
                     [PTc, Pc], [p["tp1"]])
                Pn = t["PA"] if Pc is not t["PA"] else t["PB"]
                PTn = t["TB"] if PTc is not t["TB"] else t["TA"]
                pop("act", lambda e: e.activation(out=Pn[:, :], in_=p["tp0"][:, :], func=AF.Copy), [p["tp0"]], [Pn])
                pop("act", lambda e: e.activation(out=PTn[:, :], in_=p["tp1"][:, :], func=AF.Copy), [p["tp1"]], [PTn])
                Pc, PTc = Pn, PTn
                apx = p["ap1"] if k % 2 == 0 else p["ap0"]
                Yc, Yn = Ys[yi], Ys[1 - yi]
                pop("pe", lambda e: e.matmul(apx[:, :], lhsT=PTc[:, :], rhs=Yc[:, :], start=True, stop=True), [PTc, Yc], [apx])
                pop("dve", lambda e: e.tensor_tensor(out=Yn[:, :], in0=Yc[:, :], in1=apx[:, :], op=ALU.add), [Yc, apx], [Yn])
                yi = 1 - yi
            Yf = Ys[yi]
            pop("pe", lambda e: e.transpose(p["wTp"][:, :], Yf[:, 128:256], id64), [Yf, csb], [p["wTp"]])
            pop("act", lambda e: e.activation(out=t["wT"][:, :], in_=p["wTp"][:, :], func=AF.Copy), [p["wTp"]], [t["wT"]])
            return Yf

        SEQ_N = int(os.environ.get("SEQ_N", "1000"))
        scount = [0]

        def sop(*a, **kw):
            scount[0] += 1
            if scount[0] <= SEQ_N:
                return P.op(*a, **kw)

        def seq(h, d, ck, t, Yf):
            scount[0] = 0
            p = pv[d]
            Sd = Sst[d]
            sm = t["sm"]
            qc_ = qT[:, ck * 64:(ck + 1) * 64]
            sop("pe", lambda e: e.matmul(p["ps1"][:, :], lhsT=t["wT"][:, :], rhs=Sd[:, :], start=True, stop=True),
                 [t["wT"], Sd], [p["ps1"]])
            sop("dve", lambda e: e.tensor_tensor(out=vn[d][:, :], in0=Yf[:, 0:128], in1=p["ps1"][:, :], op=ALU.subtract),
                 [Yf, p["ps1"]], [vn[d]])
            sop("pe", lambda e: e.matmul(p["ps2"][:, :], lhsT=qc_, rhs=Sd[:, :], start=True, stop=True), [qT, Sd], [p["ps2"]])
            sop("pe", lambda e: e.matmul(p["ps3"][:, :], lhsT=t["qkT"][:, :], rhs=vn[d][:, :], start=True, stop=True),
                 [t["qkT"], vn[d]], [p["ps3"]])
            sop("pe", lambda e: e.matmul(p["ps4"][:, :], lhsT=t["kd"][:, :], rhs=vn[d][:, :], start=True, stop=True),
                 [t["kd"], vn[d]], [p["ps4"]])
            sop("dve", lambda e: e.scalar_tensor_tensor(out=Sd[:, :], in0=Sd[:, :], scalar=sm[:, 1:2], in1=p["ps4"][:, :],
                                                         op0=ALU.mult, op1=ALU.add), [Sd, sm, p["ps4"]], [Sd])
            sop("dve", lambda e: e.tensor_scalar(out=t3[d][:, :], in0=p["ps3"][:, :], scalar1=sq, scalar2=None, op0=ALU.mult),
                [p["ps3"]], [t3[d]])
            sop("dve", lambda e: e.scalar_tensor_tensor(out=ot[d][:, :], in0=p["ps2"][:, :], scalar=sm[0:64, 4:5], in1=t3[d][:, :],
                                                         op0=ALU.mult, op1=ALU.add), [p["ps2"], sm, t3[d]], [ot[d]])
            sop("pool", lambda e: e.tensor_tensor(out=O[:, ck * 128:(ck + 1) * 128], in0=O[:, ck * 128:(ck + 1) * 128],
                                                   in1=ot[d][:, :], op=ALU.add), [O, ot[d]], [O])

        dz = [S.sbuf([64, 128], F32, f"dz{j}") for j in range(2)]
        yb = [S.sbuf([64, 128], F32, f"yb{j}") for j in range(2)]
        DBG = float(os.environ.get("DN_DBG", "9"))
        for h in range(H if DBG > 1 else 0):
            P.dma("sp", kT, kT[:, :], self.dn_kT, self.dn_kT.t[h * 128:(h + 1) * 128, :])
            P.dma("sp", qT, qT[:, :], self.dn_qT, self.dn_qT.t[h * 128:(h + 1) * 128, :])
            P.dma("sp", ktm, ktm[:, :].rearrange("p (c d) -> p c d", d=128), self.dn_ktm,
                  self.dn_ktm.t[:, h * 128:(h + 1) * 128].rearrange("(c p) d -> p c d", p=64))
            P.dma("sp", vtm, vtm[:, :].rearrange("p (c d) -> p c d", d=128), self.dn_vtm,
                  self.dn_vtm.t[:, h * 128:(h + 1) * 128].rearrange("(c p) d -> p c d", p=64))
            P.op("pool", lambda e: e.memset(O[:, :], 0.0), [], [O])
            for d in range(2):
                P.op("pool", lambda e: e.memset(Sst[d][:, :], 0.0), [], [Sst[d]])
            Yfs = [[None, None], [None, None]]
            for d in range(2 if DBG > 1.7 else 0):
                Yfs[d][0] = pre(h, d, order[d][0], tmp[d][0])
            for step in range(min(NCK, int(os.environ.get('NSTEP', '1000'))) if DBG > 2 else 0):
                j = step % 2
                if step + 1 < NCK:
                    for d in range(2):
                        Yfs[d][1 - j] = pre(h, d, order[d][step + 1], tmp[d][1 - j])
                for d in range(2):
                    seq(h, d, order[d][step], tmp[d][j], Yfs[d][j])
            if self.dbg:
                P.dma("sp", self.dbgO, self.dbgO.t[:, :], O, O[:, :])
                P.dma("sp", self.dbgS, self.dbgS.t[:, 0:128], Sst[0], Sst[0][:, :])
                P.dma("sp", self.dbgS, self.dbgS.t[:, 128:256], Sst[1], Sst[1][:, :])
            if DBG < 4:
                continue
            for ck in range(NCK):
                jk = junk2[ck % 2]
                P.op("act", lambda e: e.activation(out=jk[:, :], in_=O[:, ck * 128:(ck + 1) * 128], func=AF.Square), [O], [jk])
                P.op("dve", lambda e: e.tensor_reduce(out=ms[:, ck:ck + 1], in_=jk[:, :], axis=AX.X, op=ALU.add), [jk], [ms])
            P.op("act", lambda e: e.activation(out=ms[:, :], in_=ms[:, :], func=AF.Sqrt, bias=csb[0:64, C_EPS:C_EPS + 1],
                                               scale=1.0 / 128), [ms, csb], [ms])
            P.op("dve", lambda e: e.reciprocal(out=ms[:, :], in_=ms[:, :]), [ms], [ms])
            for g0 in range(0, NCK if DBG > 5 else 0, 8):
                gn = min(8, NCK - g0)
                pt = ytp[(g0 // 8) % 2]
                for jj in range(gn):
                    ck = g0 + jj
                    dzb, ybb = dz[ck % 2], yb[ck % 2]
                    P.dma("sp", dzb, dzb[:, :], self.z_dg, self.z_dg.t[ck * 64:(ck + 1) * 64, h * 128:(h + 1) * 128])
                    P.op("dve", lambda e: e.scalar_tensor_tensor(out=ybb[:, :], in0=O[:, ck * 128:(ck + 1) * 128],
                                                                 scalar=ms[:, ck:ck + 1], in1=nwb[:, :], op0=ALU.mult,
                                                                 op1=ALU.mult), [O, ms, nwb], [ybb])
                    P.op("pool", lambda e: e.tensor_tensor(out=ybb[:, :], in0=ybb[:, :], in1=dzb[:, :], op=ALU.mult),
                         [ybb, dzb], [ybb])
                    if DBG > 6:
                        P.op("pe", lambda e: e.transpose(pt[:, jj * 64:(jj + 1) * 64], ybb[:, :], id64), [ybb, csb], [pt])
                if DBG > 7:
                    P.op("act", lambda e: e.activation(out=ysb[:, g0 * 64:(g0 + gn) * 64], in_=pt[:, 0:gn * 64], func=AF.Copy),
                         [pt], [ysb])
            if DBG > 7:
                P.dma("sp", self.ybT, self.ybT.t[h * 128:(h + 1) * 128, :], ysb, ysb[:, :])


Main.dn_scan = _dn_scan


class _V:
    def __init__(self, buf, rows, c0, c1):
        self.ap = buf[0:rows, c0:c1]

    def __getitem__(self, idx):
        return self.ap[idx]


C_ONE1 = C_ONES


def _one_ap(self, rows):
    return self.csb[0:rows, C_ONES:C_ONES + 1]


def _lns_ap(self, rows):
    return self.csb[0:rows, C_EPS + 1:C_EPS + 2]


Main.one_ap = _one_ap
Main.lns_ap = _lns_ap


def _mod_phase(self):
    P, c = self.P, self.cfg
    KC = c.KC
    NQ = 9 * KC
    with P.scope() as S:
        craw = S.sbuf([128, KC * 2], F32, "craw")
        cs = S.sbuf([128, KC * 2], F32, "cs")
        P.dma("sp", craw, craw[:, :], self.cT, self.cT.t[:, :])
        P.op("act", lambda e: e.activation(out=cs[:, :], in_=craw[:, :], func=AF.Silu), [craw], [cs])
        bsb = S.sbuf([128, c.L * NQ], F32, "adab")
        P.dma("sp", bsb, bsb[:, :], self.adabT, self.adabT.t[:, :])
        wt = [S.sbuf([128, KC * 128], F32, f"aw{j}") for j in range(3)]
        ps = [S.psum([128, 512], F32, f"mps{j}") for j in range(4)]
        it = 0
        for l in range(self.nl):
            mv = self.modsb[:, l * 2 * NQ:(l + 1) * 2 * NQ].rearrange("p (g q) -> p g q", g=2)
            for q in range(NQ):
                w = wt[it % 3]
                p_ = ps[it % 4]
                it += 1
                P.dma("sp", w, w[:, :], self.adaw[l], self.adaw[l].t[q, :, :])
                for kc in range(KC):
                    P.op("pe", lambda e: e.matmul(p_[:, 0:2], lhsT=w[:, kc * 128:(kc + 1) * 128], rhs=cs[:, kc * 2:(kc + 1) * 2],
                                                  start=(kc == 0), stop=(kc == KC - 1)), [w, cs], [p_])
                P.op("dve", lambda e: e.tensor_scalar(out=mv[:, :, q], in0=p_[:, 0:2], scalar1=bsb[:, l * NQ + q:l * NQ + q + 1],
                                                      scalar2=None, op0=ALU.add), [p_, bsb], [self.modsb])


Main.mod_phase = _mod_phase


def _build(self):
    P, c = self.P, self.cfg
    L, KC, FC, T = c.L, c.KC, c.FC, c.T
    with self.stack:
        self.xT = self.din("xT", [c.D, T])
        self.cT = self.din("cT", [128, KC * 2])
        self.adaw = [self.din(f"adaw_{l}", [9 * KC, 128, KC * 128]) for l in range(L)]
        self.adabT = self.din("adabT", [128, L * 9 * KC])
        self.normT = self.din("normT", [128, L * 3 * KC])
        self.fnT = self.din("fnT", [128, KC])
        self.w13 = [[self.din(f"w13_{l}_{i}", [2 * FC, 128, KC * 128]) for i in range(2)] for l in range(L)]
        self.w2 = [[self.din(f"w2_{l}_{i}", [KC, 128, FC * 128]) for i in range(2)] for l in range(L)]
        self.win = [self.din(f"win_{l}", [c.NCH, 128, KC * 128]) for l in range(L)]
        self.pa = [self.din(f"pa_{l}", [KC, 128, c.GMW]) for l in range(L)]
        self.pb = [self.din(f"pb_{l}", [KC, 128, c.DNW]) for l in range(L)]
        self.pc = [self.din(f"pc_{l}", [KC, 128, c.NAW]) for l in range(L)]
        self.wo = [self.din(f"wo_{l}", [KC, 128, KC * 128]) for l in range(L)]
        self.sguT = self.din("sguT", [128, L * c.GMG * 128])
        self.sgub = self.din("sgub", [128, L * c.GMG * 128])
        self.sgunw = self.din("sgunw", [128, L * c.GMW])
        self.rpbm = self.din("rpbm", [L * c.NAH, c.GW, 15 * c.GW])
        self.ropec = self.din("ropec", [128, c.SEQ])
        self.ropes = self.din("ropes", [128, c.SEQ])
        self.convw = self.din("convw", [128, L * 3 * c.DNH * 5])
        self.dnc = self.din("dnc", [64, L * 2 * c.NCK * 2 * c.DNH])
        self.dnnw = self.din("dnnw", [64, L * 128])
        self.hs = [self.dscr(f"h{j}", [c.D, T]) for j in range(3 * L)]
        self.gT = self.dscr("gT", [c.FF, T], BF16)
        self.z_dn = self.dscr("z_dn", [3 * c.DNW, T])
        self.z_na = self.dscr("z_na", [2 * c.NAW, T], BF16)
        self.z_gu = self.dscr("z_gu", [c.GMW, T])
        self.z_g = self.dscr("z_g", [3 * c.D, T])
        self.v_na = self.dscr("v_na", [T, c.NAW], BF16)
        self.z_dg = self.dscr("z_dg", [T, c.DNW])
        self.z_gv = self.dscr("z_gv", [T, c.GMW])
        self.z_ab = self.dscr("z_ab", [T, 128])
        self.yaT = self.dscr("yaT", [c.GMW, T], BF16)
        self.ybT = self.dscr("ybT", [c.DNW, T], BF16)
        self.ycT = self.dscr("ycT", [c.NAW, T], BF16)
        self.dn_kT = self.dscr("dn_kT", [c.DNW, T])
        self.dn_qT = self.dscr("dn_qT", [c.DNW, T])
        self.dn_ktm = self.dscr("dn_ktm", [T, c.DNW])
        self.dn_vtm = self.dscr("dn_vtm", [T, c.DNW])
        if self.dbg:
            self.dbgO = self.dscr("dbgO", [64, c.NCK * 128])
            self.dbgS = self.dscr("dbgS", [128, 256])
        ap = self.nc.dram_tensor("outT", [c.D, c.SEQ], F32, kind="ExternalOutput").ap()
        self.outT = Buf(P, ap, "outT")
        self.load_consts()
        self.modsb = P.sbuf([128, L * 2 * 9 * KC], F32, "modsb")
        self.mod_phase()
        self.normsb = P.sbuf([128, L * 3 * KC], F32, "normsb")
        P.dma("sp", self.normsb, self.normsb[:, :], self.normT, self.normT.t[:, :])
        self.fnsb = P.sbuf([128, KC], F32, "fnsb")
        P.dma("sp", self.fnsb, self.fnsb[:, :], self.fnT, self.fnT.t[:, :])
        h = self.xT
        stop = self.stop_after
        done = False
        for l in range(self.nl):
            last = (l == c.L - 1)
            self.ffn(l, 0, h, self.hs[3 * l]); h = self.hs[3 * l]
            if stop == (l, "ffn0"): break
            self.inproj(l, h)
            if stop == (l, "inproj"): break
            self.gmlp(l)
            if stop == (l, "gmlp"): break
            self.natt(l, last)
            if stop == (l, "natt"): break
            self.dn_prep(l)
            if stop == (l, "dnprep"): break
            self.dn_scan(l)
            if stop == (l, "dnscan"): break
            self.merge(l, h, self.hs[3 * l + 1]); h = self.hs[3 * l + 1]
            if stop == (l, "merge"): break
            self.ffn(l, 1, h, self.hs[3 * l + 2]); h = self.hs[3 * l + 2]
        self.final_norm(h)
        P.finish("sp")
        P.emit()
    return self.nc


Main.build = _build


def tile_w(W, ncols_chunk=128):
    K, N = W.shape
    KCn, G = K // 128, N // 128
    return np.ascontiguousarray(W.reshape(KCn, 128, G, 128).transpose(2, 1, 0, 3)).reshape(G, 128, KCn * 128)


def vecT(v):
    sh = v.shape
    KCn = sh[-1] // 128
    x = v.reshape(*sh[:-1], KCn, 128)
    x = np.moveaxis(x, -1, 0)
    return np.ascontiguousarray(x).reshape(128, -1)


def rope_tables(c):
    nf = 32
    freqs = (10000.0 ** (-np.arange(nf, dtype=np.float32) / nf)).astype(np.float32)
    t = np.arange(c.SEQ)
    rows = (t // c.GW).astype(np.float32)
    cols = (t % c.GW).astype(np.float32)
    cosT = np.zeros((128, c.SEQ), np.float32)
    sinT = np.zeros((128, c.SEQ), np.float32)
    for p in range(128):
        pos = rows if p < 64 else cols
        ang = pos * freqs[p % 32]
        cosT[p] = np.cos(ang)
        sinT[p] = np.sin(ang) * (-1.0 if (p % 64) < 32 else 1.0)
    return cosT, sinT


def in_col_order(c):
    DNW, NAW, GMW, D, H = c.DNW, c.NAW, c.GMW, c.D, c.DNH
    sizes = [2 * DNW, 2 * H, 2 * H, NAW, NAW, DNW, NAW, DNW, GMW, GMW, D, D, D]
    offs = np.concatenate([[0], np.cumsum(sizes)])
    o = {n: offs[i] for i, n in enumerate(["kv", "al", "be", "nak", "nav", "dnq", "naq", "dng", "gmu", "gmv", "ga", "gb", "gc"])}
    r = lambda a, n: list(range(a, a + n))
    cols = (r(o["kv"], DNW) + r(o["kv"] + DNW, DNW) + r(o["dnq"], DNW) + r(o["naq"], NAW) + r(o["nak"], NAW)
            + r(o["gmu"], GMW) + r(o["ga"], D) + r(o["gb"], D) + r(o["gc"], D)
            + r(o["nav"], NAW) + r(o["dng"], DNW) + r(o["gmv"], GMW) + r(o["al"], 2 * H) + r(o["be"], 2 * H))
    return np.array(cols), int(offs[-1])


def prep_core(c, b, inp, adaw_tiled):
    L = c.L
    f32 = np.float32
    m = {}
    m["consts"] = make_consts()
    m["xT"] = np.ascontiguousarray(np.concatenate([inp["x"][b], inp["ctx"][b]], axis=0).T)
    cc = np.stack([inp["c"][b], inp["c_ctx"]], axis=0)
    m["cT"] = np.ascontiguousarray(cc.reshape(2, c.KC, 128).transpose(2, 1, 0)).reshape(128, c.KC * 2)
    for l in range(L):
        m[f"adaw_{l}"] = adaw_tiled[l]
    m["adabT"] = vecT(inp["ada_b"])
    m["normT"] = vecT(inp["norm_w"])
    m["fnT"] = vecT(inp["final_norm_w"])
    cols, inw = in_col_order(c)
    for l in range(L):
        for i in range(2):
            m[f"w13_{l}_{i}"] = np.concatenate([tile_w(inp["ffn_w1"][l, i]), tile_w(inp["ffn_w3"][l, i])], axis=0)
            m[f"w2_{l}_{i}"] = tile_w(inp["ffn_w2"][l, i])
        W = inp["w_in"][l][:, cols]
        pad = c.NCH * 128 - W.shape[1]
        W = np.concatenate([W, np.zeros((c.D, pad), f32)], axis=1)
        m[f"win_{l}"] = tile_w(W)
        m[f"pa_{l}"] = tile_w(inp["proj_a"][l])
        m[f"pb_{l}"] = tile_w(inp["proj_b"][l])
        m[f"pc_{l}"] = tile_w(inp["proj_c"][l])
        m[f"wo_{l}"] = tile_w(inp["w_out"][l])
    G = c.GMG
    m["sguT"] = np.ascontiguousarray(inp["sgu_w"].transpose(3, 0, 1, 2)).reshape(128, L * G * 128)
    m["sgub"] = np.ascontiguousarray(np.broadcast_to(inp["sgu_b"].reshape(1, L * G * 128), (128, L * G * 128)))
    m["sgunw"] = np.ascontiguousarray(np.broadcast_to(inp["sgu_norm_w"].reshape(1, L * c.GMW), (128, L * c.GMW)))
    GW = c.GW
    qc = np.arange(GW)
    col_start = np.clip(qc - 8, 0, GW - 16)
    col_ok = (qc[None, :] >= col_start[:, None]) & (qc[None, :] < col_start[:, None] + 16)
    dc = np.clip(qc[None, :] - qc[:, None] + 15, 0, 30)
    rp = inp["na_rpb"]
    g = rp[:, :, :, dc]
    g = np.where(col_ok[None, None, None], g, f32(NEGBIG)).astype(f32)
    m["rpbm"] = np.ascontiguousarray(g.transpose(0, 1, 3, 2, 4)).reshape(L * c.NAH, GW, 15 * GW)
    cosT, sinT = rope_tables(c)
    m["ropec"], m["ropes"] = cosT, sinT
    H = c.DNH
    cw = inp["conv_w"]
    m["convw"] = np.ascontiguousarray(cw.reshape(L, 5, 3 * H, 128).transpose(3, 0, 2, 1)).reshape(128, L * 3 * H * 5)
    dt = inp["dn_dt_bias"].reshape(L, 1, 2 * H)
    al = inp["dn_a_log"].reshape(L, 1, 2 * H)
    dnc = np.stack([np.broadcast_to(dt, (L, c.NCK, 2 * H)), np.broadcast_to(al, (L, c.NCK, 2 * H))], axis=1)
    m["dnc"] = np.ascontiguousarray(np.broadcast_to(dnc.reshape(1, -1), (64, dnc.size))).astype(f32)
    m["dnnw"] = np.ascontiguousarray(np.broadcast_to(inp["dn_norm_w"].reshape(1, L * 128), (64, L * 128))).astype(f32)
    return {k: np.ascontiguousarray(v, dtype=f32) for k, v in m.items()}


def kernel(x, c, ctx, c_ctx, ada_w, ada_b, norm_w, ffn_w1, ffn_w3, ffn_w2, w_in, conv_w, dn_a_log,
           dn_dt_bias, dn_norm_w, sgu_w, sgu_b, sgu_norm_w, na_rpb, proj_a, proj_b, proj_c, w_out,
           final_norm_w):
    inp = dict(x=x, c=c, ctx=ctx, c_ctx=c_ctx, ada_w=ada_w, ada_b=ada_b, norm_w=norm_w, ffn_w1=ffn_w1,
               ffn_w3=ffn_w3, ffn_w2=ffn_w2, w_in=w_in, conv_w=conv_w, dn_a_log=dn_a_log, dn_dt_bias=dn_dt_bias,
               dn_norm_w=dn_norm_w, sgu_w=sgu_w, sgu_b=sgu_b, sgu_norm_w=sgu_norm_w, na_rpb=na_rpb,
               proj_a=proj_a, proj_b=proj_b, proj_c=proj_c, w_out=w_out, final_norm_w=final_norm_w)
    inp = {k: np.asarray(v, dtype=np.float32) for k, v in inp.items()}
    cfg = FULL
    B = inp["x"].shape[0]
    M = Main(cfg)
    nc = M.build()
    adaw_tiled = [tile_w(inp["ada_w"][l]) for l in range(cfg.L)]
    in_maps = [prep_core(cfg, b, inp, adaw_tiled) for b in range(B)]
    res = run_bass_kernel_spmd(nc, in_maps, core_ids=list(range(B)))
    out = np.stack([np.ascontiguousarray(res.results[b]["outT"].T) for b in range(B)], axis=0)
    return out.astype(np.float32)
```

```python
import os
import numpy as np
from contextlib import ExitStack
import concourse.bass as bass
import concourse.mybir as mybir
from concourse.bass_utils import run_bass_kernel_spmd

F32 = mybir.dt.float32
BF16 = mybir.dt.bfloat16
AF = mybir.ActivationFunctionType
ALU = mybir.AluOpType
AX = mybir.AxisListType

NCORES = 8


class Buf:
    def __init__(self, prog, t, name):
        self.prog = prog
        self.t = t
        self.name = name
        self.w = None
        self.r = []
        self.dsem = None
        self.excl = False

    def __getitem__(self, idx):
        return self.t[idx]


class _Rec:
    def __init__(self):
        self.calls = []

    def __getattr__(self, name):
        def f(*a, **kw):
            self.calls.append((name, a, kw))
            return None
        return f


class Prog:
    ENG = ("pe", "act", "dve", "pool", "sp")

    def __init__(self, nc, stack):
        self.nc = nc
        self.stack = stack
        self.q = {e: [] for e in self.ENG}
        self.sems = {}
        self.val = {}
        self.seen = {e: {} for e in self.ENG}
        for e in self.ENG:
            self._newsem("E_" + e)
        self.nbuf = 0
        self.out_deps = []

    def _newsem(self, key):
        s = self.stack.enter_context(self.nc.semaphore(key))
        self.sems[key] = s
        self.val[key] = 0
        return key

    def sbuf(self, shape, dtype, name=None):
        self.nbuf += 1
        name = name or f"sb{self.nbuf}"
        t = self.stack.enter_context(self.nc.sbuf_tensor(name, list(shape), dtype))
        return Buf(self, t, name)

    def psum(self, shape, dtype=F32, name=None):
        self.nbuf += 1
        name = name or f"ps{self.nbuf}"
        t = self.stack.enter_context(self.nc.psum_tensor(name, list(shape), dtype))
        b = Buf(self, t, name)
        b.excl = True
        return b

    def dram(self, name, shape, dtype, kind="Internal"):
        t = self.nc.dram_tensor(name, list(shape), dtype, kind=kind)
        return Buf(self, t.ap(), name)

    def _waits(self, eng, reads, writes):
        deps = []
        for b in reads:
            if b.w is not None:
                deps.append(b.w)
        for b in writes:
            if b.w is not None:
                deps.append(b.w)
            deps.extend(b.r)
        need = {}
        for (k, v, e) in deps:
            if e == "pe" and eng == "pe":
                continue
            if self.seen[eng].get(k, 0) >= v:
                continue
            need[k] = max(need.get(k, 0), v)
        for k, v in need.items():
            self.seen[eng][k] = v
            self.q[eng].append(("wait", k, v))

    def _mark(self, dep, reads, writes):
        for b in writes:
            b.w = dep
            b.r = []
        for b in reads:
            if b not in writes:
                b.r.append(dep)
                if len(b.r) > 24:
                    m = {}
                    for (k, v, e) in b.r:
                        if k not in m or m[k][1] < v:
                            m[k] = (k, v, e)
                    b.r = list(m.values())

    def op(self, eng, fn, reads=(), writes=()):
        reads = [b for b in reads if b is not None]
        writes = [b for b in writes if b is not None]
        xr = [b for b in reads if getattr(b, "excl", False)]
        if xr:
            writes = writes + [b for b in xr if b not in writes]
        self._waits(eng, reads, writes)
        k = "E_" + eng
        self.val[k] += 1
        dep = (k, self.val[k], eng)
        rec = _Rec()
        fn(rec)
        assert len(rec.calls) == 1
        m_, a_, kw_ = rec.calls[0]
        self.q[eng].append(("op", (lambda e, m_=m_, a_=a_, kw_=kw_: getattr(e, m_)(*a_, **kw_)), k, 1))
        self._mark(dep, reads, writes)
        return dep

    def dma(self, eng, out_b, out_ap, in_b, in_ap, is_output=False, **kw):
        reads = [in_b]
        writes = [out_b]
        self._waits(eng, reads, writes)
        if out_b.dsem is None:
            out_b.dsem = self._newsem("D_" + out_b.name)
        k = out_b.dsem
        self.val[k] += 16
        dep = (k, self.val[k], "dma")
        self.q[eng].append(("op", (lambda e, o=out_ap, i=in_ap, kw=kw: e.dma_start(out=o, in_=i, **kw)), k, 16))
        self._mark(dep, reads, writes)
        if is_output:
            self.out_deps.append(dep)
        return dep

    def finish(self, eng="sp"):
        need = {}
        for (k, v, e) in self.out_deps:
            need[k] = max(need.get(k, 0), v)
        for k, v in need.items():
            self.q[eng].append(("wait", k, v))

    def emit(self):
        nc = self.nc
        with nc.Block() as block:
            def run(engine, name):
                for it in self.q[name]:
                    if it[0] == "wait":
                        engine.wait_ge(self.sems[it[1]], it[2])
                    else:
                        it[1](engine).then_inc(self.sems[it[2]], it[3])

            @block.sync
            def _(e):
                run(e, "sp")

            @block.tensor
            def _(e):
                run(e, "pe")

            @block.scalar
            def _(e):
                run(e, "act")

            @block.vector
            def _(e):
                run(e, "dve")

            @block.gpsimd
            def _(e):
                run(e, "pool")


def new_prog():
    nc = bass.Bass("TRN2", target_bir_lowering=False)
    stack = ExitStack()
    return nc, stack, Prog(nc, stack)


D = 4096
MODW = 9 * D
MODC = MODW // NCORES


def build_mod(nlayers=2):
    nc, stack, P = new_prog()
    with stack:
        cT = nc.dram_tensor("cT", [128, 96], F32, kind="ExternalInput").ap()
        adaw = nc.dram_tensor("adaw", [nlayers * D, MODC], F32, kind="ExternalInput").ap()
        adab = nc.dram_tensor("adab", [nlayers * 3, MODC], F32, kind="ExternalInput").ap()
        mod = nc.dram_tensor("mod", [nlayers * 3, MODC], F32, kind="ExternalOutput").ap()
        B_in = Buf(P, None, "dram_in")
        B_out = Buf(P, None, "dram_out")
        c_sb = P.sbuf([128, 96], F32, "c_sb")
        cs = P.sbuf([128, 96], F32, "cs")
        P.dma("sp", c_sb, c_sb[:, :], B_in, cT[:, :])
        P.op("act", lambda e: e.activation(out=cs[:, :], in_=c_sb[:, :], func=AF.Silu), [c_sb], [cs])
        HALF = MODC // 2
        wbufs = [P.sbuf([128, HALF], F32, f"w{i}") for i in range(3)]
        pss = [P.psum([128, 512], F32, f"acc{i}") for i in range(5)]
        bsb = P.sbuf([3, MODC], F32, "bsb")
        osb = P.sbuf([3, MODC], F32, "osb")
        tiles = [(o, min(512, HALF - o)) for o in range(0, HALF, 512)]
        it = 0
        for l in range(nlayers):
            P.dma("sp", bsb, bsb[:, :], B_in, adab[l * 3:(l + 1) * 3, :])
            for h in range(2):
                for kc in range(32):
                    wb = wbufs[it % 3]
                    it += 1
                    q = "sp" if kc % 2 == 0 else "act"
                    P.dma(q, wb, wb[:, :], B_in,
                          adaw[l * D + kc * 128:l * D + (kc + 1) * 128, h * HALF:(h + 1) * HALF])
                    for ti, (o, n) in enumerate(tiles):
                        P.op("pe", lambda e, ti=ti, o=o, n=n, kc=kc, wb=wb: e.matmul(
                            pss[ti][0:3, 0:n], lhsT=cs[:, kc * 3:(kc + 1) * 3], rhs=wb[:, o:o + n],
                            start=(kc == 0), stop=(kc == 31)), [cs, wb], [pss[ti]])
                for ti, (o, n) in enumerate(tiles):
                    c0 = h * HALF + o
                    P.op("dve", lambda e, ti=ti, n=n, c0=c0: e.tensor_tensor(
                        out=osb[:, c0:c0 + n], in0=pss[ti][0:3, 0:n], in1=bsb[:, c0:c0 + n], op=ALU.add),
                        [pss[ti], bsb], [osb])
            P.dma("sp", B_out, mod[l * 3:(l + 1) * 3, :], osb, osb[:, :], is_output=True)
        P.finish("sp")
        P.emit()
    return nc


def run_mod(c, c_ctx, ada_w, ada_b):
    L = ada_w.shape[0]
    call = np.concatenate([c, c_ctx[None, :]], axis=0).astype(np.float32)
    cT = np.ascontiguousarray(call.reshape(3, 32, 128).transpose(2, 1, 0)).reshape(128, 96)
    nc = build_mod(L)
    in_maps = []
    for i in range(NCORES):
        sl = slice(i * MODC, (i + 1) * MODC)
        in_maps.append({
            "cT": cT,
            "adaw": np.ascontiguousarray(ada_w[:, :, sl]).reshape(L * D, MODC),
            "adab": np.ascontiguousarray(np.repeat(ada_b[:, None, sl], 3, axis=1)).reshape(L * 3, MODC),
        })
    res = run_bass_kernel_spmd(nc, in_maps, core_ids=list(range(NCORES)))
    mod = np.concatenate([r["mod"].reshape(L, 3, MODC) for r in res.results], axis=2)
    return mod.reshape(L, 3, 3, 3, D)


def _prog_coll(self, kind, in_b, in_t, out_b, out_t, groups=None):
    eng = "pool"
    self._waits(eng, [in_b], [out_b])
    k = "C_" + out_b.name
    if k not in self.sems:
        self._newsem(k)
    self.val[k] += 1
    dep = (k, self.val[k], "dma")
    groups = groups or [list(range(NCORES))]
    self.q[eng].append(("op", (lambda e, o=out_t, i=in_t: e.collective_compute(
        kind, ALU.bypass, replica_groups=groups, ins=[i.opt()], outs=[o.opt()])), k, 1))
    self._mark(dep, [in_b], [out_b])
    return dep


Prog.coll = _prog_coll


def _prog_barrier(self):
    for e in self.ENG:
        for k, v in self.val.items():
            if v > 0 and self.seen[e].get(k, 0) < v:
                self.seen[e][k] = v
                self.q[e].append(("wait", k, v))


class _Scope:
    def __init__(self, prog):
        self.prog = prog
        self.stack = ExitStack()

    def __enter__(self):
        self.stack.__enter__()
        return self

    def __exit__(self, *a):
        self.prog.barrier()
        return self.stack.__exit__(*a)

    def sbuf(self, shape, dtype, name=None):
        P = self.prog
        P.nbuf += 1
        name = (name or "sb") + f"_{P.nbuf}"
        t = self.stack.enter_context(P.nc.sbuf_tensor(name, list(shape), dtype))
        return Buf(P, t, name)

    def psum(self, shape, dtype=F32, name=None):
        P = self.prog
        P.nbuf += 1
        name = (name or "ps") + f"_{P.nbuf}"
        t = self.stack.enter_context(P.nc.psum_tensor(name, list(shape), dtype))
        b = Buf(P, t, name)
        b.excl = True
        return b


def _prog_scope(self):
    return _Scope(self)


Prog.barrier = _prog_barrier
Prog.scope = _prog_scope


def _prog_dma(self, eng, out_b, out_ap, in_b, in_ap, is_output=False, **kw):
    reads = [in_b]
    writes = [out_b]
    self._waits(eng, reads, writes)
    if out_b.dsem is None:
        pool = getattr(self, "_dpool", None)
        if pool is None:
            pool = self._dpool = []
        if pool:
            out_b.dsem = pool.pop()
        else:
            out_b.dsem = self._newsem(f"D{len(self.sems)}")
    k = out_b.dsem
    self.val[k] += 16
    dep = (k, self.val[k], "dma")
    self.q[eng].append(("op", (lambda e, o=out_ap, i=in_ap, kw=kw: e.dma_start(out=o, in_=i, **kw)), k, 16))
    self._mark(dep, reads, writes)
    if is_output:
        self.out_deps.append(dep)
    return dep


Prog.dma = _prog_dma

_scope_sbuf0 = _Scope.sbuf
_scope_exit0 = _Scope.__exit__


def _scope_sbuf(self, shape, dtype, name=None):
    b = _scope_sbuf0(self, shape, dtype, name)
    if not hasattr(self, "bufs"):
        self.bufs = []
    self.bufs.append(b)
    return b


def _scope_exit(self, *a):
    r = _scope_exit0(self, *a)
    P = self.prog
    if not hasattr(P, "_dpool"):
        P._dpool = []
    for b in getattr(self, "bufs", []):
        if b.dsem is not None:
            P._dpool.append(b.dsem)
            b.dsem = None
    return r


_Scope.sbuf = _scope_sbuf
_Scope.__exit__ = _scope_exit


class Cfg:
    def __init__(self, D, FF, SEQ, CTX, GW, DNH, NAH, GMG, L, TB):
        self.D, self.FF, self.SEQ, self.CTX, self.GW = D, FF, SEQ, CTX, GW
        self.DNH, self.NAH, self.GMG, self.L, self.TB = DNH, NAH, GMG, L, TB
        self.KC = D // 128
        self.FC = FF // 128
        self.T = SEQ + CTX
        self.ROWS = SEQ // GW
        self.DNW, self.NAW, self.GMW = DNH * 128, NAH * 128, GMG * 128
        self.NCK = self.T // 64
        self.plan = [("dn", "fm", 3 * DNH), ("na", "fm", 2 * NAH), ("gu", "fm", GMG), ("g", "fm", 3 * self.KC),
                     ("nav", "tm", NAH), ("dg", "tm", DNH), ("gv", "tm", GMG), ("ab", "tm", 1)]
        self.NCH = sum(p[2] for p in self.plan)

    def blocks(self):
        out = []
        t = 0
        while t < self.T:
            n = min(self.TB, self.T - t)
            out.append((t, n))
            t += n
        return out

    def segs(self, t0, n):
        out = []
        if t0 < self.SEQ:
            m = min(n, self.SEQ - t0)
            out.append((0, m, 0))
            if m < n:
                out.append((m, n - m, 1))
        else:
            out.append((0, n, 1))
        return out


def tiles_of(n, w=512):
    out = []
    o = 0
    while o < n:
        m = min(w, n - o)
        out.append((o, m))
        o += m
    return out


FULL = Cfg(D=4096, FF=5632, SEQ=4096, CTX=256, GW=64, DNH=8, NAH=16, GMG=8, L=2, TB=1088)

GELU_C = 1.5957691216057308


class Main:
    def __init__(self, cfg, dbg=False, nlayers=None, stop_after=None):
        self.cfg = cfg
        self.dbg = dbg
        self.nl = nlayers or cfg.L
        self.stop_after = stop_after
        self.nc, self.stack, self.P = new_prog()
        self.qi = 0

    def din(self, name, shape, dtype=F32):
        ap = self.nc.dram_tensor(name, list(shape), dtype, kind="ExternalInput").ap()
        return Buf(self.P, ap, name)

    def dscr(self, name, shape, dtype=F32):
        kind = "ExternalOutput" if self.dbg else "Internal"
        ap = self.nc.dram_tensor(name, list(shape), dtype, kind=kind).ap()
        return Buf(self.P, ap, name)

    def dmaq(self):
        return "sp"

    def load_consts(self):
        P, c = self.P, self.cfg
        self.cst = self.din("consts", [128, CONST_W])
        self.csb = P.sbuf([128, CONST_W], F32, "csb")
        P.dma("sp", self.csb, self.csb[:, :], self.cst, self.cst[:, :])
        self.ones_bf = P.sbuf([128, 128], BF16, "ones_bf")
        P.op("dve", lambda e: e.tensor_copy(out=self.ones_bf[:, :], in_=self.csb[:, C_ONES:C_ONES + 128]),
             [self.csb], [self.ones_bf])
        self.ident_bf = P.sbuf([128, 128], BF16, "ident_bf")
        P.op("dve", lambda e: e.tensor_copy(out=self.ident_bf[:, :], in_=self.csb[:, C_ID:C_ID + 128]),
             [self.csb], [self.ident_bf])

    def mod_scalars(self, S, l, sub, half):
        P, c = self.P, self.cfg
        KC = c.KC
        a, sh, gt = [], [], []
        for grp in range(2):
            base = (((l * 2 + grp) * 3 + sub) * 3) * KC
            shift_ap = lambda b=base: self.modsb[:, b:b + KC]
            scale_ap = lambda b=base: self.modsb[:, b + KC:b + 2 * KC]
            gate_ap = lambda b=base: self.modsb[:, b + 2 * KC:b + 3 * KC]
            nb = (l * 3 + sub) * KC
            at = S.sbuf([128, KC], F32, "a_sc")
            P.op("dve", lambda e, at=at, sa=scale_ap, nb=nb: e.scalar_tensor_tensor(
                out=at[:, :], in0=sa(), scalar=1.0, in1=self.normsb[:, nb:nb + KC], op0=ALU.add, op1=ALU.mult),
                [self.modsb, self.normsb], [at])
            st = S.sbuf([128, KC], F32, "shift")
            P.op("dve", lambda e, st=st, sa=shift_ap: e.tensor_copy(out=st[:, :], in_=sa()), [self.modsb], [st])
            g = S.sbuf([128, KC], F32, "gate")
            P.op("dve", lambda e, g=g, ga=gate_ap: e.tensor_scalar(
                out=g[:, :], in0=ga(), scalar1=(0.5 if half else 1.0), scalar2=None, op0=ALU.mult),
                [self.modsb], [g])
            a.append(at)
            sh.append(st)
            gt.append(g)
        return a, sh, gt

    def norm_block(self, S, hsrc, t0, n, a, sh, xin, hst, sqt, pss, tmp, rstd_b):
        P, c = self.P, self.cfg
        KC, D = c.KC, c.D
        for si, (o, m) in enumerate(tiles_of(n, 128)):
            hs = hst[si % 2]
            P.dma(self.dmaq(), hs, hs[:, 0:KC * m].rearrange("p (k t) -> p k t", k=KC), hsrc,
                  hsrc.t[:, t0 + o:t0 + o + m].rearrange("(k p) t -> p k t", p=128))
            P.op("pool", lambda e, hs=hs, m=m: e.tensor_tensor(
                out=sqt[:, 0:KC * m], in0=hs[:, 0:KC * m], in1=hs[:, 0:KC * m], op=ALU.mult), [hs], [sqt])
            ps = pss[si % 2]
            for kc in range(KC):
                P.op("pe", lambda e, ps=ps, kc=kc, m=m: e.matmul(
                    ps[:, 0:m], lhsT=self.ones_bf[:, :], rhs=sqt[:, kc * m:(kc + 1) * m],
                    start=(kc == 0), stop=(kc == KC - 1)), [self.ones_bf, sqt], [ps])
            P.op("act", lambda e, ps=ps, m=m: e.activation(
                out=rstd_b[:, 0:m], in_=ps[:, 0:m], func=AF.Sqrt, bias=self.eps_ap(), scale=1.0 / D),
                [ps, self.csb], [rstd_b])
            P.op("dve", lambda e, m=m: e.reciprocal(out=rstd_b[:, 0:m], in_=rstd_b[:, 0:m]), [rstd_b], [rstd_b])
            for (so, sn, grp) in c.segs(t0 + o, m):
                for kc in range(KC):
                    tb = tmp[kc % 2]
                    P.op("dve", lambda e, tb=tb, hs=hs, kc=kc, m=m, so=so, sn=sn, grp=grp: e.scalar_tensor_tensor(
                        out=tb[:, 0:sn], in0=hs[:, kc * m + so:kc * m + so + sn], scalar=a[grp][:, kc:kc + 1],
                        in1=rstd_b[:, so:so + sn], op0=ALU.mult, op1=ALU.mult), [hs, a[grp], rstd_b], [tb])
                    P.op("act", lambda e, tb=tb, kc=kc, o=o, so=so, sn=sn, grp=grp: e.activation(
                        out=xin[:, kc, o + so:o + so + sn], in_=tb[:, 0:sn], func=AF.Identity,
                        bias=sh[grp][:, kc:kc + 1], scale=1.0), [tb, sh[grp]], [xin])

    def eps_ap(self):
        return self.csb[:, C_EPS:C_EPS + 1]

    def linear(self, S, xin, KCn, wsrc, chunks, n, w32, w16, psb, epi_fm=None, epi_tm=None):
        P = self.P
        tls = tiles_of(n, 512)
        subt = tiles_of(n, 128)
        cnt = getattr(self, "_lin_cnt", 0)
        W = KCn * 128
        hW = (W // 2)
        base = cnt
        nchunks = len(chunks)

        def issue_dma(i):
            wb = w32[(base + i) % 2]
            P.dma("sp", wb, wb[:, 0:W], wsrc, wsrc.t[chunks[i][0], :, 0:W])

        def issue_cast(i):
            a, b = w16[(base + i) % 2], w32[(base + i) % 2]
            P.op("act", lambda e: e.activation(out=a[:, 0:hW], in_=b[:, 0:hW], func=AF.Copy), [b], [a])
            P.op("pool", lambda e: e.tensor_copy(out=a[:, hW:W], in_=b[:, hW:W]), [b], [a])

        issue_dma(0)
        if nchunks > 1:
            issue_dma(1)
        issue_cast(0)
        for idx, (ci, mode, user) in enumerate(chunks):
            wb16 = w16[cnt % 2]
            if idx + 1 < nchunks:
                issue_cast(idx + 1)
            if idx + 2 < nchunks:
                issue_dma(idx + 2)
            if mode == "fm":
                half = (cnt % 2) * 4
                for ti, (o, m) in enumerate(tls):
                    ps = psb[half + ti]
                    for kc in range(KCn):
                        P.op("pe", lambda e, ps=ps, kc=kc, o=o, m=m, wb16=wb16: e.matmul(
                            ps[:, 0:m], lhsT=wb16[:, kc * 128:(kc + 1) * 128], rhs=xin[:, kc, o:o + m],
                            start=(kc == 0), stop=(kc == KCn - 1)), [wb16, xin], [ps])
                    epi_fm(user, ti, o, m, ps)
            else:
                for si, (o, m) in enumerate(subt):
                    ps = psb[(cnt % 2) * 4 + (si % 4)]
                    for kc in range(KCn):
                        P.op("pe", lambda e, ps=ps, kc=kc, o=o, m=m, wb16=wb16: e.matmul(
                            ps[0:m, 0:128], lhsT=xin[:, kc, o:o + m], rhs=wb16[:, kc * 128:(kc + 1) * 128],
                            start=(kc == 0), stop=(kc == KCn - 1)), [wb16, xin], [ps])
                    epi_tm(user, si, o, m, ps)
            cnt += 1
        self._lin_cnt = cnt

    def ffn(self, l, i, hsrc, hdst):
        P, c = self.P, self.cfg
        KC, FC = c.KC, c.FC
        sub = 0 if i == 0 else 2
        TB2 = c.TB // 2
        with P.scope() as S:
            a, sh, gt = self.mod_scalars(S, l, sub, True)
            WMAX = max(KC, FC) * 128
            XW = max(KC * c.TB, FC * TB2)
            xin_b = S.sbuf([128, XW], BF16, "xin")
            xin1 = xin_b[:, 0:KC * c.TB].rearrange("p (k t) -> p k t", k=KC)
            xin2 = xin_b[:, 0:FC * TB2].rearrange("p (k t) -> p k t", k=FC)
            X1 = Buf(P, xin1, "x1v"); X1 = _alias(xin_b, xin1)
            X2 = _alias(xin_b, xin2)
            w32 = [S.sbuf([128, WMAX], F32, f"w32_{j}") for j in range(2)]
            w16 = [S.sbuf([128, WMAX], BF16, f"w16_{j}") for j in range(2)]
            sqt = S.sbuf([128, KC * 128], BF16, "sqt")
            tmp = [S.sbuf([128, 512], F32, f"tmp{j}") for j in range(2)]
            rstd_b = S.sbuf([128, 128], F32, "rstd")
            s1 = [S.sbuf([128, 512], F32, f"s1_{j}") for j in range(len(tiles_of(c.TB)))]
            orow = [S.sbuf([128, c.TB], BF16, f"orow{j}") for j in range(2)]
            hrow = [S.sbuf([128, 512], F32, f"hrow{j}") for j in range(2)]
            orow32 = [S.sbuf([128, 512], F32, f"orow32{j}") for j in range(2)]
            psb = [S.psum([128, 512], F32, f"psb{j}") for j in range(8)]
            for (t0, n) in c.blocks():
                self.norm_block(S, hsrc, t0, n, a, sh, X1, w32, sqt, psb[0:2], tmp, rstd_b)

                def epi1(user, ti, o, m, ps, t0=t0, n=n):
                    f, which = user
                    if which == 0:
                        P.op("act", lambda e, ps=ps, m=m, ti=ti: e.activation(
                            out=s1[ti][:, 0:m], in_=ps[:, 0:m], func=AF.Silu), [ps], [s1[ti]])
                    else:
                        ob = orow[f % 2]
                        P.op("dve", lambda e, ps=ps, m=m, o=o, ob=ob, ti=ti: e.tensor_tensor(
                            out=ob[:, o:o + m], in0=ps[:, 0:m], in1=s1[ti][:, 0:m], op=ALU.mult),
                            [ps, s1[ti]], [ob])
                        if o + m == n:
                            P.dma("sp", self.gT, self.gT.t[f * 128:(f + 1) * 128, t0:t0 + n], ob, ob[:, 0:n])

                chunks = []
                for f in range(FC):
                    chunks.append((f, "fm", (f, 0)))
                    chunks.append((FC + f, "fm", (f, 1)))
                self.linear(S, X1, KC, self.w13[l][i], chunks, n, w32, w16, psb, epi_fm=epi1)
            cnt2 = [0]
            for (t0, n) in [(t, min(TB2, c.T - t)) for t in range(0, c.T, TB2)]:
                P.dma("sp", X2, X2[:, 0:FC, 0:n], self.gT,
                      self.gT.t[:, t0:t0 + n].rearrange("(k p) t -> p k t", p=128))

                def epi2(user, ti, o, m, ps, t0=t0, n=n):
                    d = user
                    j = cnt2[0] % 2
                    cnt2[0] += 1
                    hr, o32 = hrow[j], orow32[j]
                    P.dma("sp", hr, hr[:, 0:m], hsrc, hsrc.t[d * 128:(d + 1) * 128, t0 + o:t0 + o + m])
                    for (so, sn, grp) in c.segs(t0 + o, m):
                        P.op("dve", lambda e, ps=ps, so=so, sn=sn, grp=grp, d=d, hr=hr, o32=o32: e.scalar_tensor_tensor(
                            out=o32[:, so:so + sn], in0=ps[:, so:so + sn], scalar=gt[grp][:, d:d + 1],
                            in1=hr[:, so:so + sn], op0=ALU.mult, op1=ALU.add), [ps, gt[grp], hr], [o32])
                    P.dma("sp", hdst, hdst.t[d * 128:(d + 1) * 128, t0 + o:t0 + o + m], o32, o32[:, 0:m])

                self.linear(S, X2, FC, self.w2[l][i], [(d, "fm", d) for d in range(KC)], n, w32, w16, psb,
                            epi_fm=epi2)


class _AliasBuf:
    def __init__(self, parent, ap):
        object.__setattr__(self, "_p", parent)
        object.__setattr__(self, "_ap", ap)

    def __getattr__(self, k):
        return getattr(object.__getattribute__(self, "_p"), k)

    def __setattr__(self, k, v):
        setattr(object.__getattribute__(self, "_p"), k, v)

    def __getitem__(self, idx):
        return object.__getattribute__(self, "_ap")[idx]

    def __eq__(self, o):
        return _root(self) is _root(o)

    def __hash__(self):
        return id(_root(self))


def _root(b):
    while isinstance(b, _AliasBuf):
        b = object.__getattribute__(b, "_p")
    return b


def _alias(parent, ap):
    return _AliasBuf(parent, ap)


C_ONES, C_ID, C_NEG1, C_PERM = 0, 128, 256, 384
C_CUMF, C_CUMB, C_MSF, C_MSB, C_MITF, C_MITB = 512, 576, 640, 704, 768, 832
C_EPS = 896
CONST_W = 904
NEGBIG = -30000.0


def make_consts():
    c = np.zeros((128, CONST_W), np.float32)
    c[:, C_ONES:C_ONES + 128] = 1.0
    c[:, C_ID:C_ID + 128] = np.eye(128, dtype=np.float32)
    c[:, C_NEG1:C_NEG1 + 128] = -1.0
    pm = np.zeros((128, 128), np.float32)
    for m in range(128):
        q = m // 32
        partner = m + 32 if q % 2 == 0 else m - 32
        pm[partner, m] = 1.0
    c[:, C_PERM:C_PERM + 128] = pm
    i = np.arange(64)
    c[:64, C_CUMF:C_CUMF + 64] = (i[:, None] <= i[None, :])
    c[:64, C_CUMB:C_CUMB + 64] = (i[:, None] >= i[None, :])
    c[:64, C_MSF:C_MSF + 64] = np.where(i[None, :] < i[:, None], 0.0, NEGBIG)
    c[:64, C_MSB:C_MSB + 64] = np.where(i[None, :] > i[:, None], 0.0, NEGBIG)
    c[:64, C_MITF:C_MITF + 64] = np.where(i[:, None] <= i[None, :], 0.0, NEGBIG)
    c[:64, C_MITB:C_MITB + 64] = np.where(i[:, None] >= i[None, :], 0.0, NEGBIG)
    c[:, C_EPS] = 1e-6
    c[:, C_EPS + 1] = np.log(128.0 ** -0.5)
    return c


def _gelu(self, S, src, rows, m, dst_fn, reads, writes, k):
    P = self.P
    gx, g1 = self._gx[k % 2], self._g1[k % 2]
    P.op("act", lambda e: e.activation(out=gx[0:rows, 0:m], in_=src, func=AF.Copy), reads, [gx])
    P.op("dve", lambda e: e.tensor_tensor(out=g1[0:rows, 0:m], in0=gx[0:rows, 0:m], in1=gx[0:rows, 0:m], op=ALU.mult),
         [gx], [g1])
    P.op("pool", lambda e: e.tensor_scalar(out=g1[0:rows, 0:m], in0=g1[0:rows, 0:m], scalar1=0.044715, scalar2=1.0,
                                           op0=ALU.mult, op1=ALU.add), [g1], [g1])
    P.op("pool", lambda e: e.tensor_tensor(out=g1[0:rows, 0:m], in0=g1[0:rows, 0:m], in1=gx[0:rows, 0:m], op=ALU.mult),
         [g1, gx], [g1])
    P.op("act", lambda e: e.activation(out=g1[0:rows, 0:m], in_=g1[0:rows, 0:m], func=AF.Sigmoid, scale=GELU_C),
         [g1], [g1])
    P.op("dve", lambda e: e.tensor_tensor(out=dst_fn(), in0=gx[0:rows, 0:m], in1=g1[0:rows, 0:m], op=ALU.mult),
         [gx, g1], writes)


Main.gelu = _gelu


def _inproj(self, l, hsrc):
    P, c = self.P, self.cfg
    KC = c.KC
    with P.scope() as S:
        a, sh, gt = self.mod_scalars(S, l, 1, False)
        xin_b = S.sbuf([128, KC * c.TB], BF16, "xin")
        X1 = _alias(xin_b, xin_b[:, :].rearrange("p (k t) -> p k t", k=KC))
        w32 = [S.sbuf([128, KC * 128], F32, f"w32_{j}") for j in range(2)]
        w16 = [S.sbuf([128, KC * 128], BF16, f"w16_{j}") for j in range(2)]
        sqt = S.sbuf([128, KC * 128], BF16, "sqt")
        tmp = [S.sbuf([128, 512], F32, f"tmp{j}") for j in range(2)]
        rstd_b = S.sbuf([128, 128], F32, "rstd")
        self._gx = [S.sbuf([128, 512], F32, f"gx{j}") for j in range(2)]
        self._g1 = [S.sbuf([128, 512], F32, f"g1{j}") for j in range(2)]
        NS = len(tiles_of(c.TB, 128))
        orow32 = [S.sbuf([128, c.TB], F32, f"or32_{j}") for j in range(2)]
        orow16 = [S.sbuf([128, c.TB], BF16, f"or16_{j}") for j in range(2)]
        otm32 = [S.sbuf([128, NS * 128], F32, f"ot32_{j}") for j in range(2)]
        otm16 = [S.sbuf([128, NS * 128], BF16, f"ot16_{j}") for j in range(2)]
        psb = [S.psum([128, 512], F32, f"psb{j}") for j in range(8)]
        dst_fm = {"dn": self.z_dn, "na": self.z_na, "gu": self.z_gu, "g": self.z_g}
        dst_tm = {"nav": self.v_na, "dg": self.z_dg, "gv": self.z_gv, "ab": self.z_ab}
        chunks = []
        ci = 0
        for (name, mode, nch) in c.plan:
            for j in range(nch):
                chunks.append((ci, mode, (name, j, ci)))
                ci += 1
        gk = [0]
        for (t0, n) in c.blocks():
            self.norm_block(S, hsrc, t0, n, a, sh, X1, w32, sqt, psb[0:2], tmp, rstd_b)

            def epi_fm(user, ti, o, m, ps, t0=t0, n=n):
                name, j, ci = user
                ob = (orow16 if name == "na" else orow32)[ci % 2]
                if name in ("dn", "na"):
                    P.op("act", lambda e: e.activation(out=ob[:, o:o + m], in_=ps[:, 0:m], func=AF.Copy), [ps], [ob])
                elif name == "g":
                    P.op("act", lambda e: e.activation(out=ob[:, o:o + m], in_=ps[:, 0:m], func=AF.Sigmoid), [ps], [ob])
                else:
                    gk[0] += 1
                    self.gelu(S, ps[:, 0:m], 128, m, lambda: ob[:, o:o + m], [ps], [ob], gk[0])
                if o + m == n:
                    d = dst_fm[name]
                    P.dma("sp", d, d.t[j * 128:(j + 1) * 128, t0:t0 + n], ob, ob[:, 0:n])

            def epi_tm(user, si, o, m, ps, t0=t0, n=n):
                name, j, ci = user
                ob = (otm16 if name == "nav" else otm32)[ci % 2]
                if name in ("nav", "ab"):
                    P.op("act", lambda e: e.activation(out=ob[0:m, si * 128:(si + 1) * 128], in_=ps[0:m, 0:128],
                                                       func=AF.Copy), [ps], [ob])
                elif name == "dg":
                    P.op("act", lambda e: e.activation(out=ob[0:m, si * 128:(si + 1) * 128], in_=ps[0:m, 0:128],
                                                       func=AF.Silu), [ps], [ob])
                else:
                    gk[0] += 1
                    self.gelu(S, ps[0:m, 0:128], m, 128, lambda: ob[0:m, si * 128:(si + 1) * 128], [ps], [ob], gk[0])
                if o + m == n:
                    d = dst_tm[name]
                    nfull = n // 128
                    if nfull:
                        P.dma("sp", d, d.t[t0:t0 + nfull * 128, j * 128:(j + 1) * 128].rearrange("(s p) c -> p s c", p=128),
                              ob, ob[:, 0:nfull * 128].rearrange("p (s c) -> p s c", c=128))
                    rem = n - nfull * 128
                    if rem:
                        P.dma("sp", d, d.t[t0 + nfull * 128:t0 + n, j * 128:(j + 1) * 128],
                              ob, ob[0:rem, nfull * 128:(nfull + 1) * 128])

            self.linear(S, X1, KC, self.win[l], chunks, n, w32, w16, psb, epi_fm=epi_fm, epi_tm=epi_tm)


Main.inproj = _inproj


def _gmlp(self, l):
    P, c = self.P, self.cfg
    G = c.GMG
    GW_ = c.GMW
    with P.scope() as S:
        sgu = S.sbuf([128, G * 128], F32, "sgu32")
        P.dma("sp", sgu, sgu[:, :], self.sguT, self.sguT.t[:, l * G * 128:(l + 1) * G * 128])
        sgu16 = S.sbuf([128, G * 128], BF16, "sgu16")
        P.op("act", lambda e: e.activation(out=sgu16[:, :], in_=sgu[:, :], func=AF.Copy), [sgu], [sgu16])
        bbc = S.sbuf([128, G * 128], F32, "bbc")
        P.dma("sp", bbc, bbc[:, :], self.sgub, self.sgub.t[:, l * G * 128:(l + 1) * G * 128])
        nwb = S.sbuf([128, GW_], F32, "nwb")
        P.dma("sp", nwb, nwb[:, :], self.sgunw, self.sgunw.t[:, l * GW_:(l + 1) * GW_])
        vin = [S.sbuf([128, GW_], F32, f"vin{j}") for j in range(2)]
        uin = [S.sbuf([128, G * 128], F32, f"uin{j}") for j in range(2)]
        v16 = [S.sbuf([128, GW_], BF16, f"v16{j}") for j in range(2)]
        sq = S.sbuf([128, GW_], F32, "sq")
        st = [S.sbuf([128, 4], F32, f"st{j}") for j in range(2)]
        yo = [S.sbuf([128, G * 128], BF16, f"yo{j}") for j in range(2)]
        t32 = [S.sbuf([128, 128], F32, f"t32{j}") for j in range(2)]
        ps = [S.psum([128, 512], F32, f"gps{j}") for j in range(4)]
        for ck in range(c.T // 128):
            t0 = ck * 128
            j = ck % 2
            v, u, vb, s_, y = vin[j], uin[j], v16[j], st[j], yo[j]
            P.dma("sp", v, v[:, :], self.z_gv, self.z_gv.t[t0:t0 + 128, :])
            P.dma("sp", u, u[:, :].rearrange("p (g t) -> p g t", g=G), self.z_gu,
                  self.z_gu.t[:, t0:t0 + 128].rearrange("(g p) t -> p g t", p=128))
            P.op("dve", lambda e, v=v, s_=s_: e.tensor_reduce(out=s_[:, 0:1], in_=v[:, :], axis=AX.X, op=ALU.add),
                 [v], [s_])
            P.op("dve", lambda e, s_=s_: e.tensor_scalar(out=s_[:, 1:2], in0=s_[:, 0:1], scalar1=-1.0 / GW_, scalar2=None,
                                                         op0=ALU.mult), [s_], [s_])
            P.op("act", lambda e, v=v, s_=s_: e.activation(out=v[:, :], in_=v[:, :], func=AF.Identity, bias=s_[:, 1:2],
                                                           scale=1.0), [v, s_], [v])
            P.op("act", lambda e, v=v, s_=s_: e.activation(out=sq[:, :], in_=v[:, :], func=AF.Square,
                                                           accum_out=s_[:, 2:3]), [v], [sq, s_])
            P.op("act", lambda e, s_=s_: e.activation(out=s_[:, 3:4], in_=s_[:, 2:3], func=AF.Sqrt, bias=self.eps_ap(),
                                                      scale=1.0 / GW_), [s_, self.csb], [s_])
            P.op("dve", lambda e, s_=s_: e.reciprocal(out=s_[:, 3:4], in_=s_[:, 3:4]), [s_], [s_])
            P.op("dve", lambda e, v=v, vb=vb, s_=s_: e.scalar_tensor_tensor(
                out=vb[:, :], in0=v[:, :], scalar=s_[:, 3:4], in1=nwb[:, :], op0=ALU.mult, op1=ALU.mult),
                [v, s_, nwb], [vb])
            for g in range(G):
                p = ps[g % 4]
                tt = t32[g % 2]
                P.op("pe", lambda e, p=p, g=g, vb=vb: e.matmul(p[:, 0:128], lhsT=vb[:, g * 128:(g + 1) * 128],
                                                               rhs=sgu16[:, g * 128:(g + 1) * 128], start=True, stop=True),
                     [vb, sgu16], [p])
                P.op("dve", lambda e, p=p, g=g, tt=tt: e.tensor_tensor(out=tt[:, :], in0=p[:, 0:128],
                                                                       in1=bbc[:, g * 128:(g + 1) * 128], op=ALU.add),
                     [p, bbc], [tt])
                P.op("pool", lambda e, g=g, tt=tt, u=u, y=y: e.tensor_tensor(
                    out=y[:, g * 128:(g + 1) * 128], in0=tt[:, :], in1=u[:, g * 128:(g + 1) * 128], op=ALU.mult),
                    [tt, u], [y])
            P.dma("sp", self.yaT, self.yaT.t[:, t0:t0 + 128].rearrange("(g p) t -> p g t", p=128),
                  y, y[:, :].rearrange("p (g t) -> p g t", g=G))


Main.gmlp = _gmlp


def _merge(self, l, hsrc, hdst):
    P, c = self.P, self.cfg
    KC = c.KC
    ka, kb, kc_ = c.GMW // 128, c.DNW // 128, c.NAW // 128
    KY = ka + kb + kc_
    with P.scope() as S:
        a, sh, gt = self.mod_scalars(S, l, 1, False)
        TBm = c.TB // 2
        yin_b = S.sbuf([128, KY * TBm], BF16, "yin")
        YIN = _alias(yin_b, yin_b[:, :].rearrange("p (k t) -> p k t", k=KY))
        yy_b = S.sbuf([128, KC * TBm], BF16, "yy")
        YY = _alias(yy_b, yy_b[:, :].rearrange("p (k t) -> p k t", k=KC))
        w32 = [S.sbuf([128, KC * 128], F32, f"w32_{j}") for j in range(2)]
        w16 = [S.sbuf([128, KC * 128], BF16, f"w16_{j}") for j in range(2)]
        grow = [S.sbuf([128, 512], F32, f"grow{j}") for j in range(3)]
        accs = [S.sbuf([128, 512], F32, f"acc{j}") for j in range(3)]
        hrow = [S.sbuf([128, 512], F32, f"hrow{j}") for j in range(2)]
        o32 = [S.sbuf([128, 512], F32, f"o32{j}") for j in range(2)]
        psb = [S.psum([128, 512], F32, f"psb{j}") for j in range(8)]
        cnt = [0]
        for (t0, n) in [(t, min(TBm, c.T - t)) for t in range(0, c.T, TBm)]:
            P.dma("sp", YIN, YIN[:, 0:ka, 0:n], self.yaT, self.yaT.t[:, t0:t0 + n].rearrange("(k p) t -> p k t", p=128))
            P.dma("sp", YIN, YIN[:, ka:ka + kb, 0:n], self.ybT, self.ybT.t[:, t0:t0 + n].rearrange("(k p) t -> p k t", p=128))
            P.dma("sp", YIN, YIN[:, ka + kb:KY, 0:n], self.ycT, self.ycT.t[:, t0:t0 + n].rearrange("(k p) t -> p k t", p=128))
            tls = tiles_of(n, 512)
            specs = ((0, ka, self.pa[l]), (ka, kb, self.pb[l]), (ka + kb, kc_, self.pc[l]))
            items = [(d, bi) for d in range(KC) for bi in range(3)]
            mbase = cnt[0]

            def m_dma(i):
                d_, bi_ = items[i]
                _, kn_, ws_ = specs[bi_]
                wb = w32[(mbase + i) % 2]
                P.dma("sp", wb, wb[:, 0:kn_ * 128], ws_, ws_.t[d_, :, 0:kn_ * 128])

            def m_cast(i):
                _, bi_ = items[i]
                W_ = specs[bi_][1] * 128
                a_, b_ = w16[(mbase + i) % 2], w32[(mbase + i) % 2]
                P.op("act", lambda e: e.activation(out=a_[:, 0:W_], in_=b_[:, 0:W_], func=AF.Copy), [b_], [a_])

            m_dma(0)
            m_dma(1)
            m_cast(0)
            for ii, (d, bi) in enumerate(items):
                if True:
                    koff, kn, wsrc = specs[bi]
                    k2 = cnt[0]
                    cnt[0] += 1
                    wb32, wb16 = w32[k2 % 2], w16[k2 % 2]
                    W = kn * 128
                    if ii + 1 < len(items):
                        m_cast(ii + 1)
                    if ii + 2 < len(items):
                        m_dma(ii + 2)
                    for ti, (o, m) in enumerate(tls):
                        ps = psb[(k2 % 2) * 4 + ti]
                        for kk in range(kn):
                            P.op("pe", lambda e, ps=ps, kk=kk, o=o, m=m, wb16=wb16, koff=koff, kn=kn: e.matmul(
                                ps[:, 0:m], lhsT=wb16[:, kk * 128:(kk + 1) * 128], rhs=YIN[:, koff + kk, o:o + m],
                                start=(kk == 0), stop=(kk == kn - 1)), [wb16, YIN], [ps])
                        gr = grow[bi]
                        P.dma("sp", gr, gr[:, 0:m], self.z_g,
                              self.z_g.t[(bi * KC + d) * 128:(bi * KC + d + 1) * 128, t0 + o:t0 + o + m])
                        ac = accs[ti]
                        if bi == 0:
                            P.op("dve", lambda e, ps=ps, m=m, gr=gr, ac=ac: e.tensor_tensor(
                                out=ac[:, 0:m], in0=ps[:, 0:m], in1=gr[:, 0:m], op=ALU.mult), [ps, gr], [ac])
                        else:
                            P.op("dve", lambda e, ps=ps, m=m, gr=gr: e.tensor_tensor(
                                out=gr[:, 0:m], in0=ps[:, 0:m], in1=gr[:, 0:m], op=ALU.mult), [ps, gr], [gr])
                            if bi == 1:
                                P.op("pool", lambda e, m=m, gr=gr, ac=ac: e.tensor_tensor(
                                    out=ac[:, 0:m], in0=ac[:, 0:m], in1=gr[:, 0:m], op=ALU.add), [ac, gr], [ac])
                            else:
                                P.op("pool", lambda e, m=m, gr=gr, ac=ac, d=d, o=o: e.tensor_tensor(
                                    out=YY[:, d, o:o + m], in0=ac[:, 0:m], in1=gr[:, 0:m], op=ALU.add), [ac, gr], [YY])
            kcnt = [0]

            def epi(user, ti, o, m, ps, t0=t0, n=n):
                d = user
                j = kcnt[0] % 2
                kcnt[0] += 1
                hr, ob = hrow[j], o32[j]
                P.dma("sp", hr, hr[:, 0:m], hsrc, hsrc.t[d * 128:(d + 1) * 128, t0 + o:t0 + o + m])
                for (so, sn, grp) in c.segs(t0 + o, m):
                    P.op("dve", lambda e, ps=ps, so=so, sn=sn, grp=grp, d=d, hr=hr, ob=ob: e.scalar_tensor_tensor(
                        out=ob[:, so:so + sn], in0=ps[:, so:so + sn], scalar=gt[grp][:, d:d + 1],
                        in1=hr[:, so:so + sn], op0=ALU.mult, op1=ALU.add), [ps, gt[grp], hr], [ob])
                P.dma("sp", hdst, hdst.t[d * 128:(d + 1) * 128, t0 + o:t0 + o + m], ob, ob[:, 0:m])

            self.linear(S, YY, KC, self.wo[l], [(d, "fm", d) for d in range(KC)], n, w32, w16, psb, epi_fm=epi)


Main.merge = _merge


def _natt(self, l, last):
    P, c = self.P, self.cfg
    GW, SEQ, CTX, T, ROWS = c.GW, c.SEQ, c.CTX, c.T, c.ROWS
    nloc = 8 * GW
    NK = nloc + CTX
    NKC = NK // 128
    NTC = T // 128
    scale = 128.0 ** -0.5
    with P.scope() as S:
        qT = [S.sbuf([128, T], BF16, f"qT{j}") for j in range(2)]
        kT = [S.sbuf([128, T], BF16, f"kT{j}") for j in range(2)]
        VA = [S.sbuf([128, NTC * 128], BF16, f"VA{j}") for j in range(2)]
        VB = [S.sbuf([128, NTC * 128], BF16, f"VB{j}") for j in range(2)]
        bias = [S.sbuf([GW, 15 * GW], F32, f"bias{j}") for j in range(2)]
        yT = [S.sbuf([128, T], BF16, f"yT{j}") for j in range(2)]
        sc = [S.sbuf([128, NK], F32, f"sc{j}") for j in range(2)]
        pn = [S.sbuf([128, NK], BF16, f"pn{j}") for j in range(2)]
        st = [S.sbuf([128, 4], F32, f"st{j}") for j in range(2)]
        pT = [S.sbuf([128, NKC * 128], BF16, f"pT{j}") for j in range(2)]
        ps_s = [S.psum([128, 512], F32, f"pss{j}") for j in range(2)]
        ps_c = [S.psum([128, 512], F32, f"psc{j}") for j in range(2)]
        ps_t = [S.psum([128, 1024], BF16, f"pst{j}") for j in range(2)]
        ps_o = [S.psum([128, 512], F32, f"pso{j}") for j in range(2)]
        it = 0
        for h in range(c.NAH):
            hb = h % 2
            q, k, va, vb, bs, y = qT[hb], kT[hb], VA[hb], VB[hb], bias[hb], yT[hb]
            P.dma("sp", q, q[:, :], self.z_na, self.z_na.t[h * 128:(h + 1) * 128, :])
            P.dma("sp", k, k[:, :], self.z_na, self.z_na.t[(c.NAH + h) * 128:(c.NAH + h + 1) * 128, :])
            P.dma("sp", va, va[:, :].rearrange("p (c d) -> p c d", d=128), self.v_na,
                  self.v_na.t[:, h * 128:(h + 1) * 128].rearrange("(c p) d -> p c d", p=128))
            P.dma("sp", vb, vb[:, 0:(NTC - 1) * 128].rearrange("p (c d) -> p c d", d=128), self.v_na,
                  self.v_na.t[64:T - 64, h * 128:(h + 1) * 128].rearrange("(c p) d -> p c d", p=128))
            P.dma("sp", bs, bs[:, :], self.rpbm, self.rpbm.t[l * c.NAH + h, :, :])

            def softmax_pv(rows_n, nk, s_i, key_chunks, ydst, it):
                s_, p_, t_, pt = sc[s_i], pn[s_i], st[s_i], pT[s_i]
                P.op("dve", lambda e: e.tensor_reduce(out=t_[0:rows_n, 0:1], in_=s_[0:rows_n, 0:nk], axis=AX.X, op=ALU.max),
                     [s_], [t_])
                P.op("dve", lambda e: e.tensor_scalar(out=t_[0:rows_n, 1:2], in0=t_[0:rows_n, 0:1], scalar1=-1.0, scalar2=None,
                                                      op0=ALU.mult), [t_], [t_])
                P.op("act", lambda e: e.activation(out=s_[0:rows_n, 0:nk], in_=s_[0:rows_n, 0:nk], func=AF.Exp,
                                                   bias=t_[0:rows_n, 1:2], scale=1.0, accum_out=t_[0:rows_n, 2:3]),
                     [s_, t_], [s_, t_])
                P.op("dve", lambda e: e.reciprocal(out=t_[0:rows_n, 3:4], in_=t_[0:rows_n, 2:3]), [t_], [t_])
                P.op("dve", lambda e: e.tensor_scalar(out=p_[0:rows_n, 0:nk], in0=s_[0:rows_n, 0:nk], scalar1=t_[0:rows_n, 3:4],
                                                      scalar2=None, op0=ALU.mult), [s_, t_], [p_])
                nkc = nk // 128
                pst = ps_t[it % 2]
                for j in range(nkc):
                    P.op("pe", lambda e, j=j: e.transpose(pst[:, j * 128:j * 128 + rows_n], p_[0:rows_n, j * 128:(j + 1) * 128],
                                                          self.ident_bf[0:rows_n, 0:rows_n]), [p_, self.ident_bf], [pst])
                P.op("act", lambda e: e.activation(
                    out=pt[:, 0:nkc * 128].rearrange("p (j c) -> p j c", c=128)[:, :, 0:rows_n],
                    in_=pst[:, 0:nkc * 128].rearrange("p (j c) -> p j c", c=128)[:, :, 0:rows_n], func=AF.Copy), [pst], [pt])
                po = ps_o[it % 2]
                for j, vch in enumerate(key_chunks):
                    P.op("pe", lambda e, j=j, vch=vch: e.matmul(po[:, 0:rows_n], lhsT=vch(), rhs=pt[:, j * 128:j * 128 + rows_n],
                                                                start=(j == 0), stop=(j == nkc - 1)), [va, vb, pt], [po])
                P.op("act", lambda e: e.activation(out=ydst(), in_=po[:, 0:rows_n], func=AF.Copy), [po], [y])

            for r in range(ROWS):
                r0 = min(max(r - 4, 0), ROWS - 8)
                kst = r0 * GW
                s_i = it % 2
                pss, psc, s_ = ps_s[it % 2], ps_c[it % 2], sc[s_i]
                P.op("pe", lambda e, r=r, kst=kst, pss=pss: e.matmul(pss[0:GW, 0:nloc], lhsT=q[:, r * GW:(r + 1) * GW],
                                                                     rhs=k[:, kst:kst + nloc], start=True, stop=True), [q, k], [pss])
                P.op("pe", lambda e, r=r, psc=psc: e.matmul(psc[0:GW, 0:CTX], lhsT=q[:, r * GW:(r + 1) * GW],
                                                            rhs=k[:, SEQ:T], start=True, stop=True), [q, k], [psc])
                bo = (r0 - r + 7) * GW
                P.op("dve", lambda e, pss=pss, s_=s_, bo=bo: e.scalar_tensor_tensor(
                    out=s_[0:GW, 0:nloc], in0=pss[0:GW, 0:nloc], scalar=scale, in1=bs[:, bo:bo + nloc],
                    op0=ALU.mult, op1=ALU.add), [pss, bs], [s_])
                P.op("act", lambda e, psc=psc, s_=s_: e.activation(out=s_[0:GW, nloc:NK], in_=psc[0:GW, 0:CTX],
                                                                   func=AF.Copy, scale=scale), [psc], [s_])
                kch = []
                for j in range(nloc // 128):
                    tk = kst + j * 128
                    if tk % 128 == 0:
                        kch.append(lambda tk=tk: va[:, tk:tk + 128])
                    else:
                        kch.append(lambda tk=tk: vb[:, tk - 64:tk + 64])
                for j in range(CTX // 128):
                    kch.append(lambda j=j: va[:, SEQ + j * 128:SEQ + (j + 1) * 128])
                softmax_pv(GW, NK, s_i, kch, lambda r=r: y[:, r * GW:(r + 1) * GW], it)
                it += 1
            if not last:
                for qt in range(CTX // 128):
                    s_i = it % 2
                    psc, s_ = ps_c[it % 2], sc[s_i]
                    P.op("pe", lambda e, qt=qt, psc=psc: e.matmul(psc[:, 0:CTX], lhsT=q[:, SEQ + qt * 128:SEQ + (qt + 1) * 128],
                                                                  rhs=k[:, SEQ:T], start=True, stop=True), [q, k], [psc])
                    P.op("act", lambda e, psc=psc, s_=s_: e.activation(out=s_[:, 0:CTX], in_=psc[:, 0:CTX], func=AF.Copy,
                                                                       scale=scale), [psc], [s_])
                    kch = [(lambda j=j: va[:, SEQ + j * 128:SEQ + (j + 1) * 128]) for j in range(CTX // 128)]
                    softmax_pv(128, CTX, s_i, kch, lambda qt=qt: y[:, SEQ + qt * 128:SEQ + (qt + 1) * 128], it)
                    it += 1
            else:
                P.op("pool", lambda e: e.memset(y[:, SEQ:T], 0.0), [], [y])
            P.dma("sp", self.ycT, self.ycT.t[h * 128:(h + 1) * 128, :], y, y[:, :])


Main.natt = _natt


def _final_norm(self, hsrc):
    P, c = self.P, self.cfg
    KC, D = c.KC, c.D
    with P.scope() as S:
        hst = [S.sbuf([128, KC * 128], F32, f"hst{j}") for j in range(2)]
        sqt = S.sbuf([128, KC * 128], BF16, "sqt")
        rstd_b = S.sbuf([128, 128], F32, "rstd")
        ob = [S.sbuf([128, KC * 128], F32, f"ob{j}") for j in range(2)]
        pss = [S.psum([128, 512], F32, f"fps{j}") for j in range(2)]
        for si, (o, m) in enumerate(tiles_of(c.SEQ, 128)):
            hs, ot, ps = hst[si % 2], ob[si % 2], pss[si % 2]
            P.dma("sp", hs, hs[:, 0:KC * m].rearrange("p (k t) -> p k t", k=KC), hsrc,
                  hsrc.t[:, o:o + m].rearrange("(k p) t -> p k t", p=128))
            P.op("pool", lambda e, hs=hs, m=m: e.tensor_tensor(out=sqt[:, 0:KC * m], in0=hs[:, 0:KC * m],
                                                               in1=hs[:, 0:KC * m], op=ALU.mult), [hs], [sqt])
            for kc in range(KC):
                P.op("pe", lambda e, ps=ps, kc=kc, m=m: e.matmul(ps[:, 0:m], lhsT=self.ones_bf[:, :],
                                                                 rhs=sqt[:, kc * m:(kc + 1) * m], start=(kc == 0),
                                                                 stop=(kc == KC - 1)), [self.ones_bf, sqt], [ps])
            P.op("act", lambda e, ps=ps, m=m: e.activation(out=rstd_b[:, 0:m], in_=ps[:, 0:m], func=AF.Sqrt,
                                                           bias=self.eps_ap(), scale=1.0 / D), [ps, self.csb], [rstd_b])
            P.op("dve", lambda e, m=m: e.reciprocal(out=rstd_b[:, 0:m], in_=rstd_b[:, 0:m]), [rstd_b], [rstd_b])
            for kc in range(KC):
                P.op("dve", lambda e, hs=hs, ot=ot, kc=kc, m=m: e.scalar_tensor_tensor(
                    out=ot[:, kc * m:(kc + 1) * m], in0=hs[:, kc * m:(kc + 1) * m], scalar=self.fnsb[:, kc:kc + 1],
                    in1=rstd_b[:, 0:m], op0=ALU.mult, op1=ALU.mult), [hs, self.fnsb, rstd_b], [ot])
            P.dma("sp", self.outT, self.outT.t[:, o:o + m].rearrange("(k p) t -> p k t", p=128),
                  ot, ot[:, 0:KC * m].rearrange("p (k t) -> p k t", k=KC), is_output=True)


Main.final_norm = _final_norm


def _dn_prep(self, l):
    P, c = self.P, self.cfg
    T, SEQ, H = c.T, c.SEQ, c.DNH
    NCK = c.NCK
    with P.scope() as S:
        cos = S.sbuf([128, SEQ], F32, "cos")
        sin = S.sbuf([128, SEQ], F32, "sin")
        P.dma("sp", cos, cos[:, :], self.ropec, self.ropec.t[:, :])
        P.dma("sp", sin, sin[:, :], self.ropes, self.ropes.t[:, :])
        cw = S.sbuf([128, 3 * H * 5], F32, "cw")
        P.dma("sp", cw, cw[:, :], self.convw, self.convw.t[:, l * 3 * H * 5:(l + 1) * 3 * H * 5])
        X = [S.sbuf([128, T], F32, f"X{j}") for j in range(2)]
        A = [S.sbuf([128, T], F32, f"A{j}") for j in range(2)]
        tm = [S.sbuf([64, NCK * 128], F32, f"tm{j}") for j in range(2)]
        t1 = [S.sbuf([128, 512], F32, f"t1{j}") for j in range(2)]
        t2 = [S.sbuf([128, 512], F32, f"t2{j}") for j in range(2)]
        psn = [S.psum([128, 512], F32, f"psn{j}") for j in range(2)]
        psr = [S.psum([128, 512], F32, f"psr{j}") for j in range(2)]
        pst = [S.psum([128, 512], F32, f"pst{j}") for j in range(2)]
        ones32 = self.csb[:, C_ONES:C_ONES + 128]
        perm = self.csb[:, C_PERM:C_PERM + 128]
        id32 = self.csb[:, C_ID:C_ID + 128]
        it = 0
        tcount = 0
        for h in range(H):
            for kind in range(3):
                ch = kind * H + h
                x, a = X[it % 2], A[it % 2]
                it += 1
                P.dma("sp", x, x[:, :], self.z_dn, self.z_dn.t[ch * 128:(ch + 1) * 128, :])
                wc = lambda j, ch=ch: cw[:, ch * 5 + j:ch * 5 + j + 1]
                for (r0, r1) in ((0, SEQ), (SEQ, T)):
                    P.op("act", lambda e: e.activation(out=a[:, r0:r1], in_=x[:, r0:r1], func=AF.Copy, scale=wc(2)),
                         [x, cw], [a])
                    for j, (d0, d1, s0, s1) in ((0, (r0 + 2, r1, r0, r1 - 2)), (1, (r0 + 1, r1, r0, r1 - 1)),
                                                (3, (r0, r1 - 1, r0 + 1, r1)), (4, (r0, r1 - 2, r0 + 2, r1))):
                        P.op("dve", lambda e: e.scalar_tensor_tensor(out=a[:, d0:d1], in0=x[:, s0:s1], scalar=wc(j),
                                                                     in1=a[:, d0:d1], op0=ALU.mult, op1=ALU.add),
                             [x, cw, a], [a])
                P.op("act", lambda e: e.activation(out=a[:, :], in_=a[:, :], func=AF.Silu), [a], [a])
                if kind != 1:
                    for (o, m) in tiles_of(T, 512):
                        k2 = tcount % 2
                        tcount += 1
                        ta, tb, pn_, pr_ = t1[k2], t2[k2], psn[k2], psr[k2]
                        P.op("pool", lambda e: e.tensor_tensor(out=ta[:, 0:m], in0=a[:, o:o + m], in1=a[:, o:o + m],
                                                               op=ALU.mult), [a], [ta])
                        P.op("pe", lambda e: e.matmul(pn_[:, 0:m], lhsT=ones32, rhs=ta[:, 0:m], start=True, stop=True),
                             [self.csb, ta], [pn_])
                        P.op("act", lambda e: e.activation(out=tb[:, 0:m], in_=pn_[:, 0:m], func=AF.Sqrt,
                                                           bias=self.eps_ap(), scale=1.0), [pn_, self.csb], [tb])
                        P.op("dve", lambda e: e.reciprocal(out=tb[:, 0:m], in_=tb[:, 0:m]), [tb], [tb])
                        P.op("dve", lambda e: e.tensor_tensor(out=a[:, o:o + m], in0=a[:, o:o + m], in1=tb[:, 0:m],
                                                              op=ALU.mult), [a, tb], [a])
                        if o < SEQ:
                            mm_ = min(m, SEQ - o)
                            P.op("pe", lambda e: e.matmul(pr_[:, 0:mm_], lhsT=perm, rhs=a[:, o:o + mm_], start=True, stop=True),
                                 [self.csb, a], [pr_])
                            P.op("dve", lambda e: e.tensor_tensor(out=ta[:, 0:mm_], in0=a[:, o:o + mm_], in1=cos[:, o:o + mm_],
                                                                  op=ALU.mult), [a, cos], [ta])
                            P.op("dve", lambda e: e.tensor_tensor(out=tb[:, 0:mm_], in0=pr_[:, 0:mm_], in1=sin[:, o:o + mm_],
                                                                  op=ALU.mult), [pr_, sin], [tb])
                            P.op("pool", lambda e: e.tensor_tensor(out=a[:, o:o + mm_], in0=ta[:, 0:mm_], in1=tb[:, 0:mm_],
                                                                   op=ALU.add), [ta, tb], [a])
                if kind == 0:
                    P.dma("sp", self.dn_kT, self.dn_kT.t[h * 128:(h + 1) * 128, :], a, a[:, :])
                if kind == 2:
                    P.dma("sp", self.dn_qT, self.dn_qT.t[h * 128:(h + 1) * 128, :], a, a[:, :])
                if kind in (0, 1):
                    tmb = tm[kind]
                    for g0 in range(0, NCK, 4):
                        gn = min(4, NCK - g0)
                        pt = pst[(g0 // 4) % 2]
                        for j in range(gn):
                            ck = g0 + j
                            P.op("pe", lambda e: e.transpose(pt[0:64, j * 128:(j + 1) * 128], a[:, ck * 64:(ck + 1) * 64], id32),
                                 [a, self.csb], [pt])
                        P.op("act", lambda e: e.activation(out=tmb[:, g0 * 128:(g0 + gn) * 128], in_=pt[0:64, 0:gn * 128],
                                                           func=AF.Copy), [pt], [tmb])
                    dst = self.dn_ktm if kind == 0 else self.dn_vtm
                    P.dma("sp", dst, dst.t[:, h * 128:(h + 1) * 128].rearrange("(c p) d -> p c d", p=64),
                          tmb, tmb[:, :].rearrange("p (c d) -> p c d", d=128))


Main.dn_prep = _dn_prep


def _dn_scan(self, l):
    P, c = self.P, self.cfg
    T, SEQ, H, NCK = c.T, c.SEQ, c.DNH, c.NCK
    H2 = 2 * H
    sq = 128.0 ** -0.5
    lns = float(np.log(sq))
    csb = self.csb
    ones64 = csb[0:64, C_ONES:C_ONES + 64]
    ones64w = csb[0:64, C_ONES:C_ONES + 128]
    neg64 = csb[0:64, C_NEG1:C_NEG1 + 64]
    id64 = csb[0:64, C_ID:C_ID + 64]
    CUM = [csb[0:64, C_CUMF:C_CUMF + 64], csb[0:64, C_CUMB:C_CUMB + 64]]
    MS = [csb[0:64, C_MSF:C_MSF + 64], csb[0:64, C_MSB:C_MSB + 64]]
    MIT = [csb[0:64, C_MITF:C_MITF + 64], csb[0:64, C_MITB:C_MITB + 64]]
    nlat = SEQ // 64
    lat = list(range(nlat))
    ctx = list(range(nlat, NCK))
    order = [ctx + lat, ctx[::-1] + lat[::-1]]
    with P.scope() as S:
        gall = S.sbuf([64, NCK * H2], F32, "gall")
        ball = S.sbuf([64, NCK * H2], F32, "ball")
        with P.scope() as S2:
            ab = S2.sbuf([64, NCK * 2 * H2], F32, "ab")
            ab3 = ab[:, :].rearrange("p (c n) -> p c n", n=2 * H2)
            P.dma("sp", ab, ab3, self.z_ab, self.z_ab.t[:, 0:2 * H2].rearrange("(c p) n -> p c n", p=64))
            dtb = S2.sbuf([64, NCK * H2], F32, "dtb")
            nal = S2.sbuf([64, NCK * H2], F32, "nal")
            P.dma("sp", dtb, dtb[:, :], self.dnc, self.dnc.t[:, (l * 2) * NCK * H2:(l * 2 + 1) * NCK * H2])
            P.dma("sp", nal, nal[:, :], self.dnc, self.dnc.t[:, (l * 2 + 1) * NCK * H2:(l * 2 + 2) * NCK * H2])
            g3 = gall[:, :].rearrange("p (c n) -> p c n", n=H2)
            b3 = ball[:, :].rearrange("p (c n) -> p c n", n=H2)
            d3 = dtb[:, :].rearrange("p (c n) -> p c n", n=H2)
            P.op("dve", lambda e: e.tensor_tensor(out=g3, in0=ab3[:, :, 0:H2], in1=d3, op=ALU.add), [ab, dtb], [gall])
            P.op("act", lambda e: e.activation(out=gall[:, :], in_=gall[:, :], func=AF.Exp), [gall], [gall])
            P.op("act", lambda e: e.activation(out=gall[:, :], in_=gall[:, :], func=AF.Ln, bias=self.one_ap(64), scale=1.0),
                 [gall, csb], [gall])
            P.op("act", lambda e: e.activation(out=nal[:, :], in_=nal[:, :], func=AF.Exp), [nal], [nal])
            P.op("dve", lambda e: e.scalar_tensor_tensor(out=gall[:, :], in0=gall[:, :], scalar=-1.0, in1=nal[:, :],
                                                         op0=ALU.mult, op1=ALU.mult), [gall, nal], [gall])
            P.op("act", lambda e: e.activation(out=b3, in_=ab3[:, :, H2:2 * H2], func=AF.Sigmoid), [ab], [ball])
        nwb = S.sbuf([64, 128], F32, "nwb")
        P.dma("sp", nwb, nwb[:, :], self.dnnw, self.dnnw.t[:, l * 128:(l + 1) * 128])
        kT = S.sbuf([128, T], F32, "kT")
        qT = S.sbuf([128, T], F32, "qT")
        ktm = S.sbuf([64, NCK * 128], F32, "ktm")
        vtm = S.sbuf([64, NCK * 128], F32, "vtm")
        O = S.sbuf([64, NCK * 128], F32, "O")
        ysb = S.sbuf([128, T], BF16, "ysb")
        ms = S.sbuf([64, NCK], F32, "ms")
        junk2 = [S.sbuf([64, 128], F32, f"junk{j}") for j in range(2)]
        Sst = [S.sbuf([128, 128], F32, f"S{d}") for d in range(2)]
        banks = [S.psum([128, 512], F32, f"bk{j}") for j in range(8)]

        def view(bank, rows, c0, c1, name):
            return _alias(banks[bank], banks[bank][0:rows, c0:c1])

        def mk(d, j):
            t = {}
            for nm, shp in (("gcum", [64, 64]), ("decS", [64, 64]), ("decIT", [64, 64]), ("N0", [64, 64]), ("qkT", [64, 64]),
                            ("sm", [128, 8]), ("Y0", [64, 256]), ("Y1", [64, 256]), ("kd", [64, 128]),
                            ("PA", [64, 64]), ("PB", [64, 64]), ("TA", [64, 64]), ("TB", [64, 64]), ("wT", [128, 64])):
                t[nm] = S.sbuf(shp, F32, f"{nm}{d}{j}")
            return t

        tmp = [[mk(d, j) for j in range(2)] for d in range(2)]
        vn = [S.sbuf([64, 128], F32, f"vn{d}") for d in range(2)]
        t3 = [S.sbuf([64, 128], F32, f"t3{d}") for d in range(2)]
        ot = [S.sbuf([64, 128], F32, f"ot{d}") for d in range(2)]
        pv = []
        for d in range(2):
            b0 = d * 3
            pv.append({
                "diff": view(b0, 64, 0, 64, f"diff{d}"), "KK": view(b0, 64, 64, 128, f"KK{d}"),
                "QK": view(b0, 64, 128, 192, f"QK{d}"), "tp0": view(b0, 64, 192, 256, f"tp0{d}"),
                "tp1": view(b0, 64, 256, 320, f"tp1{d}"), "wTp": view(b0, 128, 320, 384, f"wTp{d}"),
                "gc": view(b0, 64, 384, 386, f"gc{d}"), "gt": view(b0, 128, 386, 388, f"gt{d}"),
                "ap0": view(b0 + 1, 64, 0, 256, f"ap0{d}"), "ap1": view(b0 + 1, 64, 256, 512, f"ap1{d}"),
                "ps1": view(b0 + 2, 64, 0, 128, f"ps1{d}"), "ps2": view(b0 + 2, 64, 128, 256, f"ps2{d}"),
                "ps3": view(b0 + 2, 64, 256, 384, f"ps3{d}"), "ps4": view(b0 + 2, 128, 384, 512, f"ps4{d}"),
            })
        ytp = [banks[6], banks[7]]

        PRE_N = int(os.environ.get("PRE_N", "1000"))
        pcount = [0]

        def pop(*a, **kw):
            pcount[0] += 1
            if pcount[0] <= PRE_N:
                return P.op(*a, **kw)

        def pre(h, d, ck, t):
            pcount[0] = 0
            p = pv[d]
            gi = ck * H2 + d * H + h
            gcol = gall[:, gi:gi + 1]
            bcol = ball[:, gi:gi + 1]
            kc_ = kT[:, ck * 64:(ck + 1) * 64]
            qc_ = qT[:, ck * 64:(ck + 1) * 64]
            pop("dve", lambda e: e.tensor_scalar(out=t["gcum"][:, :], in0=CUM[d], scalar1=gcol, scalar2=None, op0=ALU.mult),
                 [csb, gall], [t["gcum"]])
            pop("pe", lambda e: e.matmul(p["diff"][:, :], lhsT=t["gcum"][:, :], rhs=ones64, start=True, stop=False),
                 [t["gcum"], csb], [p["diff"]])
            pop("pe", lambda e: e.matmul(p["diff"][:, :], lhsT=neg64, rhs=t["gcum"][:, :], start=False, stop=True),
                 [t["gcum"], csb], [p["diff"]])
            pop("pe", lambda e: e.matmul(p["gc"][:, 0:1], lhsT=CUM[d], rhs=gcol, start=True, stop=True), [csb, gall], [p["gc"]])
            pop("pe", lambda e: e.matmul(p["gt"][:, 0:1], lhsT=ones64w, rhs=gcol, start=True, stop=True), [csb, gall], [p["gt"]])
            pop("dve", lambda e: e.tensor_tensor(out=t["decS"][:, :], in0=p["diff"][:, :], in1=MS[d], op=ALU.add),
                 [p["diff"], csb], [t["decS"]])
            pop("act", lambda e: e.activation(out=t["decS"][:, :], in_=t["decS"][:, :], func=AF.Exp), [t["decS"]], [t["decS"]])
            pop("dve", lambda e: e.scalar_tensor_tensor(out=t["decIT"][:, :], in0=p["diff"][:, :], scalar=-1.0, in1=MIT[d],
                                                         op0=ALU.mult, op1=ALU.add), [p["diff"], csb], [t["decIT"]])
            pop("act", lambda e: e.activation(out=t["decIT"][:, :], in_=t["decIT"][:, :], func=AF.Exp), [t["decIT"]], [t["decIT"]])
            pop("pe", lambda e: e.matmul(p["KK"][:, :], lhsT=kc_, rhs=kc_, start=True, stop=True), [kT], [p["KK"]])
            pop("pe", lambda e: e.matmul(p["QK"][:, :], lhsT=kc_, rhs=qc_, start=True, stop=True), [kT, qT], [p["QK"]])
            pop("dve", lambda e: e.scalar_tensor_tensor(out=t["N0"][:, :], in0=p["KK"][:, :], scalar=bcol, in1=t["decS"][:, :],
                                                         op0=ALU.mult, op1=ALU.mult), [p["KK"], ball, t["decS"]], [t["N0"]])
            pop("dve", lambda e: e.tensor_tensor(out=t["qkT"][:, :], in0=p["QK"][:, :], in1=t["decIT"][:, :], op=ALU.mult),
                 [p["QK"], t["decIT"]], [t["qkT"]])
            sm = t["sm"]
            pop("act", lambda e: e.activation(out=sm[:, 0:1], in_=p["gt"][:, 0:1], func=AF.Copy), [p["gt"]], [sm])
            pop("act", lambda e: e.activation(out=sm[0:64, 5:6], in_=p["gc"][:, 0:1], func=AF.Copy), [p["gc"]], [sm])
            pop("act", lambda e: e.activation(out=sm[:, 1:2], in_=sm[:, 0:1], func=AF.Exp), [sm], [sm])
            pop("act", lambda e: e.activation(out=sm[0:64, 2:3], in_=sm[0:64, 5:6], func=AF.Exp), [sm], [sm])
            pop("act", lambda e: e.activation(out=sm[0:64, 3:4], in_=sm[0:64, 5:6], func=AF.Exp, bias=sm[0:64, 0:1],
                                              scale=-1.0), [sm], [sm])
            pop("act", lambda e: e.activation(out=sm[0:64, 4:5], in_=sm[0:64, 5:6], func=AF.Exp, bias=self.lns_ap(64),
                                              scale=1.0), [sm, csb], [sm])
            Y = t["Y0"]
            pop("dve", lambda e: e.tensor_scalar(out=Y[:, 0:128], in0=vtm[:, ck * 128:(ck + 1) * 128], scalar1=bcol,
                                                  scalar2=None, op0=ALU.mult), [vtm, ball], [Y])
            pop("dve", lambda e: e.tensor_scalar(out=Y[:, 128:256], in0=ktm[:, ck * 128:(ck + 1) * 128], scalar1=bcol,
                                                  scalar2=sm[0:64, 2:3], op0=ALU.mult, op1=ALU.mult), [ktm, ball, sm], [Y])
            pop("act", lambda e: e.activation(out=t["kd"][:, :], in_=ktm[:, ck * 128:(ck + 1) * 128], func=AF.Copy,
                                               scale=sm[0:64, 3:4]), [ktm, sm], [t["kd"]])
            Pc, PTc = t["N0"], t["TA"]
            pop("pe", lambda e: e.transpose(p["tp0"][:, :], t["N0"][:, :], id64), [t["N0"], csb], [p["tp0"]])
            pop("act", lambda e: e.activation(out=PTc[:, :], in_=p["tp0"][:, :], func=AF.Copy), [p["tp0"]], [PTc])
            Ys = [t["Y0"], t["Y1"]]
            yi = 0
            pop("pe", lambda e: e.matmul(p["ap0"][:, :], lhsT=PTc[:, :], rhs=Ys[0][:, :], start=True, stop=True),
                 [PTc, Ys[0]], [p["ap0"]])
            pop("dve", lambda e: e.tensor_tensor(out=Ys[1][:, :], in0=Ys[0][:, :], in1=p["ap0"][:, :], op=ALU.subtract),
                 [Ys[0], p["ap0"]], [Ys[1]])
            yi = 1
            pbufs = [(t["PA"], t["TB"]), (t["PB"], t["TA"])]
            for k in range(5):
                Pn, PTn = pbufs[k % 2]
                if k % 2 == 1:
                    PTn = t["TA"]
                pop("pe", lambda e: e.matmul(p["tp0"][:, :], lhsT=PTc[:, :], rhs=Pc[:, :], start=True, stop=True),
                     [PTc, Pc], [p["tp0"]])
                pop("pe", lambda e: e.matmul(p["tp1"][:, :], lhsT=Pc[:, :], rhs=PTc[:, :], start=True, stop=True),
                     [PTc, Pc], [p["tp1"]])
                Pn = t["PA"] if Pc is not t["PA"] else t["PB"]
                PTn = t["TB"] if PTc is not t["TB"] else t["TA"]
                pop("act", lambda e: e.activation(out=Pn[:, :], in_=p["tp0"][:, :], func=AF.Copy), [p["tp0"]], [Pn])
                pop("act", lambda e: e.activation(out=PTn[:, :], in_=p["tp1"][:, :], func=AF.Copy), [p["tp1"]], [PTn])
                Pc, PTc = Pn, PTn
                apx = p["ap1"] if k % 2 == 0 else p["ap0"]
                Yc, Yn = Ys[yi], Ys[1 - yi]
                pop("pe", lambda e: e.matmul(apx[:, :], lhsT=PTc[:, :], rhs=Yc[:, :], start=True, stop=True), [PTc, Yc], [apx])
                pop("dve", lambda e: e.tensor_tensor(out=Yn[:, :], in0=Yc[:, :], in1=apx[:, :], op=ALU.add), [Yc, apx], [Yn])
                yi = 1 - yi
            Yf = Ys[yi]
            pop("pe", lambda e: e.transpose(p["wTp"][:, :], Yf[:, 128:256], id64), [Yf, csb], [p["wTp"]])
            pop("act", lambda e: e.activation(out=t["wT"][:, :], in_=p["wTp"][:, :], func=AF.Copy), [p["wTp"]], [t["wT"]])
            return Yf

        SEQ_N = int(os.environ.get("SEQ_N", "1000"))
        scount = [0]

        def sop(*a, **kw):
            scount[0] += 1
            if scount[0] <= SEQ_N:
                return P.op(*a, **kw)

        def seq(h, d, ck, t, Yf):
            scount[0] = 0
            p = pv[d]
            Sd = Sst[d]
            sm = t["sm"]
            qc_ = qT[:, ck * 64:(ck + 1) * 64]
            sop("pe", lambda e: e.matmul(p["ps1"][:, :], lhsT=t["wT"][:, :], rhs=Sd[:, :], start=True, stop=True),
                 [t["wT"], Sd], [p["ps1"]])
            sop("dve", lambda e: e.tensor_tensor(out=vn[d][:, :], in0=Yf[:, 0:128], in1=p["ps1"][:, :], op=ALU.subtract),
                 [Yf, p["ps1"]], [vn[d]])
            sop("pe", lambda e: e.matmul(p["ps2"][:, :], lhsT=qc_, rhs=Sd[:, :], start=True, stop=True), [qT, Sd], [p["ps2"]])
            sop("pe", lambda e: e.matmul(p["ps3"][:, :], lhsT=t["qkT"][:, :], rhs=vn[d][:, :], start=True, stop=True),
                 [t["qkT"], vn[d]], [p["ps3"]])
            sop("pe", lambda e: e.matmul(p["ps4"][:, :], lhsT=t["kd"][:, :], rhs=vn[d][:, :], start=True, stop=True),
                 [t["kd"], vn[d]], [p["ps4"]])
            sop("dve", lambda e: e.scalar_tensor_tensor(out=Sd[:, :], in0=Sd[:, :], scalar=sm[:, 1:2], in1=p["ps4"][:, :],
                                                         op0=ALU.mult, op1=ALU.add), [Sd, sm, p["ps4"]], [Sd])
            sop("dve", lambda e: e.tensor_scalar(out=t3[d][:, :], in0=p["ps3"][:, :], scalar1=sq, scalar2=None, op0=ALU.mult),
                [p["ps3"]], [t3[d]])
            sop("dve", lambda e: e.scalar_tensor_tensor(out=ot[d][:, :], in0=p["ps2"][:, :], scalar=sm[0:64, 4:5], in1=t3[d][:, :],
                                                         op0=ALU.mult, op1=ALU.add), [p["ps2"], sm, t3[d]], [ot[d]])
            sop("pool", lambda e: e.tensor_tensor(out=O[:, ck * 128:(ck + 1) * 128], in0=O[:, ck * 128:(ck + 1) * 128],
                                                   in1=ot[d][:, :], op=ALU.add), [O, ot[d]], [O])

        dz = [S.sbuf([64, 128], F32, f"dz{j}") for j in range(2)]
        yb = [S.sbuf([64, 128], F32, f"yb{j}") for j in range(2)]
        DBG = float(os.environ.get("DN_DBG", "9"))
        for h in range(H if DBG > 1 else 0):
            P.dma("sp", kT, kT[:, :], self.dn_kT, self.dn_kT.t[h * 128:(h + 1) * 128, :])
            P.dma("sp", qT, qT[:, :], self.dn_qT, self.dn_qT.t[h * 128:(h + 1) * 128, :])
            P.dma("sp", ktm, ktm[:, :].rearrange("p (c d) -> p c d", d=128), self.dn_ktm,
                  self.dn_ktm.t[:, h * 128:(h + 1) * 128].rearrange("(c p) d -> p c d", p=64))
            P.dma("sp", vtm, vtm[:, :].rearrange("p (c d) -> p c d", d=128), self.dn_vtm,
                  self.dn_vtm.t[:, h * 128:(h + 1) * 128].rearrange("(c p) d -> p c d", p=64))
            P.op("pool", lambda e: e.memset(O[:, :], 0.0), [], [O])
            for d in range(2):
                P.op("pool", lambda e: e.memset(Sst[d][:, :], 0.0), [], [Sst[d]])
            Yfs = [[None, None], [None, None]]
            for d in range(2 if DBG > 1.7 else 0):
                Yfs[d][0] = pre(h, d, order[d][0], tmp[d][0])
            for step in range(min(NCK, int(os.environ.get('NSTEP', '1000'))) if DBG > 2 else 0):
                j = step % 2
                if step + 1 < NCK:
                    for d in range(2):
                        Yfs[d][1 - j] = pre(h, d, order[d][step + 1], tmp[d][1 - j])
                for d in range(2):
                    seq(h, d, order[d][step], tmp[d][j], Yfs[d][j])
            if self.dbg:
                P.dma("sp", self.dbgO, self.dbgO.t[:, :], O, O[:, :])
                P.dma("sp", self.dbgS, self.dbgS.t[:, 0:128], Sst[0], Sst[0][:, :])
                P.dma("sp", self.dbgS, self.dbgS.t[:, 128:256], Sst[1], Sst[1][:, :])
            if DBG < 4:
                continue
            for ck in range(NCK):
                jk = junk2[ck % 2]
                P.op("act", lambda e: e.activation(out=jk[:, :], in_=O[:, ck * 128:(ck + 1) * 128], func=AF.Square), [O], [jk])
                P.op("dve", lambda e: e.tensor_reduce(out=ms[:, ck:ck + 1], in_=jk[:, :], axis=AX.X, op=ALU.add), [jk], [ms])
            P.op("act", lambda e: e.activation(out=ms[:, :], in_=ms[:, :], func=AF.Sqrt, bias=csb[0:64, C_EPS:C_EPS + 1],
                                               scale=1.0 / 128), [ms, csb], [ms])
            P.op("dve", lambda e: e.reciprocal(out=ms[:, :], in_=ms[:, :]), [ms], [ms])
            for g0 in range(0, NCK if DBG > 5 else 0, 8):
                gn = min(8, NCK - g0)
                pt = ytp[(g0 // 8) % 2]
                for jj in range(gn):
                    ck = g0 + jj
                    dzb, ybb = dz[ck % 2], yb[ck % 2]
                    P.dma("sp", dzb, dzb[:, :], self.z_dg, self.z_dg.t[ck * 64:(ck + 1) * 64, h * 128:(h + 1) * 128])
                    P.op("dve", lambda e: e.scalar_tensor_tensor(out=ybb[:, :], in0=O[:, ck * 128:(ck + 1) * 128],
                                                                 scalar=ms[:, ck:ck + 1], in1=nwb[:, :], op0=ALU.mult,
                                                                 op1=ALU.mult), [O, ms, nwb], [ybb])
                    P.op("pool", lambda e: e.tensor_tensor(out=ybb[:, :], in0=ybb[:, :], in1=dzb[:, :], op=ALU.mult),
                         [ybb, dzb], [ybb])
                    if DBG > 6:
                        P.op("pe", lambda e: e.transpose(pt[:, jj * 64:(jj + 1) * 64], ybb[:, :], id64), [ybb, csb], [pt])
                if DBG > 7:
                    P.op("act", lambda e: e.activation(out=ysb[:, g0 * 64:(g0 + gn) * 64], in_=pt[:, 0:gn * 64], func=AF.Copy),
                         [pt], [ysb])
            if DBG > 7:
                P.dma("sp", self.ybT, self.ybT.t[h * 128:(h + 1) * 128, :], ysb, ysb[:, :])


Main.dn_scan = _dn_scan


class _V:
    def __init__(self, buf, rows, c0, c1):
        self.ap = buf[0:rows, c0:c1]

    def __getitem__(self, idx):
        return self.ap[idx]


C_ONE1 = C_ONES


def _one_ap(self, rows):
    return self.csb[0:rows, C_ONES:C_ONES + 1]


def _lns_ap(self, rows):
    return self.csb[0:rows, C_EPS + 1:C_EPS + 2]


Main.one_ap = _one_ap
Main.lns_ap = _lns_ap


def _mod_phase(self):
    P, c = self.P, self.cfg
    KC = c.KC
    NQ = 9 * KC
    with P.scope() as S:
        craw = S.sbuf([128, KC * 2], F32, "craw")
        cs = S.sbuf([128, KC * 2], F32, "cs")
        P.dma("sp", craw, craw[:, :], self.cT, self.cT.t[:, :])
        P.op("act", lambda e: e.activation(out=cs[:, :], in_=craw[:, :], func=AF.Silu), [craw], [cs])
        bsb = S.sbuf([128, c.L * NQ], F32, "adab")
        P.dma("sp", bsb, bsb[:, :], self.adabT, self.adabT.t[:, :])
        wt = [S.sbuf([128, KC * 128], F32, f"aw{j}") for j in range(3)]
        ps = [S.psum([128, 512], F32, f"mps{j}") for j in range(4)]
        it = 0
        for l in range(self.nl):
            mv = self.modsb[:, l * 2 * NQ:(l + 1) * 2 * NQ].rearrange("p (g q) -> p g q", g=2)
            for q in range(NQ):
                w = wt[it % 3]
                p_ = ps[it % 4]
                it += 1
                P.dma("sp", w, w[:, :], self.adaw[l], self.adaw[l].t[q, :, :])
                for kc in range(KC):
                    P.op("pe", lambda e: e.matmul(p_[:, 0:2], lhsT=w[:, kc * 128:(kc + 1) * 128], rhs=cs[:, kc * 2:(kc + 1) * 2],
                                                  start=(kc == 0), stop=(kc == KC - 1)), [w, cs], [p_])
                P.op("dve", lambda e: e.tensor_scalar(out=mv[:, :, q], in0=p_[:, 0:2], scalar1=bsb[:, l * NQ + q:l * NQ + q + 1],
                                                      scalar2=None, op0=ALU.add), [p_, bsb], [self.modsb])


Main.mod_phase = _mod_phase


def _build(self):
    P, c = self.P, self.cfg
    L, KC, FC, T = c.L, c.KC, c.FC, c.T
    with self.stack:
        self.xT = self.din("xT", [c.D, T])
        self.cT = self.din("cT", [128, KC * 2])
        self.adaw = [self.din(f"adaw_{l}", [9 * KC, 128, KC * 128]) for l in range(L)]
        self.adabT = self.din("adabT", [128, L * 9 * KC])
        self.normT = self.din("normT", [128, L * 3 * KC])
        self.fnT = self.din("fnT", [128, KC])
        self.w13 = [[self.din(f"w13_{l}_{i}", [2 * FC, 128, KC * 128]) for i in range(2)] for l in range(L)]
        self.w2 = [[self.din(f"w2_{l}_{i}", [KC, 128, FC * 128]) for i in range(2)] for l in range(L)]
        self.win = [self.din(f"win_{l}", [c.NCH, 128, KC * 128]) for l in range(L)]
        self.pa = [self.din(f"pa_{l}", [KC, 128, c.GMW]) for l in range(L)]
        self.pb = [self.din(f"pb_{l}", [KC, 128, c.DNW]) for l in range(L)]
        self.pc = [self.din(f"pc_{l}", [KC, 128, c.NAW]) for l in range(L)]
        self.wo = [self.din(f"wo_{l}", [KC, 128, KC * 128]) for l in range(L)]
        self.sguT = self.din("sguT", [128, L * c.GMG * 128])
        self.sgub = self.din("sgub", [128, L * c.GMG * 128])
        self.sgunw = self.din("sgunw", [128, L * c.GMW])
        self.rpbm = self.din("rpbm", [L * c.NAH, c.GW, 15 * c.GW])
        self.ropec = self.din("ropec", [128, c.SEQ])
        self.ropes = self.din("ropes", [128, c.SEQ])
        self.convw = self.din("convw", [128, L * 3 * c.DNH * 5])
        self.dnc = self.din("dnc", [64, L * 2 * c.NCK * 2 * c.DNH])
        self.dnnw = self.din("dnnw", [64, L * 128])
        self.hs = [self.dscr(f"h{j}", [c.D, T]) for j in range(3 * L)]
        self.gT = self.dscr("gT", [c.FF, T], BF16)
        self.z_dn = self.dscr("z_dn", [3 * c.DNW, T])
        self.z_na = self.dscr("z_na", [2 * c.NAW, T], BF16)
        self.z_gu = self.dscr("z_gu", [c.GMW, T])
        self.z_g = self.dscr("z_g", [3 * c.D, T])
        self.v_na = self.dscr("v_na", [T, c.NAW], BF16)
        self.z_dg = self.dscr("z_dg", [T, c.DNW])
        self.z_gv = self.dscr("z_gv", [T, c.GMW])
        self.z_ab = self.dscr("z_ab", [T, 128])
        self.yaT = self.dscr("yaT", [c.GMW, T], BF16)
        self.ybT = self.dscr("ybT", [c.DNW, T], BF16)
        self.ycT = self.dscr("ycT", [c.NAW, T], BF16)
        self.dn_kT = self.dscr("dn_kT", [c.DNW, T])
        self.dn_qT = self.dscr("dn_qT", [c.DNW, T])
        self.dn_ktm = self.dscr("dn_ktm", [T, c.DNW])
        self.dn_vtm = self.dscr("dn_vtm", [T, c.DNW])
        if self.dbg:
            self.dbgO = self.dscr("dbgO", [64, c.NCK * 128])
            self.dbgS = self.dscr("dbgS", [128, 256])
        ap = self.nc.dram_tensor("outT", [c.D, c.SEQ], F32, kind="ExternalOutput").ap()
        self.outT = Buf(P, ap, "outT")
        self.load_consts()
        self.modsb = P.sbuf([128, L * 2 * 9 * KC], F32, "modsb")
        self.mod_phase()
        self.normsb = P.sbuf([128, L * 3 * KC], F32, "normsb")
        P.dma("sp", self.normsb, self.normsb[:, :], self.normT, self.normT.t[:, :])
        self.fnsb = P.sbuf([128, KC], F32, "fnsb")
        P.dma("sp", self.fnsb, self.fnsb[:, :], self.fnT, self.fnT.t[:, :])
        h = self.xT
        stop = self.stop_after
        done = False
        for l in range(self.nl):
            last = (l == c.L - 1)
            self.ffn(l, 0, h, self.hs[3 * l]); h = self.hs[3 * l]
            if stop == (l, "ffn0"): break
            self.inproj(l, h)
            if stop == (l, "inproj"): break
            self.gmlp(l)
            if stop == (l, "gmlp"): break
            self.natt(l, last)
            if stop == (l, "natt"): break
            self.dn_prep(l)
            if stop == (l, "dnprep"): break
            self.dn_scan(l)
            if stop == (l, "dnscan"): break
            self.merge(l, h, self.hs[3 * l + 1]); h = self.hs[3 * l + 1]
            if stop == (l, "merge"): break
            self.ffn(l, 1, h, self.hs[3 * l + 2]); h = self.hs[3 * l + 2]
        self.final_norm(h)
        P.finish("sp")
        P.emit()
    return self.nc


Main.build = _build


def tile_w(W, ncols_chunk=128):
    K, N = W.shape
    KCn, G = K // 128, N // 128
    return np.ascontiguousarray(W.reshape(KCn, 128, G, 128).transpose(2, 1, 0, 3)).reshape(G, 128, KCn * 128)


def vecT(v):
    sh = v.shape
    KCn = sh[-1] // 128
    x = v.reshape(*sh[:-1], KCn, 128)
    x = np.moveaxis(x, -1, 0)
    return np.ascontiguousarray(x).reshape(128, -1)


def rope_tables(c):
    nf = 32
    freqs = (10000.0 ** (-np.arange(nf, dtype=np.float32) / nf)).astype(np.float32)
    t = np.arange(c.SEQ)
    rows = (t // c.GW).astype(np.float32)
    cols = (t % c.GW).astype(np.float32)
    cosT = np.zeros((128, c.SEQ), np.float32)
    sinT = np.zeros((128, c.SEQ), np.float32)
    for p in range(128):
        pos = rows if p < 64 else cols
        ang = pos * freqs[p % 32]
        cosT[p] = np.cos(ang)
        sinT[p] = np.sin(ang) * (-1.0 if (p % 64) < 32 else 1.0)
    return cosT, sinT


def in_col_order(c):
    DNW, NAW, GMW, D, H = c.DNW, c.NAW, c.GMW, c.D, c.DNH
    sizes = [2 * DNW, 2 * H, 2 * H, NAW, NAW, DNW, NAW, DNW, GMW, GMW, D, D, D]
    offs = np.concatenate([[0], np.cumsum(sizes)])
    o = {n: offs[i] for i, n in enumerate(["kv", "al", "be", "nak", "nav", "dnq", "naq", "dng", "gmu", "gmv", "ga", "gb", "gc"])}
    r = lambda a, n: list(range(a, a + n))
    cols = (r(o["kv"], DNW) + r(o["kv"] + DNW, DNW) + r(o["dnq"], DNW) + r(o["naq"], NAW) + r(o["nak"], NAW)
            + r(o["gmu"], GMW) + r(o["ga"], D) + r(o["gb"], D) + r(o["gc"], D)
            + r(o["nav"], NAW) + r(o["dng"], DNW) + r(o["gmv"], GMW) + r(o["al"], 2 * H) + r(o["be"], 2 * H))
    return np.array(cols), int(offs[-1])


def prep_core(c, b, inp, adaw_tiled):
    L = c.L
    f32 = np.float32
    m = {}
    m["consts"] = make_consts()
    m["xT"] = np.ascontiguousarray(np.concatenate([inp["x"][b], inp["ctx"][b]], axis=0).T)
    cc = np.stack([inp["c"][b], inp["c_ctx"]], axis=0)
    m["cT"] = np.ascontiguousarray(cc.reshape(2, c.KC, 128).transpose(2, 1, 0)).reshape(128, c.KC * 2)
    for l in range(L):
        m[f"adaw_{l}"] = adaw_tiled[l]
    m["adabT"] = vecT(inp["ada_b"])
    m["normT"] = vecT(inp["norm_w"])
    m["fnT"] = vecT(inp["final_norm_w"])
    cols, inw = in_col_order(c)
    for l in range(L):
        for i in range(2):
            m[f"w13_{l}_{i}"] = np.concatenate([tile_w(inp["ffn_w1"][l, i]), tile_w(inp["ffn_w3"][l, i])], axis=0)
            m[f"w2_{l}_{i}"] = tile_w(inp["ffn_w2"][l, i])
        W = inp["w_in"][l][:, cols]
        pad = c.NCH * 128 - W.shape[1]
        W = np.concatenate([W, np.zeros((c.D, pad), f32)], axis=1)
        m[f"win_{l}"] = tile_w(W)
        m[f"pa_{l}"] = tile_w(inp["proj_a"][l])
        m[f"pb_{l}"] = tile_w(inp["proj_b"][l])
        m[f"pc_{l}"] = tile_w(inp["proj_c"][l])
        m[f"wo_{l}"] = tile_w(inp["w_out"][l])
    G = c.GMG
    m["sguT"] = np.ascontiguousarray(inp["sgu_w"].transpose(3, 0, 1, 2)).reshape(128, L * G * 128)
    m["sgub"] = np.ascontiguousarray(np.broadcast_to(inp["sgu_b"].reshape(1, L * G * 128), (128, L * G * 128)))
    m["sgunw"] = np.ascontiguousarray(np.broadcast_to(inp["sgu_norm_w"].reshape(1, L * c.GMW), (128, L * c.GMW)))
    GW = c.GW
    qc = np.arange(GW)
    col_start = np.clip(qc - 8, 0, GW - 16)
    col_ok = (qc[None, :] >= col_start[:, None]) & (qc[None, :] < col_start[:, None] + 16)
    dc = np.clip(qc[None, :] - qc[:, None] + 15, 0, 30)
    rp = inp["na_rpb"]
    g = rp[:, :, :, dc]
    g = np.where(col_ok[None, None, None], g, f32(NEGBIG)).astype(f32)
    m["rpbm"] = np.ascontiguousarray(g.transpose(0, 1, 3, 2, 4)).reshape(L * c.NAH, GW, 15 * GW)
    cosT, sinT = rope_tables(c)
    m["ropec"], m["ropes"] = cosT, sinT
    H = c.DNH
    cw = inp["conv_w"]
    m["convw"] = np.ascontiguousarray(cw.reshape(L, 5, 3 * H, 128).transpose(3, 0, 2, 1)).reshape(128, L * 3 * H * 5)
    dt = inp["dn_dt_bias"].reshape(L, 1, 2 * H)
    al = inp["dn_a_log"].reshape(L, 1, 2 * H)
    dnc = np.stack([np.broadcast_to(dt, (L, c.NCK, 2 * H)), np.broadcast_to(al, (L, c.NCK, 2 * H))], axis=1)
    m["dnc"] = np.ascontiguousarray(np.broadcast_to(dnc.reshape(1, -1), (64, dnc.size))).astype(f32)
    m["dnnw"] = np.ascontiguousarray(np.broadcast_to(inp["dn_norm_w"].reshape(1, L * 128), (64, L * 128))).astype(f32)
    return {k: np.ascontiguousarray(v, dtype=f32) for k, v in m.items()}


def kernel(x, c, ctx, c_ctx, ada_w, ada_b, norm_w, ffn_w1, ffn_w3, ffn_w2, w_in, conv_w, dn_a_log,
           dn_dt_bias, dn_norm_w, sgu_w, sgu_b, sgu_norm_w, na_rpb, proj_a, proj_b, proj_c, w_out,
           final_norm_w):
    inp = dict(x=x, c=c, ctx=ctx, c_ctx=c_ctx, ada_w=ada_w, ada_b=ada_b, norm_w=norm_w, ffn_w1=ffn_w1,
               ffn_w3=ffn_w3, ffn_w2=ffn_w2, w_in=w_in, conv_w=conv_w, dn_a_log=dn_a_log, dn_dt_bias=dn_dt_bias,
               dn_norm_w=dn_norm_w, sgu_w=sgu_w, sgu_b=sgu_b, sgu_norm_w=sgu_norm_w, na_rpb=na_rpb,
               proj_a=proj_a, proj_b=proj_b, proj_c=proj_c, w_out=w_out, final_norm_w=final_norm_w)
    inp = {k: np.asarray(v, dtype=np.float32) for k, v in inp.items()}
    cfg = FULL
    B = inp["x"].shape[0]
    M = Main(cfg)
    nc = M.build()
    adaw_tiled = [tile_w(inp["ada_w"][l]) for l in range(cfg.L)]
    in_maps = [prep_core(cfg, b, inp, adaw_tiled) for b in range(B)]
    res = run_bass_kernel_spmd(nc, in_maps, core_ids=list(range(B)))
    out = np.stack([np.ascontiguousarray(res.results[b]["outT"].T) for b in range(B)], axis=0)
    return out.astype(np.float32)
```
